# Optimizing a Trainium2 kernel written in Bass

```python
import jax, jax.numpy as jnp
from jax import lax
import numpy as np

D_MODEL = 1024
BATCH = 8
SEQ = 2048
DEPTH = 1
DEC_BATCH = 128
DEC_SEQ = 1
PAST_LEN = 16384
PAGE_SIZE = 128

MLSTM_HEADS = 4
HEAD_DIM = 128
MLSTM_W = MLSTM_HEADS * HEAD_DIM
CONV_GROUPS = 4
CONV_CH = D_MODEL - MLSTM_W
CONV_WIDTH = 3
D_FF = 4 * D_MODEL
PLE_DIM = 256
CHUNK = 64
EPS = 1e-6
M_INIT = -1e30
N_IN = 4 * MLSTM_W + 2 * MLSTM_HEADS + 3 * CONV_CH

kernel_name = 'hymba_mlstm_shortconv_decoder_step'


def rmsnorm(x, g):
    xf = x.astype(jnp.float32)
    y = xf * lax.rsqrt(jnp.mean(xf * xf, axis=-1, keepdims=True) + EPS)
    return (y * g.astype(jnp.float32)).astype(x.dtype)


def mlstm_chunkwise(q, k, v, ig, lf, C0, n0, m0):
    B, S, H, Dk = q.shape
    Dv = v.shape[-1]
    L = CHUNK if S % CHUNK == 0 else S
    nc = S // L

    def to_chunks(a):
        return jnp.moveaxis(a.reshape((B, nc, L) + a.shape[2:]), 1, 0)

    causal = jnp.tril(jnp.ones((L, L), dtype=bool))

    def step(carry, xs):
        C, n, m = carry
        qc, kc, vc, ic, fc = xs
        b = jnp.cumsum(fc, axis=1)
        dlog = b[:, :, None, :] - b[:, None, :, :] + ic[:, None, :, :]
        dlog = jnp.where(causal[None, :, :, None], dlog, -jnp.inf)
        m_inter = b + m[:, None, :]
        m_t = jnp.maximum(m_inter, jnp.max(dlog, axis=2))
        dmat = jnp.exp(dlog - m_t[:, :, None, :])
        scores = jnp.einsum('bthd,bshd->btsh', qc, kc) * dmat
        decay = jnp.exp(m_inter - m_t)
        num = (jnp.einsum('btsh,bshv->bthv', scores, vc)
               + decay[..., None] * jnp.einsum('bthk,bhkv->bthv', qc, C))
        den = jnp.sum(scores, axis=2) + decay * jnp.einsum('bthk,bhk->bth', qc, n)
        h = num / jnp.maximum(jnp.abs(den), jnp.exp(-m_t))[..., None]
        m_new = m_t[:, -1]
        w = jnp.exp(b[:, -1:, :] - b + ic - m_new[:, None, :])
        cdec = jnp.exp(b[:, -1] + m - m_new)
        C_new = cdec[..., None, None] * C + jnp.einsum('bsh,bshk,bshv->bhkv', w, kc, vc)
        n_new = cdec[..., None] * n + jnp.einsum('bsh,bshk->bhk', w, kc)
        return (C_new, n_new, m_new), h

    xs = (to_chunks(q), to_chunks(k), to_chunks(v), to_chunks(ig), to_chunks(lf))
    (C, n, m), hs = lax.scan(step, (C0, n0, m0), xs)
    h = jnp.moveaxis(hs, 0, 1).reshape(B, S, H, Dv)
    return h, C, n, m


def layer(x, p, conv_buf, C0, n0, m0, norm_mix, w_in, b_gate_i, b_gate_f, mh_norm,
          conv_w, w_out, norm_mlp, w_up, w_down, norm_ple, w_ple_gate, w_ple_proj):
    B, S, _ = x.shape
    f32 = jnp.float32
    h = rmsnorm(x, norm_mix)
    z = h @ w_in
    sizes = [MLSTM_W, MLSTM_W, MLSTM_W, MLSTM_W, MLSTM_HEADS, MLSTM_HEADS, CONV_CH, CONV_CH, CONV_CH]
    offs = []
    acc = 0
    for s in sizes[:-1]:
        acc += s
        offs.append(acc)
    q, k, v, og, ig, fg, gb, gc, u = jnp.split(z, offs, axis=-1)
    q = q.astype(f32).reshape(B, S, MLSTM_HEADS, HEAD_DIM) * (HEAD_DIM ** -0.5)
    k = k.astype(f32).reshape(B, S, MLSTM_HEADS, HEAD_DIM)
    v = v.astype(f32).reshape(B, S, MLSTM_HEADS, HEAD_DIM)
    ig = ig.astype(f32) + b_gate_i.astype(f32)
    lf = jax.nn.log_sigmoid(fg.astype(f32) + b_gate_f.astype(f32))
    hm, C, n, m = mlstm_chunkwise(q, k, v, ig, lf, C0.astype(f32), n0.astype(f32), m0.astype(f32))
    hm = hm * lax.rsqrt(jnp.mean(hm * hm, axis=-1, keepdims=True) + EPS)
    hm = hm * mh_norm.astype(f32).reshape(MLSTM_HEADS, HEAD_DIM)
    hm = hm.reshape(B, S, MLSTM_W) * jax.nn.sigmoid(og.astype(f32))
    cu = gc * u
    full = jnp.concatenate([conv_buf.astype(cu.dtype), cu], axis=1)
    yc = conv_w[0] * full[:, 0:S]
    for j in range(1, CONV_WIDTH):
        yc = yc + conv_w[j] * full[:, j:j + S]
    new_buf = full[:, S:]
    yc = gb * yc
    mix = jnp.concatenate([hm.astype(x.dtype), yc.astype(x.dtype)], axis=-1) @ w_out
    x = x + mix
    hf = jax.nn.relu(rmsnorm(x, norm_mlp) @ w_up)
    x = x + (hf * hf) @ w_down
    gate = jax.nn.sigmoid((rmsnorm(x, norm_ple) @ w_ple_gate).astype(f32))
    x = x + (gate * (p @ w_ple_proj).astype(f32)).astype(x.dtype)
    return x, new_buf, C, n, m


def setup_inputs(seed: int = 0) -> dict:
    key = jax.random.key(seed)
    ks = jax.random.split(key, 24)
    nrm = jax.random.normal
    f32 = jnp.float32
    d = {}
    d['x_prompt'] = nrm(ks[0], (BATCH, SEQ, D_MODEL), f32)
    d['x_sample'] = nrm(ks[1], (DEC_BATCH, DEC_SEQ, D_MODEL), f32)
    d['state_mlstm_C'] = 0.5 * nrm(ks[2], (DEPTH, DEC_BATCH, MLSTM_HEADS, HEAD_DIM, HEAD_DIM), f32)
    d['state_mlstm_n'] = 0.5 * nrm(ks[3], (DEPTH, DEC_BATCH, MLSTM_HEADS, HEAD_DIM), f32)
    d['state_mlstm_m'] = jax.random.uniform(ks[4], (DEPTH, DEC_BATCH, MLSTM_HEADS), f32, 0.0, 4.0)
    d['state_conv'] = nrm(ks[5], (DEPTH, DEC_BATCH, CONV_WIDTH - 1, CONV_CH), f32)
    d['p_prompt'] = nrm(ks[6], (DEPTH, BATCH, SEQ, PLE_DIM), f32)
    d['p_sample'] = nrm(ks[7], (DEPTH, DEC_BATCH, DEC_SEQ, PLE_DIM), f32)
    d['norm_mix'] = 1.0 + 0.02 * nrm(ks[8], (DEPTH, D_MODEL), f32)
    d['w_in'] = nrm(ks[9], (DEPTH, D_MODEL, N_IN), f32) * D_MODEL ** -0.5
    d['b_gate_i'] = 0.1 * nrm(ks[10], (DEPTH, MLSTM_HEADS), f32)
    d['b_gate_f'] = jnp.linspace(3.0, 6.0, MLSTM_HEADS, dtype=f32)[None, :] + 0.01 * nrm(ks[11], (DEPTH, MLSTM_HEADS), f32)
    d['mh_norm'] = 1.0 + 0.02 * nrm(ks[12], (DEPTH, MLSTM_W), f32)
    d['conv_w'] = nrm(ks[13], (DEPTH, CONV_WIDTH, CONV_CH), f32) * CONV_WIDTH ** -0.5
    d['w_out'] = nrm(ks[14], (DEPTH, MLSTM_W + CONV_CH, D_MODEL), f32) * (MLSTM_W + CONV_CH) ** -0.5
    d['norm_mlp'] = 1.0 + 0.02 * nrm(ks[15], (DEPTH, D_MODEL), f32)
    d['w_up'] = nrm(ks[16], (DEPTH, D_MODEL, D_FF), f32) * D_MODEL ** -0.5
    d['w_down'] = nrm(ks[17], (DEPTH, D_FF, D_MODEL), f32) * D_FF ** -0.5
    d['norm_ple'] = 1.0 + 0.02 * nrm(ks[18], (DEPTH, D_MODEL), f32)
    d['w_ple_gate'] = nrm(ks[19], (DEPTH, D_MODEL, D_MODEL), f32) * D_MODEL ** -0.5
    d['w_ple_proj'] = nrm(ks[20], (DEPTH, PLE_DIM, D_MODEL), f32) * PLE_DIM ** -0.5
    d['norm_final'] = 1.0 + 0.02 * nrm(ks[21], (D_MODEL,), f32)
    return d


def reference(x_prompt, x_sample, state_mlstm_C, state_mlstm_n, state_mlstm_m, state_conv,
              p_prompt, p_sample, norm_mix, w_in, b_gate_i, b_gate_f, mh_norm, conv_w, w_out,
              norm_mlp, w_up, w_down, norm_ple, w_ple_gate, w_ple_proj, norm_final):
    f32 = jnp.float32
    Bp = x_prompt.shape[0]
    xp = x_prompt
    xs = x_sample
    pC, pn, pm, pconv = [], [], [], []
    sC, sn, sm, sconv = [], [], [], []
    for i in range(DEPTH):
        params = (norm_mix[i], w_in[i], b_gate_i[i], b_gate_f[i], mh_norm[i], conv_w[i], w_out[i],
                  norm_mlp[i], w_up[i], w_down[i], norm_ple[i], w_ple_gate[i], w_ple_proj[i])
        conv0 = jnp.zeros((Bp, CONV_WIDTH - 1, CONV_CH), xp.dtype)
        C0 = jnp.zeros((Bp, MLSTM_HEADS, HEAD_DIM, HEAD_DIM), f32)
        n0 = jnp.zeros((Bp, MLSTM_HEADS, HEAD_DIM), f32)
        m0 = jnp.full((Bp, MLSTM_HEADS), M_INIT, f32)
        xp, b1, c1, n1, m1 = layer(xp, p_prompt[i], conv0, C0, n0, m0, *params)
        pC.append(c1); pn.append(n1); pm.append(m1); pconv.append(b1)
        xs, b2, c2, n2, m2 = layer(xs, p_sample[i], state_conv[i], state_mlstm_C[i],
                                   state_mlstm_n[i], state_mlstm_m[i], *params)
        sC.append(c2); sn.append(n2); sm.append(m2); sconv.append(b2)
    y_prompt = rmsnorm(xp, norm_final)
    y_sample = rmsnorm(xs, norm_final)
    return (y_prompt, y_sample,
            jnp.stack(pC), jnp.stack(pn), jnp.stack(pm), jnp.stack(pconv),
            jnp.stack(sC), jnp.stack(sn), jnp.stack(sm), jnp.stack(sconv))
```

```python
import numpy as np
import concourse.bass as bass
import concourse.mybir as mybir
from concourse.bass_utils import run_bass_kernel_spmd
from contextlib import ExitStack

F32 = mybir.dt.float32
BF16 = mybir.dt.bfloat16
ALU = mybir.AluOpType
AF = mybir.ActivationFunctionType
AX = mybir.AxisListType

D = 1024
SEQ = 2048
NT = 16
NS = 16
TC = SEQ + NS
NIN = 3592
DFF = 4096
EPS = 1e-6
M_INIT = -1e30
QSCALE = 128.0 ** -0.5

ENGS = ['pe', 'act', 'dve', 'pool', 'sp']
PERSIST = {'x', 'hT', 'ss', 'rs1', 'rstd', 'ones', 'identf', 'identb', 'mhalf', 'gcol', 'mhcol', 'cwcol', 'gbb',
           'ps0', 'ps1', 'ps2', 'ps3', 'ps4', 'ps5', 'ps6', 'ps7', 'mixT', 'wbuf', 'wg', 'tokq', 'cdec_bc', 'cols',
           'zsh', 'zsh_q', 'qTs', 'kTs', 'gscr', 'dscr', 'wvscr', 's_dm', 's_dec', 's_emt', 's_mt', 'gsm', 'cuTs',
           'bufT', 'gtm', 'R_ig', 'R_t1', 'R_sp', 'R_bn', 'R_a', 'R_A', 'R_g', 'R_m', 'R_emt', 'R_dec', 'R_w',
           'rowA', 'mnew', 'mprev', 'mprev0', 'cdec_a', 'cdec_b', 'cdecr', 'wlr', 'sm_sb', 's_t0', 's_t1', 's_mi',
           's_t0b', 's_t1b'}
NDMASEM = 28
DMA_POOL = {'sp': list(range(0, 16)), 'pool': list(range(16, 24)), 'act': list(range(24, 28))}


class Sched:
    def __init__(self, nc):
        self.nc = nc
        self.ops = {e: [] for e in ENGS}
        self.last_write = {}
        self.readers = {}
        self.dma_n = {e: 0 for e in DMA_POOL}
        self.dma_cnt = [0] * NDMASEM
        self.region = None
        self.regnames = {}
        self.cur_barrier = None
        self.deferred = None
        self._replaying = False

    @staticmethod
    def _base(r):
        return r[0] if isinstance(r, tuple) else r

    def barrier(self, old=None, new=()):
        names = set()
        if old is None:
            names |= set(self.last_write) | set(self.readers)
        else:
            for rg in old:
                names |= self.regnames.get(rg, set())
        wr = set(names)
        for rg in new:
            wr |= self.regnames.get(rg, set())
        saved = self.region
        self.region = None
        b = self.op('sp', lambda e: e.nop(), reads=sorted(names, key=str), writes=sorted(wr, key=str))
        self.region = saved
        self.cur_barrier = b

    def defer_begin(self):
        self.deferred = []

    def mark(self):
        if self.deferred is not None and not self._replaying:
            self.deferred.append(None)

    def defer_end(self):
        self._stages = self.deferred
        self.deferred = None

    def replay_stage(self):
        st_ = getattr(self, '_stages', None)
        if not st_:
            return
        self._replaying = True
        while st_:
            item = st_.pop(0)
            if item is None:
                break
            self.op(*item)
        self._replaying = False

    def replay_all(self):
        while getattr(self, '_stages', None):
            self.replay_stage()

    def op(self, eng, fn, reads=(), writes=(), dma=False):
        if self.deferred is not None and not self._replaying:
            self.deferred.append((eng, fn, list(reads), list(writes), dma))
            return None
        idx = len(self.ops[eng])
        deps = set()
        for r in list(reads) + list(writes):
            if self.region is not None and self._base(r) not in PERSIST:
                self.regnames.setdefault(self.region, set()).add(r)
            if self.cur_barrier is not None and r not in self.last_write and r not in self.readers:
                deps.add(self.cur_barrier)
        ps_reads = [r for r in reads if isinstance(r, str) and r.startswith('ps') and r[2:].isdigit()]
        reads = [r for r in reads if r not in ps_reads]
        writes = list(writes) + ps_reads
        for r in reads:
            w = self.last_write.get(r)
            if w is not None:
                deps.add(w)
        for r in writes:
            w = self.last_write.get(r)
            if w is not None:
                deps.add(w)
            for rd in self.readers.get(r, ()):
                deps.add(rd)
        deps.discard((eng, idx))
        self._g = getattr(self, '_g', 0) + 1
        rec = dict(fn=fn, deps=deps, dma=dma, g=self._g)
        if dma:
            pool = DMA_POOL[eng]
            k = pool[self.dma_n[eng] % len(pool)]
            self.dma_n[eng] += 1
            self.dma_cnt[k] += 1
            rec['dsem'] = k
            rec['dval'] = 16 * self.dma_cnt[k]
        self.ops[eng].append(rec)
        for r in reads:
            self.readers.setdefault(r, []).append((eng, idx))
        for r in writes:
            self.last_write[r] = (eng, idx)
            self.readers[r] = []
        return (eng, idx)

    def emit(self, stack, final_wait_eng='sp'):
        nc = self.nc
        sem = {e: stack.enter_context(nc.semaphore('s_' + e)) for e in ENGS}
        dsem = [stack.enter_context(nc.semaphore('d_%d' % i)) for i in range(NDMASEM)]
        needed = set()
        for e in ENGS:
            for i, rec in enumerate(self.ops[e]):
                for d in rec['deps']:
                    de, di = d
                    if de == 'pe' and e == 'pe':
                        continue
                    if not self.ops[de][di]['dma']:
                        needed.add(d)
        rank = {}
        for e in ENGS:
            c = 0
            for i, rec in enumerate(self.ops[e]):
                if (e, i) in needed:
                    c += 1
                    rank[(e, i)] = c
        upto = {}
        for e in ENGS:
            c = 0
            for i, rec in enumerate(self.ops[e]):
                if (e, i) in rank:
                    c = rank[(e, i)]
                upto[(e, i)] = c
        K = {}
        prevK = {e: {} for e in ENGS}
        order = sorted((rec['g'], e, i) for e in ENGS for i, rec in enumerate(self.ops[e]))
        for _, e, i in order:
            rec = self.ops[e][i]
            k = dict(prevK[e])
            for d in rec['deps']:
                de, di = d
                if de == 'pe' and e == 'pe':
                    continue
                for en, r in K[d].items():
                    if k.get(en, 0) < r:
                        k[en] = r
            prevK[e] = k
            kk = dict(k)
            if not rec['dma']:
                if kk.get(e, 0) < upto[(e, i)]:
                    kk[e] = upto[(e, i)]
            K[(e, i)] = kk
        self._K = K
        block = stack.enter_context(nc.Block())
        handles = {'pe': block.tensor, 'act': block.scalar, 'dve': block.vector,
                   'pool': block.gpsimd, 'sp': block.sync}
        sched = self

        def make_body(e):
            def body(eng):
                waited = {}
                known = {}

                def learn(d):
                    for en, r in sched._K[d].items():
                        if known.get(en, 0) < r:
                            known[en] = r

                def wait(s, key, val):
                    if waited.get(key, 0) >= val:
                        return
                    eng.wait_ge(s, val)
                    waited[key] = val

                for i, rec in enumerate(sched.ops[e]):
                    for d in sorted(rec['deps']):
                        de, di = d
                        if de == 'pe' and e == 'pe':
                            continue
                        prod = sched.ops[de][di]
                        if prod['dma']:
                            wait(dsem[prod['dsem']], ('d', prod['dsem']), prod['dval'])
                        elif known.get(de, 0) < rank[d]:
                            wait(sem[de], ('c', de), rank[d])
                        learn(d)
                    if rec['dma']:
                        k = rec['dsem']
                        if rec['dval'] > 16:
                            wait(dsem[k], ('d', k), rec['dval'] - 16)
                        inst = rec['fn'](eng)
                        inst.then_inc(dsem[k], 16)
                    else:
                        inst = rec['fn'](eng)
                        if (e, i) in needed:
                            inst.then_inc(sem[e], 1)
                if e == final_wait_eng:
                    for k in range(NDMASEM):
                        if sched.dma_cnt[k] > 0:
                            wait(dsem[k], ('d', k), 16 * sched.dma_cnt[k])
            return body

        for e in ENGS:
            if self.ops[e] or e == final_wait_eng:
                handles[e](make_body(e))


def _view(ap2d, shape):
    if len(shape) == 1:
        return ap2d
    names = ['a%d' % i for i in range(len(shape))]
    pat = "p (" + " ".join(names) + ") -> p " + " ".join(names)
    kw = {n: s for n, s in zip(names[1:], shape[1:])}
    return ap2d.rearrange(pat, **kw)


class Arena:
    def __init__(self, t, nwords):
        self.t = t
        self.n = nwords
        self.top = 0

    def alloc(self, words):
        off = self.top
        self.top += int(words)
        assert self.top <= self.n, ("arena overflow", self.top, self.n)
        return off

    def f32(self, off, shape, parts=128):
        w = int(np.prod(shape))
        return _view(self.t[0:parts, off:off + w], shape)

    def bf16(self, off, shape, parts=128):
        n = int(np.prod(shape))
        w = (n + 1) // 2
        ap = self.t[0:parts, off:off + w].bitcast(BF16)
        if n != 2 * w:
            ap = ap[:, 0:n]
        return _view(ap, shape)


class _Stop(Exception):
    pass


def build(debug=None, stop=None):
    nc = bass.Bass("TRN2", target_bir_lowering=False)

    def din(name, shape):
        return nc.dram_tensor(name, list(shape), F32, kind="ExternalInput").ap()

    def dout(name, shape):
        return nc.dram_tensor(name, list(shape), F32, kind="ExternalOutput").ap()

    xp = din("xp", [SEQ, D]); xs = din("xs", [NS, D])
    pp = din("pp", [SEQ, 256]); psm = din("psm", [NS, 256])
    sC = din("sC", [NS, 4, 128, 128]); sn = din("sn", [NS, 4, 128]); sm = din("sm", [NS, 4])
    sconv = din("sconv", [NS, 2, 512])
    w_in = din("w_in", [D, NIN]); w_out = din("w_out", [D, D]); w_up = din("w_up", [D, DFF])
    w_down = din("w_down", [DFF, D]); w_pg = din("w_pg", [D, D]); w_pp = din("w_pp", [256, D])
    nmix = din("nmix", [D]); nmlp = din("nmlp", [D]); nple = din("nple", [D]); nfin = din("nfin", [D])
    bgi = din("bgi", [4]); bgf = din("bgf", [4]); mhn = din("mhn", [512]); cw = din("cw", [3, 512])

    yp = dout("yp", [SEQ, D]); ys = dout("ys", [NS, D])
    pC = dout("pC", [4, 128, 128]); pn = dout("pn", [4, 128]); pm = dout("pm", [4]); pconv = dout("pconv", [2, 512])
    sCo = dout("sCo", [NS, 4, 128, 128]); sno = dout("sno", [NS, 4, 128]); smo = dout("smo", [NS, 4])
    sconvo = dout("sconvo", [NS, 2, 512])

    gscr_t = nc.dram_tensor("gscr", [64, 128], F32, kind="Internal")
    gscr = gscr_t.ap()
    wvscr_t = nc.dram_tensor("wvscr", [4, NS, 128], F32, kind="Internal")
    wvscr = wvscr_t.ap()
    dscr_t = nc.dram_tensor("dscr", [4, NS], F32, kind="Internal")
    dscr = dscr_t.ap()

    dbg_outs = {}

    with ExitStack() as st:
        NW = 53000
        arena_t = st.enter_context(nc.sbuf_tensor("arena", [128, NW], F32))
        P = st.enter_context(nc.psum_tensor("psum", [128, 8, 512], F32))
        A = Arena(arena_t, NW)
        S = Sched(nc)

        psi = [0]

        bank_mode = {'m': None}
        psj = [0]

        def bank():
            if bank_mode['m'] == 'aux':
                b = 6 + psj[0] % 2
                psj[0] += 1
                return b
            if bank_mode['m'] == 'main6':
                b = psi[0] % 6
                psi[0] += 1
                return b
            b = psi[0] % 8
            psi[0] += 1
            return b

        def Pf(b):
            return P[:, b, :]

        def Pb(b):
            return P[:, b, :].bitcast(BF16)

        def PSR(b):
            return 'ps%d' % b

        def dbg(name, ap, shape, reads):
            if debug is None or name not in debug:
                return
            o = dout("dbg_" + name, shape)
            dbg_outs[name] = shape
            S.op('pool', lambda e: e.dma_start(out=o, in_=ap), reads=reads, dma=True)

        def done(tag):
            if stop == tag:
                raise _Stop()

        try:
            o_x = A.alloc(17 * D)
            x_sb = A.f32(o_x, [17, D])
            o_ht = A.alloc(8 * TC // 2)
            hT = A.bf16(o_ht, [8, TC])
            o_identf = A.alloc(128); identf = A.f32(o_identf, [128])
            o_identb = A.alloc(64); identb = A.bf16(o_identb, [128])
            o_ones = A.alloc(128); ones_f = A.f32(o_ones, [128])
            o_gcol = A.alloc(32); gcol = A.f32(o_gcol, [4, 8])
            o_mhc = A.alloc(4); mhcol = A.f32(o_mhc, [4])
            o_cwc = A.alloc(12); cwcol = A.f32(o_cwc, [4, 3])
            o_gbb = A.alloc(128); gb_bc = A.f32(o_gbb, [2, 16, 4])
            o_mh = A.alloc(32); mhalf = A.f32(o_mh, [32])
            o_ss = A.alloc(20); ss = A.f32(o_ss, [20])
            o_rs1 = A.alloc(20); rs1 = A.f32(o_rs1, [20])
            o_rstd = A.alloc(20); rstd = A.f32(o_rstd, [20])
            PH = A.top

            for t in range(NT):
                S.op('sp', lambda e, t=t: e.dma_start(out=x_sb[:, t, :], in_=xp[t * 128:(t + 1) * 128, :]),
                     writes=[('x', t)], dma=True)
            S.op('sp', lambda e: e.dma_start(out=x_sb[0:NS, 16, :], in_=xs), writes=[('x', 16)], dma=True)

            S.op('pool', lambda e: e.memset(ones_f, 1.0), writes=['ones'])
            S.op('pool', lambda e: e.affine_select(out=identf, in_=ones_f, pattern=[[1, 128]], compare_op=ALU.is_equal,
                                                   fill=0.0, base=0, channel_multiplier=-1), reads=['ones'], writes=['identf'])
            S.op('pool', lambda e: e.tensor_copy(out=identb, in_=identf), reads=['identf'], writes=['identb'])
            S.op('pool', lambda e: e.memset(mhalf, -0.5), writes=['mhalf'])
            S.op('pool', lambda e: e.memset(ss, 1.0), writes=[('ss', t) for t in range(17)])
            for j, nv in enumerate([nmix, nmlp, nple]):
                S.op('sp', lambda e, j=j, nv=nv: e.dma_start(out=gcol[:, j, :], in_=nv.rearrange("(kc p) -> p kc", p=128),
                                                             allow_slow_non_contiguous=True), writes=[('gcol', j)], dma=True)
            S.op('sp', lambda e: e.dma_start(out=mhcol, in_=mhn.rearrange("(h p) -> p h", p=128),
                                             allow_slow_non_contiguous=True), writes=['mhcol'], dma=True)
            for jw in range(3):
                S.op('sp', lambda e, jw=jw: e.dma_start(out=cwcol[:, :, jw], in_=cw[jw].rearrange("(c p) -> p c", p=128),
                                                        allow_slow_non_contiguous=True), writes=['cwcol'], dma=True)
            for g, bv in enumerate([bgi, bgf]):
                S.op('sp', lambda e, g=g, bv=bv: e.dma_start(
                    out=gb_bc[:, g, :, :], in_=bass.AP(tensor=bv.tensor, offset=0, ap=[[0, 128], [0, 16], [1, 4]])),
                    writes=[('gbb', g)], dma=True)

            def rows(t):
                return 128 if t < NT else NS

            def norm_sq(t, junk_ap, junk_names):
                r = rows(t)
                S.op('act', lambda e, t=t, r=r: e.activation(out=junk_ap[0:r, :], in_=x_sb[0:r, t, :], func=AF.Square,
                                                             accum_out=ss[0:r, t:t + 1]),
                     reads=[('x', t)], writes=list(junk_names) + [('ss', t)])

            def norm_rstd(g):
                tiles = list(range(4 * g, 4 * g + 4)) if g < 4 else [16]
                c0, c1 = tiles[0], tiles[-1] + 1
                S.op('dve', lambda e: e.tensor_scalar(out=rs1[:, c0:c1], in0=ss[:, c0:c1], scalar1=1.0 / D, scalar2=EPS,
                                                      op0=ALU.mult, op1=ALU.add),
                     reads=[('ss', t) for t in tiles], writes=[('rs1', g)])
                S.op('pool', lambda e: e.tensor_tensor(out=rstd[:, c0:c1], in0=rs1[:, c0:c1], in1=mhalf[:, c0:c1], op=ALU.pow),
                     reads=[('rs1', g), 'mhalf'], writes=[('rstd', g)])

            def norm_to_hT(j, o_tmp, presq=False):
                S.region = 'norm'
                junk = A.bf16(o_tmp, [D])
                xn = [A.bf16(o_tmp + 512 + i * 2048, [4, D]) for i in range(2)]
                groups = [list(range(4 * g, 4 * g + 4)) for g in range(4)] + [[16]]

                def stage_sq(g):
                    tiles = groups[g]
                    if presq:
                        return
                    for t in tiles:
                        norm_sq(t, junk, ['junk'])
                    norm_rstd(g)

                def _unused(g):
                    tiles = groups[g]
                    c0, c1 = tiles[0], tiles[-1] + 1
                    S.op('dve', lambda e: e.tensor_scalar(out=rs1[:, c0:c1], in0=ss[:, c0:c1], scalar1=1.0 / D, scalar2=EPS,
                                                          op0=ALU.mult, op1=ALU.add),
                         reads=[('ss', t) for t in tiles], writes=[('rs1', g)])
                    S.op('pool', lambda e: e.tensor_tensor(out=rstd[:, c0:c1], in0=rs1[:, c0:c1], in1=mhalf[:, c0:c1], op=ALU.pow),
                         reads=[('rs1', g), 'mhalf'], writes=[('rstd', g)])

                def stage_main(g):
                    buf = xn[g % 2]
                    tiles = groups[g]
                    for i, t in enumerate(tiles):
                        r = rows(t)
                        if (t % 2) == 0:
                            S.op('act', lambda e, t=t, r=r, i=i, buf=buf: e.activation(
                                out=buf[0:r, i, :], in_=x_sb[0:r, t, :], func=AF.Copy, scale=rstd[0:r, t:t + 1]),
                                reads=[('x', t), ('rstd', g)], writes=[('xn', g % 2, i)])
                        else:
                            S.op('dve', lambda e, t=t, r=r, i=i, buf=buf: e.tensor_scalar(
                                out=buf[0:r, i, :], in0=x_sb[0:r, t, :], scalar1=rstd[0:r, t:t + 1], scalar2=None,
                                op0=ALU.mult), reads=[('x', t), ('rstd', g)], writes=[('xn', g % 2, i)])
                    if g < 4:
                        for kc in range(8):
                            b = bank()

                            def tr(e, b=b, kc=kc, buf=buf):
                                for i in range(4):
                                    inst = e.transpose(out=Pb(b)[:, i * 128:(i + 1) * 128],
                                                       in_=buf[:, i, kc * 128:(kc + 1) * 128], identity=identb)
                                return inst
                            S.op('pe', tr, reads=[('xn', g % 2, i) for i in range(4)] + ['identb'], writes=[PSR(b)])
                            dst = hT[:, kc, g * 512:(g + 1) * 512]
                            if kc % 2 == 0:
                                S.op('dve', lambda e, b=b, kc=kc, dst=dst: e.tensor_scalar(
                                    out=dst, in0=Pb(b)[:, 0:512], scalar1=gcol[:, j, kc:kc + 1], scalar2=None, op0=ALU.mult),
                                    reads=[PSR(b), ('gcol', j)], writes=[('hT', kc, g)])
                            else:
                                S.op('act', lambda e, b=b, kc=kc, dst=dst: e.activation(
                                    out=dst, in_=Pb(b)[:, 0:512], func=AF.Copy, scale=gcol[:, j, kc:kc + 1]),
                                    reads=[PSR(b), ('gcol', j)], writes=[('hT', kc, g)])
                    else:
                        b = bank()

                        def trs(e, b=b, buf=buf):
                            for kc in range(8):
                                inst = e.transpose(out=Pb(b)[:, kc * 16:(kc + 1) * 16],
                                                   in_=buf[0:NS, 0, kc * 128:(kc + 1) * 128], identity=identb[0:NS, 0:NS])
                            return inst
                        S.op('pe', trs, reads=[('xn', g % 2, 0), 'identb'], writes=[PSR(b)])
                        for kc in range(8):
                            S.op('dve', lambda e, b=b, kc=kc: e.tensor_scalar(
                                out=hT[:, kc, SEQ:TC], in0=Pb(b)[:, kc * 16:(kc + 1) * 16], scalar1=gcol[:, j, kc:kc + 1],
                                scalar2=None, op0=ALU.mult), reads=[PSR(b), ('gcol', j)], writes=[('hT', kc, 4)])

                stage_sq(0)
                stage_sq(1)
                for g in range(5):
                    if g + 2 < 5:
                        stage_sq(g + 2)
                    stage_main(g)

            def hT_reads(kcs=range(8), gs=range(5)):
                return [('hT', kc, g) for kc in kcs for g in gs]

            def mm_group(out_ap, pairs, reads, b):
                def fn(e):
                    n = len(pairs)
                    for i, (l, r) in enumerate(pairs):
                        inst = e.matmul(out_ap, l, r, start=(i == 0), stop=(i == n - 1))
                    return inst
                S.op('pe', fn, reads=reads, writes=[PSR(b)])

            win_v = w_in.rearrange("(kc p) n -> p kc n", p=128)

            A.top = PH
            o_mixT = A.alloc(8 * TC // 2); mixT = A.bf16(o_mixT, [8, TC])
            o_wbuf = A.alloc(2048); wbufs = [A.bf16(o_wbuf, [8, 4, 128]), A.bf16(NW - 2048, [8, 4, 128])]
            o_wg = A.alloc(32); wg = A.bf16(o_wg, [8, 8])
            o_gtm = A.alloc(128); gtm = A.f32(o_gtm, [128])
            o_gs = A.alloc(8); gsm = A.f32(o_gs, [8])
            o_tokq = A.alloc(256); tokq = A.f32(o_tokq, [4, 16, 4])
            o_cdbc = A.alloc(64); cdec_bc = A.f32(o_cdbc, [16, 4])
            o_sg = A.alloc(64); sg = A.f32(o_sg, [16, 4])
            o_cuTs = A.alloc(64); cuTs = A.f32(o_cuTs, [4, 16])
            o_bufT = A.alloc(128); bufT = A.f32(o_bufT, [8, 16])
            o_R0 = A.top
            R = {}
            for nm in ['ig', 't1', 'sp', 'bn', 'a', 'A', 'g', 'm', 'emt', 'dec', 'w']:
                R[nm] = A.f32(A.alloc(128), [128])
            o_rows = A.alloc(64 * 6); rowsb = A.f32(o_rows, [6, 64])
            o_cols = A.alloc(4); colsb = A.f32(o_cols, [4])
            A.top = max(A.top, o_R0 + 2048 + 128)
            zsh_all = A.f32(o_R0, [4, 512]); qTs_all = A.f32(o_R0 + 2048, [4, 16]); kTs_all = A.f32(o_R0 + 2112, [4, 16])
            GATE_NAMES = ['R_ig', 'R_t1', 'R_sp', 'R_bn', 'R_a', 'R_A', 'R_g', 'R_m', 'R_emt', 'R_dec', 'R_w', 'rowA', 'mprev',
                          'mprev0', 'cdec_a', 'cdec_b', 'cdecr', 'wlr', 'cols'] + [('mnew', h_) for h_ in range(4)]
            o_sm = A.alloc(4); sm_sb = A.f32(o_sm, [4])
            o_sg2 = A.alloc(16); sg2 = A.f32(o_sg2, [4, 4])
            HT0 = A.top

            norm_to_hT(0, HT0)
            dbg('hT', hT, [128, 8, TC], hT_reads())
            done('norm1')

            S.region = None
            S.defer_begin()
            bank_mode['m'] = 'aux'
            S.op('pool', lambda e: e.dma_start(out=wg, in_=win_v[:, :, 2048:2056]), writes=['wg'], dma=True)
            bg1 = bank()
            for t in range(NT):
                mm_group(Pf(bg1)[:, t * 8:(t + 1) * 8],
                         [(hT[:, kc, t * 128:(t + 1) * 128], wg[:, kc, :]) for kc in range(8)],
                         hT_reads(gs=[t // 4]) + ['wg'], bg1)
            bg2 = bank()
            mm_group(Pf(bg2)[0:NS, 0:8], [(hT[:, kc, SEQ:TC], wg[:, kc, :]) for kc in range(8)],
                     hT_reads(gs=[4]) + ['wg'], bg2)
            S.mark()
            S.op('dve', lambda e: e.tensor_tensor(
                out=gtm.rearrange("p (g c h) -> p g c h", g=2, c=16),
                in0=Pf(bg1)[:, 0:128].rearrange("p (c g h) -> p g c h", c=16, g=2),
                in1=gb_bc, op=ALU.add), reads=[PSR(bg1), ('gbb', 0), ('gbb', 1)], writes=['gtm'])
            S.op('dve', lambda e: e.tensor_tensor(
                out=gsm[0:NS, :].rearrange("p (g h) -> p g h", g=2), in0=Pf(bg2)[0:NS, 0:8].rearrange("p (g h) -> p g h", g=2),
                in1=gb_bc[0:NS, :, 0, :], op=ALU.add), reads=[PSR(bg2), ('gbb', 0), ('gbb', 1)], writes=['gsm'])
            bt = bank()

            def trg(e):
                e.transpose(out=Pf(bt)[0:64, 0:128], in_=gtm[:, 0:64], identity=identf)
                return e.transpose(out=Pf(bt)[0:64, 128:256], in_=gtm[:, 64:128], identity=identf)
            S.mark()
            S.op('pe', trg, reads=['gtm', 'identf'], writes=[PSR(bt)])
            S.mark()
            r64 = lambda nm: R[nm][0:64, :]
            S.op('act', lambda e: e.activation(out=r64('ig'), in_=Pf(bt)[0:64, 0:128], func=AF.Copy),
                 reads=[PSR(bt)], writes=['R_ig'])
            S.op('act', lambda e: e.activation(out=r64('t1'), in_=Pf(bt)[0:64, 128:256], func=AF.Exp, scale=-1.0),
                 reads=[PSR(bt)], writes=['R_t1'])
            S.op('act', lambda e: e.activation(out=r64('sp'), in_=r64('t1'), func=AF.Ln, bias=1.0),
                 reads=['R_t1'], writes=['R_sp'])
            S.op('dve', lambda e: e.tensor_tensor_scan(out=r64('bn'), data0=ones_f[0:64, :], data1=r64('sp'), initial=0.0,
                                                       op0=ALU.mult, op1=ALU.add), reads=['R_sp', 'ones'], writes=['R_bn'])
            S.op('dve', lambda e: e.tensor_tensor(out=r64('a'), in0=r64('ig'), in1=r64('bn'), op=ALU.add),
                 reads=['R_ig', 'R_bn'], writes=['R_a'])
            S.op('dve', lambda e: e.tensor_tensor_scan(out=r64('A'), data0=ones_f[0:64, :], data1=r64('a'), initial=-3.0e38,
                                                       op0=ALU.mult, op1=ALU.max), reads=['R_a', 'ones'], writes=['R_A'])
            bt2 = bank()

            def trl(e):
                e.transpose(out=Pf(bt2)[0:1, 0:64], in_=R['A'][0:64, 127:128], identity=identf[0:64, 0:64])
                return e.transpose(out=Pf(bt2)[0:1, 64:128], in_=R['bn'][0:64, 127:128], identity=identf[0:64, 0:64])
            S.mark()
            S.op('pe', trl, reads=['R_A', 'R_bn', 'identf'], writes=[PSR(bt2)])
            S.mark()
            rowA = rowsb[0:1, 0, :]; rowbn = rowsb[0:1, 1, :]; mnew = rowsb[0:1, 2, :]; mprev = rowsb[0:1, 3, :]
            cdecr = rowsb[0:1, 4, :]; wlr = rowsb[0:1, 5, :]
            S.op('act', lambda e: e.activation(out=rowsb[0:1, 0:2, :], in_=Pf(bt2)[0:1, 0:128].rearrange("p (a b) -> p a b", a=2),
                                               func=AF.Copy), reads=[PSR(bt2)], writes=['rowA'])
            for h in range(4):
                sl = lambda ap, h=h: ap.rearrange("p (c h) -> p c h", h=4)[:, :, h]
                S.op('dve', lambda e, sl=sl: e.tensor_tensor_scan(out=sl(mnew), data0=sl(rowA), data1=sl(rowbn),
                                                                  initial=M_INIT, op0=ALU.max, op1=ALU.subtract),
                     reads=['rowA'], writes=[('mnew', h)])
            mn_r = [('mnew', h) for h in range(4)]
            S.op('pool', lambda e: e.memset(mprev[:, 0:4], M_INIT), writes=['mprev0'])
            S.op('dve', lambda e: e.tensor_copy(out=mprev[:, 4:64], in_=mnew[:, 0:60]), reads=mn_r, writes=['mprev'])
            S.op('dve', lambda e: e.tensor_tensor(out=cdecr, in0=mprev, in1=mnew, op=ALU.subtract),
                 reads=mn_r + ['mprev', 'mprev0'], writes=['cdec_a'])
            S.op('dve', lambda e: e.tensor_tensor(out=cdecr, in0=cdecr, in1=rowbn, op=ALU.subtract),
                 reads=['cdec_a', 'rowA'], writes=['cdec_b'])
            S.op('act', lambda e: e.activation(out=cdecr, in_=cdecr, func=AF.Exp), reads=['cdec_b'], writes=['cdecr'])
            S.op('dve', lambda e: e.scalar_tensor_tensor(out=wlr, in0=rowbn, scalar=-1.0, in1=mnew, op0=ALU.mult,
                                                         op1=ALU.subtract), reads=mn_r + ['rowA'], writes=['wlr'])
            S.op('sp', lambda e: e.dma_start(out=pm.rearrange("(o h) -> o h", o=1), in_=mnew[:, 60:64]), reads=mn_r, dma=True)
            bt3 = bank()

            def col_mm(e):
                e.matmul(Pf(bt3)[0:64, 0:2], mprev, ones_f[0:1, 0:2], start=True, stop=True)
                e.matmul(Pf(bt3)[0:64, 2:4], wlr, ones_f[0:1, 0:2], start=True, stop=True)
                return e.matmul(Pf(bt3)[:, 64:128], ones_f[0:1, :], cdecr, start=True, stop=True)
            S.mark()
            S.op('pe', col_mm, reads=['mprev', 'mprev0', 'wlr', 'cdecr', 'ones'], writes=[PSR(bt3)])
            S.mark()
            S.op('act', lambda e: e.activation(out=colsb[0:64, :], in_=Pf(bt3)[0:64, 0:4], func=AF.Copy),
                 reads=[PSR(bt3)], writes=['cols'])
            S.op('act', lambda e: e.activation(out=cdec_bc.rearrange("p c h -> p (c h)"), in_=Pf(bt3)[:, 64:128], func=AF.Copy),
                 reads=[PSR(bt3)], writes=['cdec_bc'])
            mprev_col = colsb[0:64, 0:1]; wl_col = colsb[0:64, 2:3]
            S.op('dve', lambda e: e.tensor_scalar(out=r64('g'), in0=r64('A'), scalar1=mprev_col, scalar2=None, op0=ALU.max),
                 reads=['R_A', 'cols'], writes=['R_g'])
            S.op('sp', lambda e: e.dma_start(out=gscr, in_=r64('g')), reads=['R_g'], writes=['gscr'], dma=True)
            S.op('dve', lambda e: e.tensor_tensor(out=r64('m'), in0=r64('g'), in1=r64('bn'), op=ALU.subtract),
                 reads=['R_g', 'R_bn'], writes=['R_m'])
            S.op('act', lambda e: e.activation(out=r64('emt'), in_=r64('m'), func=AF.Exp, scale=-1.0),
                 reads=['R_m'], writes=['R_emt'])
            S.op('act', lambda e: e.activation(out=r64('dec'), in_=r64('g'), func=AF.Exp, scale=-1.0, bias=mprev_col),
                 reads=['R_g', 'cols'], writes=['R_dec'])
            S.op('act', lambda e: e.activation(out=r64('w'), in_=r64('a'), func=AF.Exp, bias=wl_col),
                 reads=['R_a', 'cols'], writes=['R_w'])
            bt4 = bank()

            def trq(e):
                for q, nm in enumerate(['a', 'w', 'dec', 'emt']):
                    inst = e.transpose(out=Pf(bt4)[:, q * 64:(q + 1) * 64], in_=r64(nm), identity=identf[0:64, 0:64])
                return inst
            S.mark()
            S.op('pe', trq, reads=['R_a', 'R_w', 'R_dec', 'R_emt', 'identf'], writes=[PSR(bt4)])
            S.mark()
            S.op('act', lambda e: e.activation(out=tokq.rearrange("p q c h -> p (q c h)"), in_=Pf(bt4)[:, 0:256], func=AF.Copy),
                 reads=[PSR(bt4)], writes=['tokq'])
            dbg('tokq', tokq.rearrange("p q c h -> p (q c h)"), [128, 256], ['tokq'])
            dbg('cdec', cdec_bc.rearrange("p c h -> p (c h)"), [128, 64], ['cdec_bc'])
            S.op('sp', lambda e: e.nop(), reads=GATE_NAMES, writes=GATE_NAMES + [(n_, h_) for n_ in ('zsh', 'qTs', 'kTs') for h_ in range(4)])

            S.op('sp', lambda e: e.dma_start(out=sm_sb[0:NS, :], in_=sm), writes=['sm_sb'], dma=True)
            igs = gsm[0:NS, 0:4]; fps = gsm[0:NS, 4:8]
            s_mt = sg[0:NS, 0, :]; s_dm = sg[0:NS, 1, :]; s_dec = sg[0:NS, 2, :]; s_emt = sg[0:NS, 3, :]
            s_t0 = sg2[0:NS, 0, :]; s_t1 = sg2[0:NS, 1, :]; s_mi = sg2[0:NS, 2, :]
            S.op('act', lambda e: e.activation(out=s_t0, in_=fps, func=AF.Exp, scale=-1.0), reads=['gsm'], writes=['s_t0'])
            S.op('act', lambda e: e.activation(out=s_t1, in_=s_t0, func=AF.Ln, bias=1.0), reads=['s_t0'], writes=['s_t1'])
            S.op('dve', lambda e: e.tensor_tensor(out=s_mi, in0=sm_sb[0:NS, :], in1=s_t1, op=ALU.subtract),
                 reads=['s_t1', 'sm_sb'], writes=['s_mi'])
            S.op('dve', lambda e: e.tensor_tensor(out=s_mt, in0=s_mi, in1=igs, op=ALU.max), reads=['s_mi', 'gsm'], writes=['s_mt'])
            S.op('sp', lambda e: e.dma_start(out=smo, in_=s_mt), reads=['s_mt'], dma=True)
            S.op('dve', lambda e: e.tensor_tensor(out=s_t0, in0=igs, in1=s_mt, op=ALU.subtract),
                 reads=['s_mt', 'gsm', 's_t0'], writes=['s_t0b'])
            S.op('act', lambda e: e.activation(out=s_dm, in_=s_t0, func=AF.Exp), reads=['s_t0b'], writes=['s_dm'])
            S.op('dve', lambda e: e.tensor_tensor(out=s_t1, in0=s_mi, in1=s_mt, op=ALU.subtract),
                 reads=['s_mt', 's_mi', 's_t1'], writes=['s_t1b'])
            S.op('act', lambda e: e.activation(out=s_dec, in_=s_t1, func=AF.Exp), reads=['s_t1b'], writes=['s_dec'])
            S.op('act', lambda e: e.activation(out=s_emt, in_=s_mt, func=AF.Exp, scale=-1.0), reads=['s_mt'], writes=['s_emt'])
            S.op('sp', lambda e: e.dma_start(out=dscr.rearrange("h b -> b h"), in_=s_dec, allow_slow_non_contiguous=True),
                 reads=['s_dec'], writes=['dscr'], dma=True)

            S.defer_end()
            bank_mode['m'] = 'main6'
            S.barrier(old=['norm'], new=['conv'])
            S.region = 'conv'
            o_cu = HT0
            cu_sb = A.f32(o_cu, [2050])
            gct = [A.f32(o_cu + 2050 + i * 512, [512]) for i in range(2)]
            acc = [A.f32(o_cu + 2050 + 1024 + i * 512, [512]) for i in range(2)]
            gbt = [A.f32(o_cu + 2050 + 2048 + i * 512, [512]) for i in range(2)]
            o_cs = o_cu + 2050 + 3072
            sct = A.f32(o_cs, [1024])
            cvs = A.f32(o_cs + 1024, [3, 16])
            cvt = A.f32(o_cs + 1024 + 48, [2, 16])
            cuo = A.f32(o_cs + 1024 + 48 + 32, [512])
            S.op('pool', lambda e: e.memset(cu_sb[:, 0:2], 0.0), writes=['cu0'])
            S.op('sp', lambda e: e.dma_start(out=sct[0:NS, :], in_=sconv.rearrange("b j c -> b (j c)")), writes=['sct'], dma=True)
            S.op('sp', lambda e: e.dma_start(out=sconvo[:, 0, :], in_=sconv[:, 1, :]), dma=True)
            bs = bank()

            def trsc(e):
                for i in range(8):
                    inst = e.transpose(out=Pf(bs)[:, i * 16:(i + 1) * 16], in_=sct[0:NS, i * 128:(i + 1) * 128],
                                       identity=identf[0:NS, 0:NS])
                return inst
            S.op('pe', trsc, reads=['sct', 'identf'], writes=[PSR(bs)])
            S.op('act', lambda e: e.activation(out=bufT.rearrange("p a b -> p (a b)"), in_=Pf(bs)[:, 0:128], func=AF.Copy),
                 reads=[PSR(bs)], writes=['bufT'])
            for c in range(4):
                wi = c % 2; wbuf = wbufs[wi]
                for jj in range(3):
                    col0 = 2056 + jj * 512 + c * 128
                    S.op('pool', lambda e, jj=jj, col0=col0, wbuf=wbuf: e.dma_start(out=wbuf[:, :, jj, :], in_=win_v[:, :, col0:col0 + 128]),
                         writes=[('wbuf', wi, jj)], dma=True)
                for tb in range(4):
                    bb = [bank(), bank(), bank()]
                    for jj in range(3):
                        mm_group(Pf(bb[jj]), [(wbuf[:, kc, jj, :], hT[:, kc, tb * 512:(tb + 1) * 512]) for kc in range(8)],
                                 hT_reads(gs=[tb]) + [('wbuf', wi, jj)], bb[jj])
                    i2 = tb % 2
                    S.op('act', lambda e, i2=i2, b=bb[1]: e.activation(out=gct[i2], in_=Pf(b), func=AF.Copy),
                         reads=[PSR(bb[1])], writes=[('gct', i2)])
                    S.op('dve', lambda e, i2=i2, b=bb[2], tb=tb: e.tensor_tensor(
                        out=cu_sb[:, 2 + tb * 512: 2 + (tb + 1) * 512], in0=gct[i2], in1=Pf(b), op=ALU.mult),
                        reads=[('gct', i2), PSR(bb[2])], writes=[('cu', tb)])
                    S.op('act', lambda e, i2=i2, b=bb[0]: e.activation(out=gbt[i2], in_=Pf(b), func=AF.Copy),
                         reads=[PSR(bb[0])], writes=[('gbt', i2)])
                    cur = [('cu', tb), ('cu', tb - 1) if tb > 0 else 'cu0']
                    S.op('dve', lambda e, i2=i2, tb=tb, c=c: e.tensor_scalar(
                        out=acc[i2], in0=cu_sb[:, tb * 512: tb * 512 + 512], scalar1=cwcol[:, c, 0:1], scalar2=None, op0=ALU.mult),
                        reads=cur + ['cwcol'], writes=[('acc', i2)])
                    for jw in (1, 2):
                        S.op('dve', lambda e, i2=i2, tb=tb, c=c, jw=jw: e.scalar_tensor_tensor(
                            out=acc[i2], in0=cu_sb[:, tb * 512 + jw: tb * 512 + jw + 512], scalar=cwcol[:, c, jw:jw + 1],
                            in1=acc[i2], op0=ALU.mult, op1=ALU.add), reads=cur + ['cwcol', ('acc', i2)], writes=[('acc', i2)])
                    S.op('dve', lambda e, i2=i2, tb=tb, c=c: e.tensor_tensor(
                        out=mixT[:, 4 + c, tb * 512:(tb + 1) * 512], in0=acc[i2], in1=gbt[i2], op=ALU.mult),
                        reads=[('acc', i2), ('gbt', i2)], writes=[('mixT', 4 + c, tb)])
                    S.replay_stage()
                S.op('sp', lambda e, c=c: e.dma_start(out=pconv.rearrange("j (c p) -> p c j", p=128)[:, c, :],
                                                      in_=cu_sb[:, 2048:2050], allow_slow_non_contiguous=True),
                     reads=[('cu', 3)], dma=True)
                bsx = bank()
                for jj in range(3):
                    mm_group(Pf(bsx)[:, jj * 16:(jj + 1) * 16], [(wbuf[:, kc, jj, :], hT[:, kc, SEQ:TC]) for kc in range(8)],
                             hT_reads(gs=[4]) + [('wbuf', wi, jj)], bsx)
                S.op('act', lambda e, b=bsx: e.activation(out=cvs.rearrange("p a b -> p (a b)"), in_=Pf(b)[:, 0:48], func=AF.Copy),
                     reads=[PSR(bsx)], writes=['cvs'])
                S.op('dve', lambda e, c=c: e.tensor_tensor(out=cuTs[:, c, :], in0=cvs[:, 1, :], in1=cvs[:, 2, :], op=ALU.mult),
                     reads=['cvs'], writes=[('cuTs', c)])
                S.op('dve', lambda e, c=c: e.tensor_scalar(out=cvt[:, 0, :], in0=bufT[:, c, :], scalar1=cwcol[:, c, 0:1],
                                                           scalar2=None, op0=ALU.mult), reads=['bufT', 'cwcol'], writes=['cvt0'])
                S.op('dve', lambda e, c=c: e.scalar_tensor_tensor(out=cvt[:, 1, :], in0=bufT[:, 4 + c, :], scalar=cwcol[:, c, 1:2],
                                                                  in1=cvt[:, 0, :], op0=ALU.mult, op1=ALU.add),
                     reads=['bufT', 'cwcol', 'cvt0'], writes=['cvt1'])
                S.op('dve', lambda e, c=c: e.scalar_tensor_tensor(out=cvt[:, 0, :], in0=cuTs[:, c, :], scalar=cwcol[:, c, 2:3],
                                                                  in1=cvt[:, 1, :], op0=ALU.mult, op1=ALU.add),
                     reads=[('cuTs', c), 'cwcol', 'cvt1', 'cvt0'], writes=['cvt2'])
                S.op('dve', lambda e, c=c: e.tensor_tensor(out=mixT[:, 4 + c, SEQ:TC], in0=cvt[:, 0, :], in1=cvs[:, 0, :], op=ALU.mult),
                     reads=['cvt2', 'cvs'], writes=[('mixT', 4 + c, 4)])
            S.replay_all()
            bank_mode['m'] = None
            bso = bank()

            def trcu(e):
                for c in range(4):
                    inst = e.transpose(out=Pf(bso)[0:NS, c * 128:(c + 1) * 128], in_=cuTs[:, c, :], identity=identf)
                return inst
            S.op('pe', trcu, reads=[('cuTs', c) for c in range(4)] + ['identf'], writes=[PSR(bso)])
            S.op('act', lambda e: e.activation(out=cuo[0:NS, :], in_=Pf(bso)[0:NS, :], func=AF.Copy), reads=[PSR(bso)], writes=['cuo'])
            S.op('sp', lambda e: e.dma_start(out=sconvo[:, 1, :], in_=cuo[0:NS, :]), reads=['cuo'], dma=True)
            dbg('mixT', mixT, [128, 8, TC], [('mixT', 4 + c, g) for c in range(4) for g in range(5)])

            done('conv')
            S.barrier(old=['conv'], new=['headp'])
            A.top = HT0
            o_qT = A.alloc(TC // 2); qT = A.bf16(o_qT, [TC])
            o_kT = A.alloc(TC // 2); kT = A.bf16(o_kT, [TC])
            o_ktm = A.alloc(16 * 64); k_tm = A.bf16(o_ktm, [16, 128])
            o_va = A.alloc(16 * 65); v_aug = A.bf16(o_va, [16, 130])
            o_og = A.alloc(16 * 64); og_t = A.bf16(o_og, [16, 128])
            o_kw = A.alloc(2 * 64); kw = [A.bf16(o_kw + i * 64, [128]) for i in range(2)]
            o_na = A.alloc(16 * 129); numaug = A.f32(o_na, [16, 129])
            o_gbc = A.alloc(2048); gbc = A.f32(o_gbc, [16, 128]); sqtmp = gbc
            o_E = A.alloc(2 * 128); E = [A.f32(o_E + i * 128, [128]) for i in range(2)]
            o_PT = A.alloc(2 * 64); PT = [A.bf16(o_PT + i * 64, [128]) for i in range(2)]
            o_ti = A.alloc(2 * 129); tmpi = [A.f32(o_ti + i * 129, [129]) for i in range(2)]
            o_Cs = A.alloc(2 * 129); Cst = [A.f32(o_Cs + i * 129, [129]) for i in range(2)]
            o_Cb = A.alloc(16 * 65); Cb_all = A.bf16(o_Cb, [16, 130])
            o_sm2 = A.alloc(80); hsm = A.f32(o_sm2, [5, 16])
            hmix = og_t
            HEAD_END = A.top
            assert HEAD_END <= NW - 2048, (HEAD_END, NW)

            def load_head_w(h_):
                wi_ = h_ % 2
                for jj in range(4):
                    col0 = jj * 512 + h_ * 128
                    S.op('pool', lambda e, jj=jj, col0=col0, wb_=wbufs[wi_]: e.dma_start(out=wb_[:, :, jj, :], in_=win_v[:, :, col0:col0 + 128]),
                         writes=[('wbuf', wi_, jj)], dma=True)

            load_head_w(0)
            for h in range(4):
                S.region = 'headp'
                wi = h % 2; wbuf = wbufs[wi]
                S.op('pool', lambda e: e.memset(v_aug[:, :, 128:130], 1.0), writes=['v_one'])
                done('h%da' % h)
                def fm_proj(h=h, wi=wi, wbuf=wbuf):
                    for jj, dstT, sc in ((0, qT, QSCALE), (1, kT, 1.0)):
                        for tb in range(4):
                            b = bank()
                            mm_group(Pf(b), [(wbuf[:, kc, jj, :], hT[:, kc, tb * 512:(tb + 1) * 512]) for kc in range(8)],
                                     hT_reads(gs=[tb]) + [('wbuf', wi, jj)], b)
                            nm = 'qT' if jj == 0 else 'kT'
                            if tb % 2 == 0:
                                S.op('act', lambda e, b=b, dstT=dstT, tb=tb, sc=sc: e.activation(
                                    out=dstT[:, tb * 512:(tb + 1) * 512], in_=Pf(b), func=AF.Copy, scale=sc),
                                    reads=[PSR(b)], writes=[(nm, tb)])
                            else:
                                S.op('dve', lambda e, b=b, dstT=dstT, tb=tb, sc=sc: e.tensor_scalar(
                                    out=dstT[:, tb * 512:(tb + 1) * 512], in0=Pf(b), scalar1=sc, scalar2=None, op0=ALU.mult),
                                    reads=[PSR(b)], writes=[(nm, tb)])
                            yield
                            yield
                        b = bank()
                        mm_group(Pf(b)[:, 0:16], [(wbuf[:, kc, jj, :], hT[:, kc, SEQ:TC]) for kc in range(8)],
                                 hT_reads(gs=[4]) + [('wbuf', wi, jj)], b)
                        dsts = qTs_all[:, h, :] if jj == 0 else kTs_all[:, h, :]
                        S.op('act', lambda e, b=b, dsts=dsts, sc=sc: e.activation(out=dsts, in_=Pf(b)[:, 0:16], func=AF.Copy, scale=sc),
                             reads=[PSR(b)], writes=[('qTs', h) if jj == 0 else ('kTs', h)])
                done('h%db' % h)
                for t in range(NT):
                    b = bank()
                    mm_group(Pf(b)[:, 0:384], [(hT[:, kc, t * 128:(t + 1) * 128], wbuf[:, kc, 1:4, :]) for kc in range(8)],
                             hT_reads(gs=[t // 4]) + [('wbuf', wi, jj) for jj in (1, 2, 3)], b)
                    import os
                    VAR = int(os.environ.get('KVAR', '7'))
                    if VAR & 1:
                        S.op('act', lambda e, b=b, t=t: e.activation(out=k_tm[:, t, :], in_=Pf(b)[:, 0:128], func=AF.Copy),
                             reads=[PSR(b)], writes=[('k_tm', t)])
                    if VAR & 2:
                        S.op('dve', lambda e, b=b, t=t: e.tensor_copy(out=v_aug[:, t, 0:128], in_=Pf(b)[:, 128:256]),
                             reads=[PSR(b)], writes=[('v_aug', t)])
                    if VAR & 4:
                        S.op('act', lambda e, b=b, t=t: e.activation(out=og_t[:, t, :], in_=Pf(b)[:, 256:384], func=AF.Tanh, scale=0.5),
                             reads=[PSR(b)], writes=[('og_t', t)])
                done('h%dc' % h)
                b = bank()
                mm_group(Pf(b)[0:NS, 0:512], [(hT[:, kc, SEQ:TC], wbuf[:, kc, 0:4, :]) for kc in range(8)],
                         hT_reads(gs=[4]) + [('wbuf', wi, jj) for jj in range(4)], b)
                S.op('act', lambda e, b=b, h=h: e.activation(out=zsh_all[0:NS, h, :], in_=Pf(b)[0:NS, 0:512], func=AF.Copy),
                     reads=[PSR(b)], writes=[('zsh', h)])
                done('h%dproj' % h)
                if h + 1 < 4:
                    load_head_w(h + 1)
                S.op('sp', lambda e, h=h: e.dma_start(
                    out=gbc, in_=bass.AP(tensor=gscr_t, offset=h * 128, ap=[[0, 128], [512, 16], [1, 128]])),
                    reads=['gscr'], writes=['gbc'], dma=True)
                def pass12(h=h):
                    S.op('pool', lambda e: e.memset(Cst[0], 0.0), writes=[('Cst', 0)])
                    S.op('pool', lambda e: e.memset(Cb_all[:, 0, :], 0.0), writes=[('Cb', 0)])
                    dcb = {}
                    for i in range(NT + 4):
                        if i < NT:
                            c = i; i3 = c % 2
                            S.op('dve', lambda e, i3=i3, c=c, h=h: e.tensor_scalar(out=kw[i3], in0=k_tm[:, c, :], scalar1=tokq[:, 1, c, h:h + 1],
                                                                                   scalar2=None, op0=ALU.mult),
                                 reads=[('k_tm', c), 'tokq'], writes=[('kw', i3)])
                        if 0 <= i - 1 < NT:
                            c = i - 1; i3 = c % 2
                            b3 = bank(); dcb[c] = b3
                            mm_group(Pf(b3)[:, 0:129], [(kw[i3], v_aug[:, c, 0:129])], [('kw', i3), ('v_aug', c), 'v_one'], b3)
                        if 0 <= i - 2 < NT:
                            c = i - 2
                            S.op('act', lambda e, c=c, b3=dcb[c]: e.activation(out=numaug[:, c, :], in_=Pf(b3)[:, 0:129], func=AF.Copy),
                                 reads=[PSR(dcb[c])], writes=[('numaug', c)])
                        if 0 <= i - 3 < NT:
                            c = i - 3; j0_ = c % 2; j1_ = (c + 1) % 2
                            S.op('dve', lambda e, j0_=j0_, j1_=j1_, c=c, h=h: e.scalar_tensor_tensor(
                                out=Cst[j1_], in0=Cst[j0_], scalar=cdec_bc[:, c, h:h + 1], in1=numaug[:, c, :], op0=ALU.mult, op1=ALU.add),
                                reads=[('numaug', c), ('Cst', j0_), 'cdec_bc'], writes=[('Cst', j1_)])
                            if c < NT - 1:
                                S.op('act', lambda e, j1_=j1_, c=c: e.activation(out=Cb_all[:, c + 1, 0:129], in_=Cst[j1_], func=AF.Copy),
                                     reads=[('Cst', j1_)], writes=[('Cb', c + 1)])
                        yield

                g1 = pass12(); g2 = fm_proj()
                alive = True
                while alive:
                    alive = False
                    for g_ in (g1, g2):
                        try:
                            next(g_); alive = True
                        except StopIteration:
                            pass
                sb_ = {}; pvb = {}
                ssn = hsm[:, 4, :]
                for i in range(NT + 4):
                    if i < NT:
                        c = i; i3 = c % 2
                        cs = slice(c * 128, (c + 1) * 128)
                        b1 = bank(); sb_[c] = b1
                        mm_group(Pf(b1)[:, 0:128], [(kT[:, cs], qT[:, cs])], [('kT', c // 4), ('qT', c // 4)], b1)
                        S.op('act', lambda e, i3=i3, c=c, h=h: e.activation(out=E[i3], in_=gbc[:, c, :], func=AF.Exp, scale=-1.0,
                                                                            bias=tokq[:, 0, c, h:h + 1]),
                             reads=['gbc', 'tokq'], writes=[('E', i3)])
                        S.op('pool', lambda e, i3=i3: e.affine_select(out=E[i3], in_=E[i3], pattern=[[1, 128]], compare_op=ALU.is_ge,
                                                                      fill=0.0, base=0, channel_multiplier=-1),
                             reads=[('E', i3)], writes=[('E', i3)])
                    if 0 <= i - 1 < NT:
                        c = i - 1; i3 = c % 2
                        b1 = sb_[c]
                        S.op('dve', lambda e, i3=i3, b1=b1: e.tensor_tensor(out=PT[i3], in0=E[i3], in1=Pf(b1)[:, 0:128], op=ALU.mult),
                             reads=[('E', i3), PSR(b1)], writes=[('PT', i3)])
                    if 0 <= i - 2 < NT:
                        c = i - 2; i3 = c % 2
                        cs = slice(c * 128, (c + 1) * 128)
                        b2 = bank(); pvb[c] = b2

                        def pv(e, b2=b2, i3=i3, c=c, cs=cs):
                            e.matmul(Pf(b2)[:, 0:129], PT[i3], v_aug[:, c, 0:129], start=True, stop=True)
                            return e.matmul(Pf(b2)[:, 256:385], qT[:, cs], Cb_all[:, c, 0:129], start=True, stop=True)
                        S.op('pe', pv, reads=[('PT', i3), ('v_aug', c), 'v_one', ('qT', c // 4), ('Cb', c)], writes=[PSR(b2)])
                    if 0 <= i - 3 < NT:
                        c = i - 3; i2 = c % 2
                        b2 = pvb[c]
                        S.op('act', lambda e, b2=b2, i2=i2, c=c, h=h: e.activation(out=tmpi[i2], in_=Pf(b2)[:, 256:385], func=AF.Copy,
                                                                                   scale=tokq[:, 2, c, h:h + 1]),
                             reads=[PSR(b2), 'tokq'], writes=[('tmpi', i2)])
                        S.op('dve', lambda e, b2=b2, i2=i2, c=c: e.tensor_tensor(out=numaug[:, c, :], in0=tmpi[i2], in1=Pf(b2)[:, 0:129],
                                                                                 op=ALU.add),
                             reads=[PSR(b2), ('tmpi', i2)], writes=[('numaug', c)])
                    if 0 <= i - 4 < NT:
                        c = i - 4
                        S.op('dve', lambda e, c=c: e.scalar_tensor_tensor(out=kw[0], in0=numaug[:, c, 0:128], scalar=1.0, in1=numaug[:, c, 0:128],
                                                                          op0=ALU.mult, op1=ALU.mult, accum_out=ssn[:, c:c + 1]),
                             reads=[('numaug', c)], writes=[('kw', 0), ('ssn', c)])
                jf = NT % 2
                S.op('sp', lambda e, h=h, jf=jf: e.dma_start(out=pC[h], in_=Cst[jf][:, 0:128]), reads=[('Cst', jf)], dma=True)
                S.op('sp', lambda e, h=h, jf=jf: e.dma_start(out=pn[h].rearrange("(p o) -> p o", o=1), in_=Cst[jf][:, 128:129]),
                     reads=[('Cst', jf)], dma=True)
                done('h%dchunk' % h)
                na_r = [('numaug', c) for c in range(NT)]
                dn = hsm[:, 0, :]; rr = hsm[:, 1, :]; ss2 = hsm[:, 2, :]; rs = hsm[:, 3, :]
                hmv = numaug[:, :, 0:128]
                S.op('dve', lambda e: e.scalar_tensor_tensor(out=dn, in0=numaug[:, :, 128], scalar=-1.0, in1=numaug[:, :, 128],
                                                             op0=ALU.mult, op1=ALU.max), reads=na_r, writes=['dn0'])
                S.op('dve', lambda e, h=h: e.tensor_tensor(out=dn, in0=dn, in1=tokq[:, 3, :, h], op=ALU.max),
                     reads=['dn0', 'tokq'], writes=['dn'])
                S.op('dve', lambda e: e.reciprocal(out=rr, in_=dn), reads=['dn'], writes=['rr'])
                S.op('dve', lambda e: e.tensor_tensor(out=ss2, in0=ssn, in1=rr, op=ALU.mult), reads=['rr'] + [('ssn', c) for c in range(NT)], writes=['ss2'])
                S.op('dve', lambda e: e.tensor_tensor(out=ss2, in0=ss2, in1=rr, op=ALU.mult), reads=['ss2', 'rr'], writes=['ss2a'])
                S.op('dve', lambda e: e.tensor_scalar(out=ss2, in0=ss2, scalar1=1.0 / 128, scalar2=EPS, op0=ALU.mult, op1=ALU.add),
                     reads=['ss2a'], writes=['ss2b'])
                S.op('pool', lambda e: e.tensor_tensor(out=rs, in0=ss2, in1=mhalf[:, 0:16], op=ALU.pow), reads=['ss2b', 'mhalf'], writes=['rs'])
                S.op('dve', lambda e: e.scalar_tensor_tensor(out=rs, in0=rs, scalar=0.5, in1=rr, op0=ALU.mult, op1=ALU.mult),
                     reads=['rs', 'rr'], writes=['rsb'])
                S.op('dve', lambda e: e.tensor_tensor(out=hmv, in0=hmv, in1=rs.unsqueeze(2).broadcast_to([128, 16, 128]), op=ALU.mult),
                     reads=na_r + ['rsb'], writes=['hmn'])
                S.op('dve', lambda e: e.scalar_tensor_tensor(out=hmix, in0=og_t, scalar=1.0, in1=hmv, op0=ALU.add, op1=ALU.mult),
                     reads=['hmn'] + [('og_t', t) for t in range(NT)], writes=[('og_t', t) for t in range(NT)])
                for half in range(2):
                    b = bank()

                    def trh(e, b=b, half=half):
                        for i in range(8):
                            inst = e.transpose(out=Pb(b)[:, i * 128:(i + 1) * 128], in_=hmix[:, half * 8 + i, :], identity=identb)
                        return inst
                    S.op('pe', trh, reads=['identb'] + [('og_t', half * 8 + i) for i in range(8)], writes=[PSR(b)])
                    if half == 0:
                        S.op('act', lambda e, b=b, h=h: e.activation(out=mixT[:, h, 0:1024], in_=Pb(b), func=AF.Copy,
                                                                     scale=mhcol[:, h:h + 1]),
                             reads=[PSR(b), 'mhcol'], writes=[('mixT', h, 0), ('mixT', h, 1)])
                    else:
                        S.op('dve', lambda e, b=b, h=h: e.tensor_scalar(out=mixT[:, h, 1024:2048], in0=Pb(b), scalar1=mhcol[:, h:h + 1],
                                                                        scalar2=None, op0=ALU.mult),
                             reads=[PSR(b), 'mhcol'], writes=[('mixT', h, 2), ('mixT', h, 3)])

                done('h%dpost' % h)
            dbg('mixT2', mixT, [128, 8, TC], [('mixT', k, g) for k in range(8) for g in range(5)])

            done('heads')
            S.barrier(old=['headp'], new=['wo'])
            S.region = 'wo'
            A.top = HT0
            o_wo = A.alloc(8 * D // 2); wo = A.bf16(o_wo, [8, D])
            sChb = [A.f32(A.alloc(2048), [16, 128]) for i in range(2)]
            wvbc = A.f32(o_wbuf, [16, 128])
            tmpC = [A.f32(A.alloc(128), [128]) for i in range(2)]
            snhb = [A.f32(A.alloc(128), [128]) for i in range(2)]
            dbcb = [A.f32(A.alloc(16), [16]) for i in range(2)]
            st0 = A.f32(A.alloc(128), [128]); st1 = A.f32(A.alloc(128), [128]); st2 = A.f32(A.alloc(128), [128])
            qCT = A.f32(A.alloc(16), [16]); ssm = A.f32(A.alloc(16), [16]); hms = A.bf16(A.alloc(64), [128])
            for half in range(2):
                S.op('pool', lambda e, half=half: e.dma_start(out=wo[:, :, half * 512:(half + 1) * 512],
                                                              in_=w_out.rearrange("(kc p) n -> p kc n", p=128)[:, :, half * 512:(half + 1) * 512]),
                     writes=[('wo', half)], dma=True)
            mix_all = lambda g: [('mixT', k, g) for k in range(8)]

            def load_state(h):
                hb = h % 2
                S.op('sp', lambda e, h=h, hb=hb: e.dma_start(out=sChb[hb], in_=sC[:, h, :, :].rearrange("b k v -> k b v")),
                     writes=[('sCh', hb)], dma=True)
                S.op('sp', lambda e, h=h, hb=hb: e.dma_start(out=snhb[hb][0:NS, :], in_=sn[:, h, :]), writes=[('snh', hb)], dma=True)
                S.op('sp', lambda e, h=h, hb=hb: e.dma_start(out=dbcb[hb], in_=bass.AP(tensor=dscr_t, offset=h * NS, ap=[[0, 128], [1, NS]])),
                     reads=['dscr'], writes=[('dbc', hb)], dma=True)

            def out_proj(t):
                r = rows(t)
                cs = slice(t * 128, t * 128 + r)
                for nb in range(2):
                    b = bank()
                    mm_group(Pf(b)[0:r, :], [(mixT[:, kc, cs], wo[:, kc, nb * 512:(nb + 1) * 512]) for kc in range(8)],
                             mix_all(t // 4 if t < 16 else 4) + [('wo', nb)], b)
                    S.op('dve', lambda e, b=b, t=t, r=r, nb=nb: e.tensor_tensor(out=x_sb[0:r, t, nb * 512:(nb + 1) * 512],
                                                                                in0=x_sb[0:r, t, nb * 512:(nb + 1) * 512], in1=Pf(b)[0:r, :], op=ALU.add),
                         reads=[PSR(b), ('x', t)], writes=[('x', t)])
                norm_sq(t, hT[:, 0, 0:1024], [('hT', 0, 0), ('hT', 0, 1)])
                if t == 16 or t % 4 == 3:
                    norm_rstd(4 if t == 16 else t // 4)

            def sample_head(h):
                hb = h % 2
                sCh = sChb[hb]; snh = snhb[hb]; dbc = dbcb[hb]
                zs = zsh_all[0:NS, h, :]
                qs = zs[:, 0:128]; ks = zs[:, 128:256]; vs = zs[:, 256:384]; ogs = zs[:, 384:512]
                qTs = qTs_all[:, h, :]; kTs = kTs_all[:, h, :]
                ZS = ('zsh', h); ZQ = ('zsh_q', h); SC = ('sCh', hb)
                t0 = st0[0:NS, :]; t1 = st1[0:NS, :]; t2 = st2[0:NS, :]
                sc = lambda i: ssm[0:NS, i:i + 1]
                S.op('dve', lambda e: e.tensor_scalar(out=qs, in0=qs, scalar1=QSCALE, scalar2=None, op0=ALU.mult),
                     reads=[ZS], writes=[ZQ])
                S.op('dve', lambda e: e.tensor_tensor(out=t0, in0=qs, in1=ks, op=ALU.mult), reads=[ZQ, ZS], writes=['st0'])
                S.op('dve', lambda e: e.tensor_reduce(out=sc(0), in_=t0, axis=AX.X, op=ALU.add), reads=['st0'], writes=['qk'])
                S.op('dve', lambda e: e.tensor_tensor(out=t0, in0=qs, in1=snh[0:NS, :], op=ALU.mult), reads=[ZQ, ('snh', hb), 'qk'], writes=['st0b'])
                S.op('dve', lambda e: e.tensor_reduce(out=sc(1), in_=t0, axis=AX.X, op=ALU.add), reads=['st0b'], writes=['qn'])
                S.op('dve', lambda e: e.tensor_tensor(out=sc(2), in0=sc(0), in1=s_dm[:, h:h + 1], op=ALU.mult),
                     reads=['qk', 's_dm'], writes=['scores'])
                S.op('dve', lambda e: e.scalar_tensor_tensor(out=sc(3), in0=sc(1), scalar=s_dec[:, h:h + 1], in1=sc(2),
                                                             op0=ALU.mult, op1=ALU.add), reads=['qn', 's_dec', 'scores'], writes=['den'])
                S.op('dve', lambda e: e.scalar_tensor_tensor(out=sc(4), in0=sc(3), scalar=-1.0, in1=sc(3), op0=ALU.mult, op1=ALU.max),
                     reads=['den'], writes=['denom0'])
                S.op('dve', lambda e: e.tensor_tensor(out=sc(4), in0=sc(4), in1=s_emt[:, h:h + 1], op=ALU.max),
                     reads=['denom0', 's_emt'], writes=['denom'])
                S.op('dve', lambda e: e.reciprocal(out=sc(5), in_=sc(4)), reads=['denom'], writes=['rden'])
                bq = bank()

                def qc(e, bq=bq):
                    for b_ in range(NS):
                        inst = e.matmul(Pf(bq)[:, b_:b_ + 1], sCh[:, b_, :], qTs[:, b_:b_ + 1], start=True, stop=True)
                    return inst
                S.op('pe', qc, reads=[SC, ('qTs', h)], writes=[PSR(bq)])
                S.op('act', lambda e, bq=bq: e.activation(out=qCT, in_=Pf(bq)[:, 0:16], func=AF.Copy), reads=[PSR(bq)], writes=['qCT'])
                bq2 = bank()
                S.op('pe', lambda e, bq2=bq2: e.transpose(out=Pf(bq2)[0:NS, 0:128], in_=qCT, identity=identf),
                     reads=['qCT', 'identf'], writes=[PSR(bq2)])
                S.op('dve', lambda e, bq2=bq2: e.tensor_scalar(out=t1, in0=Pf(bq2)[0:NS, 0:128], scalar1=s_dec[:, h:h + 1],
                                                               scalar2=None, op0=ALU.mult), reads=[PSR(bq2), 's_dec'], writes=['st1'])
                S.op('dve', lambda e: e.scalar_tensor_tensor(out=t1, in0=vs, scalar=sc(2), in1=t1, op0=ALU.mult, op1=ALU.add),
                     reads=[ZS, 'scores', 'st1'], writes=['num_s'])
                S.op('dve', lambda e: e.tensor_scalar(out=t1, in0=t1, scalar1=sc(5), scalar2=None, op0=ALU.mult),
                     reads=['num_s', 'rden'], writes=['hm_s'])
                S.op('dve', lambda e: e.tensor_tensor(out=t0, in0=t1, in1=t1, op=ALU.mult), reads=['hm_s', 'qn'], writes=['st0c'])
                S.op('dve', lambda e: e.tensor_reduce(out=sc(6), in_=t0, axis=AX.X, op=ALU.add), reads=['st0c'], writes=['ss_s'])
                S.op('dve', lambda e: e.tensor_scalar(out=sc(6), in0=sc(6), scalar1=1.0 / 128, scalar2=EPS, op0=ALU.mult, op1=ALU.add),
                     reads=['ss_s'], writes=['ss_sb'])
                S.op('pool', lambda e: e.tensor_tensor(out=sc(7), in0=sc(6), in1=mhalf[0:NS, 0:1], op=ALU.pow), reads=['ss_sb', 'mhalf'], writes=['rs_s'])
                S.op('dve', lambda e: e.tensor_scalar(out=sc(7), in0=sc(7), scalar1=0.5, scalar2=None, op0=ALU.mult), reads=['rs_s'], writes=['rs_sb'])
                S.op('act', lambda e: e.activation(out=t2, in_=ogs, func=AF.Tanh, scale=0.5), reads=[ZS], writes=['st2'])
                S.op('dve', lambda e: e.tensor_scalar(out=t1, in0=t1, scalar1=sc(7), scalar2=None, op0=ALU.mult), reads=['hm_s', 'rs_sb'], writes=['hmn_s'])
                S.op('dve', lambda e: e.scalar_tensor_tensor(out=hms[0:NS, :], in0=t2, scalar=1.0, in1=t1, op0=ALU.add, op1=ALU.mult),
                     reads=['st2', 'hmn_s'], writes=['hms'])
                bq3 = bank()
                S.op('pe', lambda e, bq3=bq3: e.transpose(out=Pb(bq3)[:, 0:16], in_=hms[0:NS, :], identity=identb[0:NS, 0:NS]),
                     reads=['hms', 'identb'], writes=[PSR(bq3)])
                S.op('dve', lambda e, bq3=bq3: e.tensor_scalar(out=mixT[:, h, SEQ:TC], in0=Pb(bq3)[:, 0:16], scalar1=mhcol[:, h:h + 1],
                                                               scalar2=None, op0=ALU.mult), reads=[PSR(bq3), 'mhcol'], writes=[('mixT', h, 4)])
                S.op('dve', lambda e: e.tensor_scalar(out=t0, in0=ks, scalar1=s_dm[:, h:h + 1], scalar2=None, op0=ALU.mult),
                     reads=[ZS, 's_dm', 'ss_s'], writes=['st0d'])
                S.op('dve', lambda e: e.scalar_tensor_tensor(out=t0, in0=snh[0:NS, :], scalar=s_dec[:, h:h + 1], in1=t0,
                                                             op0=ALU.mult, op1=ALU.add), reads=[('snh', hb), 's_dec', 'st0d'], writes=['nnew'])
                S.op('sp', lambda e: e.dma_start(out=sno[:, h, :], in_=t0), reads=['nnew'], writes=['st0'], dma=True)
                S.op('dve', lambda e: e.tensor_scalar(out=t2, in0=vs, scalar1=s_dm[:, h:h + 1], scalar2=None, op0=ALU.mult),
                     reads=[ZS, 's_dm', 'hms'], writes=['wv'])
                S.op('sp', lambda e: e.dma_start(out=wvscr[h], in_=t2), reads=['wv'], writes=[('wvscr', h), 'st2'], dma=True)
                S.op('sp', lambda e: e.dma_start(out=wvbc, in_=bass.AP(tensor=wvscr_t, offset=h * NS * 128,
                                                                        ap=[[0, 128], [128, NS], [1, 128]])),
                     reads=[('wvscr', h)], writes=['wvbc'] + [('wbuf', 0, jj) for jj in range(4)], dma=True)
                for b_ in range(NS):
                    i2 = b_ % 2
                    S.op('act', lambda e, b_=b_, i2=i2: e.activation(out=tmpC[i2], in_=sCh[:, b_, :], func=AF.Copy, scale=dbc[:, b_:b_ + 1]),
                         reads=[SC, ('dbc', hb), 'qCT'], writes=[('tmpC', i2)])
                    S.op('dve', lambda e, b_=b_, i2=i2: e.scalar_tensor_tensor(out=sCh[:, b_, :], in0=wvbc[:, b_, :], scalar=kTs[:, b_:b_ + 1],
                                                                               in1=tmpC[i2], op0=ALU.mult, op1=ALU.add),
                         reads=['wvbc', ('kTs', h), ('tmpC', i2)], writes=[('sChn', hb, b_)])
                S.op('sp', lambda e: e.dma_start(out=sCo[:, h, :, :].rearrange("b k v -> k b v"), in_=sCh),
                     reads=[('sChn', hb, b_) for b_ in range(NS)], writes=[SC], dma=True)

            load_state(0)
            load_state(1)
            for h in range(4):
                for t in range(4 * h, 4 * h + 4):
                    out_proj(t)
                sample_head(h)
                if h + 2 < 4:
                    load_state(h + 2)
            out_proj(16)
            dbg('x1', x_sb, [128, 17, D], [('x', t) for t in range(17)])
            done('outproj')

            A.top = PH
            wu = [None, None]; wd = [None, None]; hid = [None, None]
            wu[0] = A.bf16(A.alloc(8 * 512 // 2), [8, 512]); wd[0] = A.bf16(A.alloc(4 * D // 2), [4, D])
            hid[0] = A.bf16(A.alloc(4 * TC // 2), [4, TC])
            rtmp = [A.f32(A.alloc(512), [512]) for i in range(2)]
            o_nt = A.top
            wu[1] = A.bf16(A.alloc(8 * 512 // 2), [8, 512]); wd[1] = A.bf16(A.alloc(4 * D // 2), [4, D])
            hid[1] = A.bf16(A.alloc(4 * TC // 2), [4, TC])
            B_END = A.top
            o_wgp = A.alloc(8 * D // 2); wgp = A.bf16(o_wgp, [8, D])
            o_wpp = A.alloc(2 * D // 2); wpp = A.bf16(o_wpp, [2, D])
            o_pT = A.alloc(2 * TC // 2); pT = A.bf16(o_pT, [2, TC])
            o_gf = A.alloc(D); gfin = A.f32(o_gf, [D])
            pld = [A.f32(A.alloc(256), [256]) for i in range(1)]
            pbf = [A.bf16(A.alloc(128), [256]) for i in range(1)]
            NORM_NAMES = ['junk'] + [('xn', i_, k_) for i_ in range(2) for k_ in range(4)]
            S.barrier(old=None)
            S.region = 'B'
            S.op('sp', lambda e: e.nop(), writes=['phaseB_ok'])
            def prefetch_c():
                S.region = 'Cpre'
                S.op('pool', lambda e: e.dma_start(out=wgp, in_=w_pg.rearrange("(kc p) n -> p kc n", p=128)), writes=['wgp'], dma=True)
                S.op('pool', lambda e: e.dma_start(out=wpp, in_=w_pp.rearrange("(kc p) n -> p kc n", p=128)), writes=['wpp'], dma=True)
                S.op('sp', lambda e: e.dma_start(out=gfin, in_=nfin.partition_broadcast(128)), writes=['gfin'], dma=True)
                S.region = 'B'
                yield
                for t in range(17):
                    S.region = 'Cpre'
                    r = rows(t)
                    i2 = 0
                    src = pp[t * 128:(t + 1) * 128, :] if t < 16 else psm
                    S.op('sp', lambda e, i2=i2, r=r, src=src: e.dma_start(out=pld[i2][0:r, :], in_=src), writes=[('pld', i2)], dma=True)
                    S.op('act', lambda e, i2=i2, r=r: e.activation(out=pbf[i2][0:r, :], in_=pld[i2][0:r, :], func=AF.Copy),
                         reads=[('pld', i2)], writes=[('pbf', i2)])
                    b = bank()

                    def trp(e, b=b, i2=i2, r=r):
                        for kc in range(2):
                            inst = e.transpose(out=Pb(b)[:, kc * 128:kc * 128 + r], in_=pbf[i2][0:r, kc * 128:(kc + 1) * 128],
                                               identity=identb[0:r, 0:r])
                        return inst
                    S.op('pe', trp, reads=[('pbf', i2), 'identb'], writes=[PSR(b)])
                    S.op('dve', lambda e, b=b, t=t, r=r: e.tensor_copy(out=pT[:, :, t * 128:t * 128 + r],
                                                                       in_=Pb(b)[:, 0:256].rearrange("p (k t) -> p k t", k=2)[:, :, 0:r]),
                         reads=[PSR(b)], writes=[('pT', t)])
                    S.region = 'B'
                    yield

            wu_v = w_up.rearrange("(kc p) n -> p kc n", p=128)
            wd_v = w_down.rearrange("(fc p) n -> p fc n", p=128)
            NFB = 8
            for fb in range(NFB):
                i2 = fb % 2
                extra = NORM_NAMES if fb == 1 else []
                S.op('pool', lambda e, fb=fb, i2=i2: e.dma_start(out=wu[i2], in_=wu_v[:, :, fb * 512:(fb + 1) * 512]),
                     reads=['phaseB_ok'], writes=[('wu', i2)] + extra, dma=True)
                S.op('pool', lambda e, fb=fb, i2=i2: e.dma_start(out=wd[i2], in_=wd_v[:, fb * 4:(fb + 1) * 4, :]),
                     reads=['phaseB_ok'], writes=[('wd', i2)] + extra, dma=True)
                if fb == 0:
                    norm_to_hT(1, o_nt, presq=True)
                    S.region = 'B'
                if fb == 1:
                    pre_gen = prefetch_c()
                    next(pre_gen, None)
                ri = 0
                for fc in range(4):
                    for tb in range(5):
                        ncol = 512 if tb < 4 else NS
                        cs = slice(tb * 512, tb * 512 + ncol)
                        b = bank()
                        mm_group(Pf(b)[:, 0:ncol], [(wu[i2][:, kc, fc * 128:(fc + 1) * 128], hT[:, kc, cs]) for kc in range(8)],
                                 hT_reads(gs=[tb]) + [('wu', i2), 'phaseB_ok'], b)
                        if fb >= 1 and tb in (0, 2):
                            next(pre_gen, None)
                        rt = rtmp[ri % 2]; rk = ri % 2; ri += 1
                        S.op('act', lambda e, b=b, rt=rt, ncol=ncol: e.activation(out=rt[:, 0:ncol], in_=Pf(b)[:, 0:ncol], func=AF.Relu),
                             reads=[PSR(b)], writes=[('rtmp', rk)])
                        S.op('dve', lambda e, b=b, rt=rt, ncol=ncol, i2=i2, fc=fc, cs=cs: e.tensor_tensor(
                            out=hid[i2][:, fc, cs], in0=rt[:, 0:ncol], in1=Pf(b)[:, 0:ncol], op=ALU.mult),
                            reads=[PSR(b), ('rtmp', rk)], writes=[('hid', i2, fc, tb)] + (NORM_NAMES if (fb == 1 and fc == 0 and tb == 0) else []))
                for t in range(17):
                    r = rows(t)
                    cs = slice(t * 128, t * 128 + r)
                    tbk = t // 4 if t < 16 else 4
                    for nb in range(2):
                        b = bank()
                        mm_group(Pf(b)[0:r, :], [(hid[i2][:, fc, cs], wd[i2][:, fc, nb * 512:(nb + 1) * 512]) for fc in range(4)],
                                 [('hid', i2, fc, tbk) for fc in range(4)] + [('wd', i2)], b)
                        S.op('dve', lambda e, b=b, t=t, r=r, nb=nb: e.tensor_tensor(out=x_sb[0:r, t, nb * 512:(nb + 1) * 512],
                                                                                    in0=x_sb[0:r, t, nb * 512:(nb + 1) * 512], in1=Pf(b)[0:r, :], op=ALU.add),
                             reads=[PSR(b), ('x', t)], writes=[('x', t)])
                    if fb == NFB - 1:
                        S.region = 'norm'
                        norm_sq(t, hT[:, 0, 0:1024], [('hT', 0, 0), ('hT', 0, 1)])
                        if t == 16 or t % 4 == 3:
                            norm_rstd(4 if t == 16 else t // 4)
                        S.region = 'B'
            for _ in pre_gen:
                pass
            dbg('x2', x_sb, [128, 17, D], [('x', t) for t in range(17)])
            done('mlp')

            A.top = PH
            o_nt = A.alloc(512 + 4096)
            tht = [A.f32(A.alloc(D), [D]) for i in range(2)]
            ut = [A.f32(A.alloc(D), [D]) for i in range(2)]
            ysb = [A.f32(A.alloc(D), [D]) for i in range(2)]
            sqj = [A.bf16(A.alloc(D // 2), [D]) for i in range(2)]
            o_fs = A.alloc(40); fss = A.f32(o_fs, [2, 20])
            assert A.top <= B_END
            S.barrier(old=['B'], new=['norm'])
            S.region = 'C'
            S.op('pool', lambda e: e.memset(fss, 1.0), writes=['fss0'])
            norm_to_hT(2, o_nt, presq=True)
            S.region = 'C'
            def c_main(t):
                r = rows(t)
                cs = slice(t * 128, t * 128 + r)
                tbk = t // 4 if t < 16 else 4
                i2 = t % 2
                for nb in range(2):
                    bg_ = bank(); bp_ = bank()
                    ns = slice(nb * 512, (nb + 1) * 512)
                    mm_group(Pf(bg_)[0:r, :], [(hT[:, kc, cs], wgp[:, kc, ns]) for kc in range(8)], hT_reads(gs=[tbk]) + ['wgp'], bg_)
                    mm_group(Pf(bp_)[0:r, :], [(pT[:, kc, cs], wpp[:, kc, ns]) for kc in range(2)], [('pT', t), 'wpp'], bp_)
                    S.op('act', lambda e, b=bg_, i2=i2, r=r, ns=ns: e.activation(out=tht[i2][0:r, ns], in_=Pf(b)[0:r, :], func=AF.Tanh, scale=0.5),
                         reads=[PSR(bg_)], writes=[('tht', i2, nb)])
                    S.op('dve', lambda e, b=bp_, i2=i2, r=r, ns=ns: e.scalar_tensor_tensor(out=ut[i2][0:r, ns], in0=tht[i2][0:r, ns], scalar=1.0,
                                                                                          in1=Pf(b)[0:r, :], op0=ALU.add, op1=ALU.mult),
                         reads=[PSR(bp_), ('tht', i2, nb)], writes=[('ut', i2, nb)])
                    S.op('dve', lambda e, i2=i2, r=r, ns=ns, t=t: e.scalar_tensor_tensor(out=x_sb[0:r, t, ns], in0=ut[i2][0:r, ns], scalar=0.5,
                                                                                        in1=x_sb[0:r, t, ns], op0=ALU.mult, op1=ALU.add),
                         reads=[('ut', i2, nb), ('x', t)], writes=[('x', t)])

            def c_fin1(t):
                r = rows(t)
                i2 = t % 2
                S.op('act', lambda e, i2=i2, r=r, t=t: e.activation(out=sqj[i2][0:r, :], in_=x_sb[0:r, t, :], func=AF.Square,
                                                                   accum_out=fss[0:r, 0, t:t + 1]),
                     reads=[('x', t), 'fss0'], writes=[('sqj', i2), ('fss', t)])
                S.op('dve', lambda e, r=r, t=t: e.tensor_scalar(out=fss[0:r, 1, t:t + 1], in0=fss[0:r, 0, t:t + 1], scalar1=1.0 / D, scalar2=EPS,
                                                                op0=ALU.mult, op1=ALU.add), reads=[('fss', t)], writes=[('fss1', t)])
                S.op('pool', lambda e, r=r, t=t: e.tensor_tensor(out=fss[0:r, 0, t:t + 1], in0=fss[0:r, 1, t:t + 1], in1=mhalf[0:r, 0:1], op=ALU.pow),
                     reads=[('fss1', t), 'mhalf'], writes=[('frs', t)])

            def c_fin2(t):
                r = rows(t)
                i2 = t % 2
                S.op('dve', lambda e, i2=i2, r=r, t=t: e.scalar_tensor_tensor(out=ysb[i2][0:r, :], in0=x_sb[0:r, t, :], scalar=fss[0:r, 0, t:t + 1],
                                                                             in1=gfin[0:r, :], op0=ALU.mult, op1=ALU.mult),
                     reads=[('x', t), ('frs', t), 'gfin'], writes=[('ysb', i2)])
                dst = yp[t * 128:(t + 1) * 128, :] if t < 16 else ys
                S.op('sp', lambda e, i2=i2, r=r, dst=dst: e.dma_start(out=dst, in_=ysb[i2][0:r, :]), reads=[('ysb', i2)], dma=True)

            for i in range(17 + 2):
                if i < 17:
                    c_main(i)
                if 0 <= i - 1 < 17:
                    c_fin1(i - 1)
                if 0 <= i - 2 < 17:
                    c_fin2(i - 2)

        except _Stop:
            pass
        S.emit(st)
    return nc, dbg_outs


_CACHE = {}


def _prep(inputs, c):
    f = lambda a: np.ascontiguousarray(np.asarray(a, dtype=np.float32))
    sl = slice(c * NS, (c + 1) * NS)
    return {
        "xp": f(inputs["x_prompt"][c]), "xs": f(inputs["x_sample"][sl, 0]),
        "pp": f(inputs["p_prompt"][0, c]), "psm": f(inputs["p_sample"][0, sl, 0]),
        "sC": f(inputs["state_mlstm_C"][0, sl]), "sn": f(inputs["state_mlstm_n"][0, sl]),
        "sm": f(inputs["state_mlstm_m"][0, sl]), "sconv": f(inputs["state_conv"][0, sl]),
        "w_in": f(inputs["w_in"][0]), "w_out": f(inputs["w_out"][0]), "w_up": f(inputs["w_up"][0]),
        "w_down": f(inputs["w_down"][0]), "w_pg": f(inputs["w_ple_gate"][0]), "w_pp": f(inputs["w_ple_proj"][0]),
        "nmix": f(inputs["norm_mix"][0]), "nmlp": f(inputs["norm_mlp"][0]), "nple": f(inputs["norm_ple"][0]),
        "nfin": f(inputs["norm_final"]), "bgi": f(inputs["b_gate_i"][0]), "bgf": f(inputs["b_gate_f"][0]),
        "mhn": f(inputs["mh_norm"][0]), "cw": f(inputs["conv_w"][0]),
    }


def kernel(**inputs):
    if 'nc' not in _CACHE:
        _CACHE['nc'] = build()[0]
    nc = _CACHE['nc']
    in_maps = [_prep(inputs, c) for c in range(8)]
    res = run_bass_kernel_spmd(nc, in_maps, core_ids=list(range(8)))
    R = res.results
    g = lambda k: [np.asarray(R[c][k], dtype=np.float32) for c in range(8)]
    y_prompt = np.stack(g("yp"), 0)
    y_sample = np.concatenate(g("ys"), 0)[:, None, :]
    pC_ = np.stack(g("pC"), 0)[None]
    pn_ = np.stack(g("pn"), 0)[None]
    pm_ = np.stack(g("pm"), 0)[None]
    pconv_ = np.stack(g("pconv"), 0)[None]
    sC_ = np.concatenate(g("sCo"), 0)[None]
    sn_ = np.concatenate(g("sno"), 0)[None]
    sm_ = np.concatenate(g("smo"), 0)[None]
    sconv_ = np.concatenate(g("sconvo"), 0)[None]
    return (y_prompt, y_sample, pC_, pn_, pm_, pconv_, sC_, sn_, sm_, sconv_)
```

```python
import numpy as np
import concourse.bass as bass
import concourse.mybir as mybir
from concourse.bass_utils import run_bass_kernel_spmd
from contextlib import ExitStack

F32 = mybir.dt.float32
BF16 = mybir.dt.bfloat16
ALU = mybir.AluOpType
AF = mybir.ActivationFunctionType
AX = mybir.AxisListType

D = 1024
SEQ = 2048
NT = 16
NS = 16
TC = SEQ + NS
NIN = 3592
DFF = 4096
EPS = 1e-6
M_INIT = -1e30
QSCALE = 128.0 ** -0.5

ENGS = ['pe', 'act', 'dve', 'pool', 'sp']
PERSIST = {'x', 'hT', 'ss', 'rs1', 'rstd', 'ones', 'identf', 'identb', 'mhalf', 'gcol', 'mhcol', 'cwcol', 'gbb',
           'ps0', 'ps1', 'ps2', 'ps3', 'ps4', 'ps5', 'ps6', 'ps7', 'mixT', 'wbuf', 'wg', 'tokq', 'cdec_bc', 'cols',
           'zsh', 'zsh_q', 'qTs', 'kTs', 'gscr', 'dscr', 'wvscr', 's_dm', 's_dec', 's_emt', 's_mt', 'gsm', 'cuTs',
           'bufT', 'gtm', 'R_ig', 'R_t1', 'R_sp', 'R_bn', 'R_a', 'R_A', 'R_g', 'R_m', 'R_emt', 'R_dec', 'R_w',
           'rowA', 'mnew', 'mprev', 'mprev0', 'cdec_a', 'cdec_b', 'cdecr', 'wlr', 'sm_sb', 's_t0', 's_t1', 's_mi',
           's_t0b', 's_t1b'}
NDMASEM = 28
DMA_POOL = {'sp': list(range(0, 16)), 'pool': list(range(16, 24)), 'act': list(range(24, 28))}


class Sched:
    def __init__(self, nc):
        self.nc = nc
        self.ops = {e: [] for e in ENGS}
        self.last_write = {}
        self.readers = {}
        self.dma_n = {e: 0 for e in DMA_POOL}
        self.dma_cnt = [0] * NDMASEM
        self.region = None
        self.regnames = {}
        self.cur_barrier = None
        self.deferred = None
        self._replaying = False

    @staticmethod
    def _base(r):
        return r[0] if isinstance(r, tuple) else r

    def barrier(self, old=None, new=()):
        names = set()
        if old is None:
            names |= set(self.last_write) | set(self.readers)
        else:
            for rg in old:
                names |= self.regnames.get(rg, set())
        wr = set(names)
        for rg in new:
            wr |= self.regnames.get(rg, set())
        saved = self.region
        self.region = None
        b = self.op('sp', lambda e: e.nop(), reads=sorted(names, key=str), writes=sorted(wr, key=str))
        self.region = saved
        self.cur_barrier = b

    def defer_begin(self):
        self.deferred = []

    def mark(self):
        if self.deferred is not None and not self._replaying:
            self.deferred.append(None)

    def defer_end(self):
        self._stages = self.deferred
        self.deferred = None

    def replay_stage(self):
        st_ = getattr(self, '_stages', None)
        if not st_:
            return
        self._replaying = True
        while st_:
            item = st_.pop(0)
            if item is None:
                break
            self.op(*item)
        self._replaying = False

    def replay_all(self):
        while getattr(self, '_stages', None):
            self.replay_stage()

    def op(self, eng, fn, reads=(), writes=(), dma=False):
        if self.deferred is not None and not self._replaying:
            self.deferred.append((eng, fn, list(reads), list(writes), dma))
            return None
        idx = len(self.ops[eng])
        deps = set()
        for r in list(reads) + list(writes):
            if self.region is not None and self._base(r) not in PERSIST:
                self.regnames.setdefault(self.region, set()).add(r)
            if self.cur_barrier is not None and r not in self.last_write and r not in self.readers:
                deps.add(self.cur_barrier)
        ps_reads = [r for r in reads if isinstance(r, str) and r.startswith('ps') and r[2:].isdigit()]
        reads = [r for r in reads if r not in ps_reads]
        writes = list(writes) + ps_reads
        for r in reads:
            w = self.last_write.get(r)
            if w is not None:
                deps.add(w)
        for r in writes:
            w = self.last_write.get(r)
            if w is not None:
                deps.add(w)
            for rd in self.readers.get(r, ()):
                deps.add(rd)
        deps.discard((eng, idx))
        self._g = getattr(self, '_g', 0) + 1
        rec = dict(fn=fn, deps=deps, dma=dma, g=self._g)
        if dma:
            pool = DMA_POOL[eng]
            k = pool[self.dma_n[eng] % len(pool)]
            self.dma_n[eng] += 1
            self.dma_cnt[k] += 1
            rec['dsem'] = k
            rec['dval'] = 16 * self.dma_cnt[k]
        self.ops[eng].append(rec)
        for r in reads:
            self.readers.setdefault(r, []).append((eng, idx))
        for r in writes:
            self.last_write[r] = (eng, idx)
            self.readers[r] = []
        return (eng, idx)

    def emit(self, stack, final_wait_eng='sp'):
        nc = self.nc
        sem = {e: stack.enter_context(nc.semaphore('s_' + e)) for e in ENGS}
        dsem = [stack.enter_context(nc.semaphore('d_%d' % i)) for i in range(NDMASEM)]
        needed = set()
        for e in ENGS:
            for i, rec in enumerate(self.ops[e]):
                for d in rec['deps']:
                    de, di = d
                    if de == 'pe' and e == 'pe':
                        continue
                    if not self.ops[de][di]['dma']:
                        needed.add(d)
        rank = {}
        for e in ENGS:
            c = 0
            for i, rec in enumerate(self.ops[e]):
                if (e, i) in needed:
                    c += 1
                    rank[(e, i)] = c
        upto = {}
        for e in ENGS:
            c = 0
            for i, rec in enumerate(self.ops[e]):
                if (e, i) in rank:
                    c = rank[(e, i)]
                upto[(e, i)] = c
        K = {}
        prevK = {e: {} for e in ENGS}
        order = sorted((rec['g'], e, i) for e in ENGS for i, rec in enumerate(self.ops[e]))
        for _, e, i in order:
            rec = self.ops[e][i]
            k = dict(prevK[e])
            for d in rec['deps']:
                de, di = d
                if de == 'pe' and e == 'pe':
                    continue
                for en, r in K[d].items():
                    if k.get(en, 0) < r:
                        k[en] = r
            prevK[e] = k
            kk = dict(k)
            if not rec['dma']:
                if kk.get(e, 0) < upto[(e, i)]:
                    kk[e] = upto[(e, i)]
            K[(e, i)] = kk
        self._K = K
        block = stack.enter_context(nc.Block())
        handles = {'pe': block.tensor, 'act': block.scalar, 'dve': block.vector,
                   'pool': block.gpsimd, 'sp': block.sync}
        sched = self

        def make_body(e):
            def body(eng):
                waited = {}
                known = {}

                def learn(d):
                    for en, r in sched._K[d].items():
                        if known.get(en, 0) < r:
                            known[en] = r

                def wait(s, key, val):
                    if waited.get(key, 0) >= val:
                        return
                    eng.wait_ge(s, val)
                    waited[key] = val

                for i, rec in enumerate(sched.ops[e]):
                    for d in sorted(rec['deps']):
                        de, di = d
                        if de == 'pe' and e == 'pe':
                            continue
                        prod = sched.ops[de][di]
                        if prod['dma']:
                            wait(dsem[prod['dsem']], ('d', prod['dsem']), prod['dval'])
                        elif known.get(de, 0) < rank[d]:
                            wait(sem[de], ('c', de), rank[d])
                        learn(d)
                    if rec['dma']:
                        k = rec['dsem']
                        if rec['dval'] > 16:
                            wait(dsem[k], ('d', k), rec['dval'] - 16)
                        inst = rec['fn'](eng)
                        inst.then_inc(dsem[k], 16)
                    else:
                        inst = rec['fn'](eng)
                        if (e, i) in needed:
                            inst.then_inc(sem[e], 1)
                if e == final_wait_eng:
                    for k in range(NDMASEM):
                        if sched.dma_cnt[k] > 0:
                            wait(dsem[k], ('d', k), 16 * sched.dma_cnt[k])
            return body

        for e in ENGS:
            if self.ops[e] or e == final_wait_eng:
                handles[e](make_body(e))


def _view(ap2d, shape):
    if len(shape) == 1:
        return ap2d
    names = ['a%d' % i for i in range(len(shape))]
    pat = "p (" + " ".join(names) + ") -> p " + " ".join(names)
    kw = {n: s for n, s in zip(names[1:], shape[1:])}
    return ap2d.rearrange(pat, **kw)


class Arena:
    def __init__(self, t, nwords):
        self.t = t
        self.n = nwords
        self.top = 0

    def alloc(self, words):
        off = self.top
        self.top += int(words)
        assert self.top <= self.n, ("arena overflow", self.top, self.n)
        return off

    def f32(self, off, shape, parts=128):
        w = int(np.prod(shape))
        return _view(self.t[0:parts, off:off + w], shape)

    def bf16(self, off, shape, parts=128):
        n = int(np.prod(shape))
        w = (n + 1) // 2
        ap = self.t[0:parts, off:off + w].bitcast(BF16)
        if n != 2 * w:
            ap = ap[:, 0:n]
        return _view(ap, shape)


class _Stop(Exception):
    pass


def build(debug=None, stop=None):
    nc = bass.Bass("TRN2", target_bir_lowering=False)

    def din(name, shape):
        return nc.dram_tensor(name, list(shape), F32, kind="ExternalInput").ap()

    def dout(name, shape):
        return nc.dram_tensor(name, list(shape), F32, kind="ExternalOutput").ap()

    xp = din("xp", [SEQ, D]); xs = din("xs", [NS, D])
    pp = din("pp", [SEQ, 256]); psm = din("psm", [NS, 256])
    sC = din("sC", [NS, 4, 128, 128]); sn = din("sn", [NS, 4, 128]); sm = din("sm", [NS, 4])
    sconv = din("sconv", [NS, 2, 512])
    w_in = din("w_in", [D, NIN]); w_out = din("w_out", [D, D]); w_up = din("w_up", [D, DFF])
    w_down = din("w_down", [DFF, D]); w_pg = din("w_pg", [D, D]); w_pp = din("w_pp", [256, D])
    nmix = din("nmix", [D]); nmlp = din("nmlp", [D]); nple = din("nple", [D]); nfin = din("nfin", [D])
    bgi = din("bgi", [4]); bgf = din("bgf", [4]); mhn = din("mhn", [512]); cw = din("cw", [3, 512])

    yp = dout("yp", [SEQ, D]); ys = dout("ys", [NS, D])
    pC = dout("pC", [4, 128, 128]); pn = dout("pn", [4, 128]); pm = dout("pm", [4]); pconv = dout("pconv", [2, 512])
    sCo = dout("sCo", [NS, 4, 128, 128]); sno = dout("sno", [NS, 4, 128]); smo = dout("smo", [NS, 4])
    sconvo = dout("sconvo", [NS, 2, 512])

    gscr_t = nc.dram_tensor("gscr", [64, 128], F32, kind="Internal")
    gscr = gscr_t.ap()
    wvscr_t = nc.dram_tensor("wvscr", [4, NS, 128], F32, kind="Internal")
    wvscr = wvscr_t.ap()
    dscr_t = nc.dram_tensor("dscr", [4, NS], F32, kind="Internal")
    dscr = dscr_t.ap()

    dbg_outs = {}

    with ExitStack() as st:
        NW = 53000
        arena_t = st.enter_context(nc.sbuf_tensor("arena", [128, NW], F32))
        P = st.enter_context(nc.psum_tensor("psum", [128, 8, 512], F32))
        A = Arena(arena_t, NW)
        S = Sched(nc)

        psi = [0]

        bank_mode = {'m': None}
        psj = [0]

        def bank():
            if bank_mode['m'] == 'aux':
                b = 6 + psj[0] % 2
                psj[0] += 1
                return b
            if bank_mode['m'] == 'main6':
                b = psi[0] % 6
                psi[0] += 1
                return b
            b = psi[0] % 8
            psi[0] += 1
            return b

        def Pf(b):
            return P[:, b, :]

        def Pb(b):
            return P[:, b, :].bitcast(BF16)

        def PSR(b):
            return 'ps%d' % b

        def dbg(name, ap, shape, reads):
            if debug is None or name not in debug:
                return
            o = dout("dbg_" + name, shape)
            dbg_outs[name] = shape
            S.op('pool', lambda e: e.dma_start(out=o, in_=ap), reads=reads, dma=True)

        def done(tag):
            if stop == tag:
                raise _Stop()

        try:
            o_x = A.alloc(17 * D)
            x_sb = A.f32(o_x, [17, D])
            o_ht = A.alloc(8 * TC // 2)
            hT = A.bf16(o_ht, [8, TC])
            o_identf = A.alloc(128); identf = A.f32(o_identf, [128])
            o_identb = A.alloc(64); identb = A.bf16(o_identb, [128])
            o_ones = A.alloc(128); ones_f = A.f32(o_ones, [128])
            o_gcol = A.alloc(32); gcol = A.f32(o_gcol, [4, 8])
            o_mhc = A.alloc(4); mhcol = A.f32(o_mhc, [4])
            o_cwc = A.alloc(12); cwcol = A.f32(o_cwc, [4, 3])
            o_gbb = A.alloc(128); gb_bc = A.f32(o_gbb, [2, 16, 4])
            o_mh = A.alloc(32); mhalf = A.f32(o_mh, [32])
            o_ss = A.alloc(20); ss = A.f32(o_ss, [20])
            o_rs1 = A.alloc(20); rs1 = A.f32(o_rs1, [20])
            o_rstd = A.alloc(20); rstd = A.f32(o_rstd, [20])
            PH = A.top

            for t in range(NT):
                S.op('act' if t % 2 else 'sp', lambda e, t=t: e.dma_start(out=x_sb[:, t, :], in_=xp[t * 128:(t + 1) * 128, :]),
                     writes=[('x', t)], dma=True)
            S.op('sp', lambda e: e.dma_start(out=x_sb[0:NS, 16, :], in_=xs), writes=[('x', 16)], dma=True)

            S.op('pool', lambda e: e.memset(ones_f, 1.0), writes=['ones'])
            S.op('pool', lambda e: e.affine_select(out=identf, in_=ones_f, pattern=[[1, 128]], compare_op=ALU.is_equal,
                                                   fill=0.0, base=0, channel_multiplier=-1), reads=['ones'], writes=['identf'])
            S.op('pool', lambda e: e.tensor_copy(out=identb, in_=identf), reads=['identf'], writes=['identb'])
            S.op('pool', lambda e: e.memset(mhalf, -0.5), writes=['mhalf'])
            S.op('pool', lambda e: e.memset(ss, 1.0), writes=[('ss', t) for t in range(17)])
            for j, nv in enumerate([nmix, nmlp, nple]):
                S.op('sp', lambda e, j=j, nv=nv: e.dma_start(out=gcol[:, j, :], in_=nv.rearrange("(kc p) -> p kc", p=128),
                                                             allow_slow_non_contiguous=True), writes=[('gcol', j)], dma=True)
            S.op('sp', lambda e: e.dma_start(out=mhcol, in_=mhn.rearrange("(h p) -> p h", p=128),
                                             allow_slow_non_contiguous=True), writes=['mhcol'], dma=True)
            for jw in range(3):
                S.op('sp', lambda e, jw=jw: e.dma_start(out=cwcol[:, :, jw], in_=cw[jw].rearrange("(c p) -> p c", p=128),
                                                        allow_slow_non_contiguous=True), writes=['cwcol'], dma=True)
            for g, bv in enumerate([bgi, bgf]):
                S.op('sp', lambda e, g=g, bv=bv: e.dma_start(
                    out=gb_bc[:, g, :, :], in_=bass.AP(tensor=bv.tensor, offset=0, ap=[[0, 128], [0, 16], [1, 4]])),
                    writes=[('gbb', g)], dma=True)

            def rows(t):
                return 128 if t < NT else NS

            def norm_sq(t, junk_ap, junk_names):
                r = rows(t)
                S.op('act', lambda e, t=t, r=r: e.activation(out=junk_ap[0:r, :], in_=x_sb[0:r, t, :], func=AF.Square,
                                                             accum_out=ss[0:r, t:t + 1]),
                     reads=[('x', t)], writes=list(junk_names) + [('ss', t)])

            def norm_rstd(g):
                tiles = list(range(4 * g, 4 * g + 4)) if g < 4 else [16]
                c0, c1 = tiles[0], tiles[-1] + 1
                S.op('dve', lambda e: e.tensor_scalar(out=rs1[:, c0:c1], in0=ss[:, c0:c1], scalar1=1.0 / D, scalar2=EPS,
                                                      op0=ALU.mult, op1=ALU.add),
                     reads=[('ss', t) for t in tiles], writes=[('rs1', g)])
                S.op('pool', lambda e: e.tensor_tensor(out=rstd[:, c0:c1], in0=rs1[:, c0:c1], in1=mhalf[:, c0:c1], op=ALU.pow),
                     reads=[('rs1', g), 'mhalf'], writes=[('rstd', g)])

            def norm_to_hT(j, o_tmp, presq=False):
                S.region = 'norm'
                junk = A.bf16(o_tmp, [D])
                xn = [A.bf16(o_tmp + 512 + i * 2048, [4, D]) for i in range(2)]
                groups = [list(range(4 * g, 4 * g + 4)) for g in range(4)] + [[16]]

                def stage_sq(g):
                    tiles = groups[g]
                    if presq:
                        return
                    for t in tiles:
                        norm_sq(t, junk, ['junk'])
                    norm_rstd(g)

                def _unused(g):
                    tiles = groups[g]
                    c0, c1 = tiles[0], tiles[-1] + 1
                    S.op('dve', lambda e: e.tensor_scalar(out=rs1[:, c0:c1], in0=ss[:, c0:c1], scalar1=1.0 / D, scalar2=EPS,
                                                          op0=ALU.mult, op1=ALU.add),
                         reads=[('ss', t) for t in tiles], writes=[('rs1', g)])
                    S.op('pool', lambda e: e.tensor_tensor(out=rstd[:, c0:c1], in0=rs1[:, c0:c1], in1=mhalf[:, c0:c1], op=ALU.pow),
                         reads=[('rs1', g), 'mhalf'], writes=[('rstd', g)])

                def stage_main(g):
                    buf = xn[g % 2]
                    tiles = groups[g]
                    for i, t in enumerate(tiles):
                        r = rows(t)
                        if (t % 2) == 0:
                            S.op('act', lambda e, t=t, r=r, i=i, buf=buf: e.activation(
                                out=buf[0:r, i, :], in_=x_sb[0:r, t, :], func=AF.Copy, scale=rstd[0:r, t:t + 1]),
                                reads=[('x', t), ('rstd', g)], writes=[('xn', g % 2, i)])
                        else:
                            S.op('dve', lambda e, t=t, r=r, i=i, buf=buf: e.tensor_scalar(
                                out=buf[0:r, i, :], in0=x_sb[0:r, t, :], scalar1=rstd[0:r, t:t + 1], scalar2=None,
                                op0=ALU.mult), reads=[('x', t), ('rstd', g)], writes=[('xn', g % 2, i)])
                    if g < 4:
                        for kc in range(8):
                            b = bank()

                            def tr(e, b=b, kc=kc, buf=buf):
                                for i in range(4):
                                    inst = e.transpose(out=Pb(b)[:, i * 128:(i + 1) * 128],
                                                       in_=buf[:, i, kc * 128:(kc + 1) * 128], identity=identb)
                                return inst
                            S.op('pe', tr, reads=[('xn', g % 2, i) for i in range(4)] + ['identb'], writes=[PSR(b)])
                            dst = hT[:, kc, g * 512:(g + 1) * 512]
                            if kc % 2 == 0:
                                S.op('dve', lambda e, b=b, kc=kc, dst=dst: e.tensor_scalar(
                                    out=dst, in0=Pb(b)[:, 0:512], scalar1=gcol[:, j, kc:kc + 1], scalar2=None, op0=ALU.mult),
                                    reads=[PSR(b), ('gcol', j)], writes=[('hT', kc, g)])
                            else:
                                S.op('act', lambda e, b=b, kc=kc, dst=dst: e.activation(
                                    out=dst, in_=Pb(b)[:, 0:512], func=AF.Copy, scale=gcol[:, j, kc:kc + 1]),
                                    reads=[PSR(b), ('gcol', j)], writes=[('hT', kc, g)])
                    else:
                        b = bank()

                        def trs(e, b=b, buf=buf):
                            for kc in range(8):
                                inst = e.transpose(out=Pb(b)[:, kc * 16:(kc + 1) * 16],
                                                   in_=buf[0:NS, 0, kc * 128:(kc + 1) * 128], identity=identb[0:NS, 0:NS])
                            return inst
                        S.op('pe', trs, reads=[('xn', g % 2, 0), 'identb'], writes=[PSR(b)])
                        for kc in range(8):
                            S.op('dve', lambda e, b=b, kc=kc: e.tensor_scalar(
                                out=hT[:, kc, SEQ:TC], in0=Pb(b)[:, kc * 16:(kc + 1) * 16], scalar1=gcol[:, j, kc:kc + 1],
                                scalar2=None, op0=ALU.mult), reads=[PSR(b), ('gcol', j)], writes=[('hT', kc, 4)])

                stage_sq(0)
                for g in range(5):
                    if g + 1 < 5:
                        stage_sq(g + 1)
                    stage_main(g)

            def hT_reads(kcs=range(8), gs=range(5)):
                return [('hT', kc, g) for kc in kcs for g in gs]

            def mm_group(out_ap, pairs, reads, b):
                def fn(e):
                    n = len(pairs)
                    for i, (l, r) in enumerate(pairs):
                        inst = e.matmul(out_ap, l, r, start=(i == 0), stop=(i == n - 1))
                    return inst
                S.op('pe', fn, reads=reads, writes=[PSR(b)])

            win_v = w_in.rearrange("(kc p) n -> p kc n", p=128)

            A.top = PH
            o_mixT = A.alloc(8 * TC // 2); mixT = A.bf16(o_mixT, [8, TC])
            o_wbuf = A.alloc(2048); wbufs = [A.bf16(o_wbuf, [8, 4, 128]), A.bf16(NW - 2048, [8, 4, 128])]
            o_wg = A.alloc(32); wg = A.bf16(o_wg, [8, 8])
            o_gtm = A.alloc(128); gtm = A.f32(o_gtm, [128])
            o_gs = A.alloc(8); gsm = A.f32(o_gs, [8])
            o_tokq = A.alloc(256); tokq = A.f32(o_tokq, [4, 16, 4])
            o_cdbc = A.alloc(64); cdec_bc = A.f32(o_cdbc, [16, 4])
            o_sg = A.alloc(64); sg = A.f32(o_sg, [16, 4])
            o_cuTs = A.alloc(64); cuTs = A.f32(o_cuTs, [4, 16])
            o_bufT = A.alloc(128); bufT = A.f32(o_bufT, [8, 16])
            o_R0 = A.top
            R = {}
            for nm in ['ig', 't1', 'sp', 'bn', 'a', 'A', 'g', 'm', 'emt', 'dec', 'w']:
                R[nm] = A.f32(A.alloc(128), [128])
            o_rows = A.alloc(64 * 6); rowsb = A.f32(o_rows, [6, 64])
            o_cols = A.alloc(4); colsb = A.f32(o_cols, [4])
            A.top = max(A.top, o_R0 + 2048 + 128)
            zsh_all = A.f32(o_R0, [4, 512]); qTs_all = A.f32(o_R0 + 2048, [4, 16]); kTs_all = A.f32(o_R0 + 2112, [4, 16])
            GATE_NAMES = ['R_ig', 'R_t1', 'R_sp', 'R_bn', 'R_a', 'R_A', 'R_g', 'R_m', 'R_emt', 'R_dec', 'R_w', 'rowA', 'mprev',
                          'mprev0', 'cdec_a', 'cdec_b', 'cdecr', 'wlr', 'cols'] + [('mnew', h_) for h_ in range(4)]
            o_sm = A.alloc(4); sm_sb = A.f32(o_sm, [4])
            o_sg2 = A.alloc(16); sg2 = A.f32(o_sg2, [4, 4])
            HT0 = A.top

            norm_to_hT(0, HT0)
            dbg('hT', hT, [128, 8, TC], hT_reads())
            done('norm1')

            S.region = None
            S.defer_begin()
            bank_mode['m'] = 'aux'
            S.op('pool', lambda e: e.dma_start(out=wg, in_=win_v[:, :, 2048:2056]), writes=['wg'], dma=True)
            bg1 = bank()
            for t in range(NT):
                mm_group(Pf(bg1)[:, t * 8:(t + 1) * 8],
                         [(hT[:, kc, t * 128:(t + 1) * 128], wg[:, kc, :]) for kc in range(8)],
                         hT_reads(gs=[t // 4]) + ['wg'], bg1)
            bg2 = bank()
            mm_group(Pf(bg2)[0:NS, 0:8], [(hT[:, kc, SEQ:TC], wg[:, kc, :]) for kc in range(8)],
                     hT_reads(gs=[4]) + ['wg'], bg2)
            S.mark()
            S.op('dve', lambda e: e.tensor_tensor(
                out=gtm.rearrange("p (g c h) -> p g c h", g=2, c=16),
                in0=Pf(bg1)[:, 0:128].rearrange("p (c g h) -> p g c h", c=16, g=2),
                in1=gb_bc, op=ALU.add), reads=[PSR(bg1), ('gbb', 0), ('gbb', 1)], writes=['gtm'])
            S.op('dve', lambda e: e.tensor_tensor(
                out=gsm[0:NS, :].rearrange("p (g h) -> p g h", g=2), in0=Pf(bg2)[0:NS, 0:8].rearrange("p (g h) -> p g h", g=2),
                in1=gb_bc[0:NS, :, 0, :], op=ALU.add), reads=[PSR(bg2), ('gbb', 0), ('gbb', 1)], writes=['gsm'])
            bt = bank()

            def trg(e):
                e.transpose(out=Pf(bt)[0:64, 0:128], in_=gtm[:, 0:64], identity=identf)
                return e.transpose(out=Pf(bt)[0:64, 128:256], in_=gtm[:, 64:128], identity=identf)
            S.mark()
            S.op('pe', trg, reads=['gtm', 'identf'], writes=[PSR(bt)])
            S.mark()
            r64 = lambda nm: R[nm][0:64, :]
            S.op('act', lambda e: e.activation(out=r64('ig'), in_=Pf(bt)[0:64, 0:128], func=AF.Copy),
                 reads=[PSR(bt)], writes=['R_ig'])
            S.op('act', lambda e: e.activation(out=r64('t1'), in_=Pf(bt)[0:64, 128:256], func=AF.Exp, scale=-1.0),
                 reads=[PSR(bt)], writes=['R_t1'])
            S.op('act', lambda e: e.activation(out=r64('sp'), in_=r64('t1'), func=AF.Ln, bias=1.0),
                 reads=['R_t1'], writes=['R_sp'])
            S.op('dve', lambda e: e.tensor_tensor_scan(out=r64('bn'), data0=ones_f[0:64, :], data1=r64('sp'), initial=0.0,
                                                       op0=ALU.mult, op1=ALU.add), reads=['R_sp', 'ones'], writes=['R_bn'])
            S.op('dve', lambda e: e.tensor_tensor(out=r64('a'), in0=r64('ig'), in1=r64('bn'), op=ALU.add),
                 reads=['R_ig', 'R_bn'], writes=['R_a'])
            S.op('dve', lambda e: e.tensor_tensor_scan(out=r64('A'), data0=ones_f[0:64, :], data1=r64('a'), initial=-3.0e38,
                                                       op0=ALU.mult, op1=ALU.max), reads=['R_a', 'ones'], writes=['R_A'])
            bt2 = bank()

            def trl(e):
                e.transpose(out=Pf(bt2)[0:1, 0:64], in_=R['A'][0:64, 127:128], identity=identf[0:64, 0:64])
                return e.transpose(out=Pf(bt2)[0:1, 64:128], in_=R['bn'][0:64, 127:128], identity=identf[0:64, 0:64])
            S.mark()
            S.op('pe', trl, reads=['R_A', 'R_bn', 'identf'], writes=[PSR(bt2)])
            S.mark()
            rowA = rowsb[0:1, 0, :]; rowbn = rowsb[0:1, 1, :]; mnew = rowsb[0:1, 2, :]; mprev = rowsb[0:1, 3, :]
            cdecr = rowsb[0:1, 4, :]; wlr = rowsb[0:1, 5, :]
            S.op('act', lambda e: e.activation(out=rowsb[0:1, 0:2, :], in_=Pf(bt2)[0:1, 0:128].rearrange("p (a b) -> p a b", a=2),
                                               func=AF.Copy), reads=[PSR(bt2)], writes=['rowA'])
            for h in range(4):
                sl = lambda ap, h=h: ap.rearrange("p (c h) -> p c h", h=4)[:, :, h]
                S.op('dve', lambda e, sl=sl: e.tensor_tensor_scan(out=sl(mnew), data0=sl(rowA), data1=sl(rowbn),
                                                                  initial=M_INIT, op0=ALU.max, op1=ALU.subtract),
                     reads=['rowA'], writes=[('mnew', h)])
            mn_r = [('mnew', h) for h in range(4)]
            S.op('pool', lambda e: e.memset(mprev[:, 0:4], M_INIT), writes=['mprev0'])
            S.op('dve', lambda e: e.tensor_copy(out=mprev[:, 4:64], in_=mnew[:, 0:60]), reads=mn_r, writes=['mprev'])
            S.op('dve', lambda e: e.tensor_tensor(out=cdecr, in0=mprev, in1=mnew, op=ALU.subtract),
                 reads=mn_r + ['mprev', 'mprev0'], writes=['cdec_a'])
            S.op('dve', lambda e: e.tensor_tensor(out=cdecr, in0=cdecr, in1=rowbn, op=ALU.subtract),
                 reads=['cdec_a', 'rowA'], writes=['cdec_b'])
            S.op('act', lambda e: e.activation(out=cdecr, in_=cdecr, func=AF.Exp), reads=['cdec_b'], writes=['cdecr'])
            S.op('dve', lambda e: e.scalar_tensor_tensor(out=wlr, in0=rowbn, scalar=-1.0, in1=mnew, op0=ALU.mult,
                                                         op1=ALU.subtract), reads=mn_r + ['rowA'], writes=['wlr'])
            S.op('sp', lambda e: e.dma_start(out=pm.rearrange("(o h) -> o h", o=1), in_=mnew[:, 60:64]), reads=mn_r, dma=True)
            bt3 = bank()

            def col_mm(e):
                e.matmul(Pf(bt3)[0:64, 0:2], mprev, ones_f[0:1, 0:2], start=True, stop=True)
                e.matmul(Pf(bt3)[0:64, 2:4], wlr, ones_f[0:1, 0:2], start=True, stop=True)
                return e.matmul(Pf(bt3)[:, 64:128], ones_f[0:1, :], cdecr, start=True, stop=True)
            S.mark()
            S.op('pe', col_mm, reads=['mprev', 'mprev0', 'wlr', 'cdecr', 'ones'], writes=[PSR(bt3)])
            S.mark()
            S.op('act', lambda e: e.activation(out=colsb[0:64, :], in_=Pf(bt3)[0:64, 0:4], func=AF.Copy),
                 reads=[PSR(bt3)], writes=['cols'])
            S.op('act', lambda e: e.activation(out=cdec_bc.rearrange("p c h -> p (c h)"), in_=Pf(bt3)[:, 64:128], func=AF.Copy),
                 reads=[PSR(bt3)], writes=['cdec_bc'])
            mprev_col = colsb[0:64, 0:1]; wl_col = colsb[0:64, 2:3]
            S.op('dve', lambda e: e.tensor_scalar(out=r64('g'), in0=r64('A'), scalar1=mprev_col, scalar2=None, op0=ALU.max),
                 reads=['R_A', 'cols'], writes=['R_g'])
            S.op('sp', lambda e: e.dma_start(out=gscr, in_=r64('g')), reads=['R_g'], writes=['gscr'], dma=True)
            S.op('dve', lambda e: e.tensor_tensor(out=r64('m'), in0=r64('g'), in1=r64('bn'), op=ALU.subtract),
                 reads=['R_g', 'R_bn'], writes=['R_m'])
            S.op('act', lambda e: e.activation(out=r64('emt'), in_=r64('m'), func=AF.Exp, scale=-1.0),
                 reads=['R_m'], writes=['R_emt'])
            S.op('act', lambda e: e.activation(out=r64('dec'), in_=r64('g'), func=AF.Exp, scale=-1.0, bias=mprev_col),
                 reads=['R_g', 'cols'], writes=['R_dec'])
            S.op('act', lambda e: e.activation(out=r64('w'), in_=r64('a'), func=AF.Exp, bias=wl_col),
                 reads=['R_a', 'cols'], writes=['R_w'])
            bt4 = bank()

            def trq(e):
                for q, nm in enumerate(['a', 'w', 'dec', 'emt']):
                    inst = e.transpose(out=Pf(bt4)[:, q * 64:(q + 1) * 64], in_=r64(nm), identity=identf[0:64, 0:64])
                return inst
            S.mark()
            S.op('pe', trq, reads=['R_a', 'R_w', 'R_dec', 'R_emt', 'identf'], writes=[PSR(bt4)])
            S.mark()
            S.op('act', lambda e: e.activation(out=tokq.rearrange("p q c h -> p (q c h)"), in_=Pf(bt4)[:, 0:256], func=AF.Copy),
                 reads=[PSR(bt4)], writes=['tokq'])
            dbg('tokq', tokq.rearrange("p q c h -> p (q c h)"), [128, 256], ['tokq'])
            dbg('cdec', cdec_bc.rearrange("p c h -> p (c h)"), [128, 64], ['cdec_bc'])
            S.op('sp', lambda e: e.nop(), reads=GATE_NAMES, writes=GATE_NAMES + [(n_, h_) for n_ in ('zsh', 'qTs', 'kTs') for h_ in range(4)])

            S.op('sp', lambda e: e.dma_start(out=sm_sb[0:NS, :], in_=sm), writes=['sm_sb'], dma=True)
            igs = gsm[0:NS, 0:4]; fps = gsm[0:NS, 4:8]
            s_mt = sg[0:NS, 0, :]; s_dm = sg[0:NS, 1, :]; s_dec = sg[0:NS, 2, :]; s_emt = sg[0:NS, 3, :]
            s_t0 = sg2[0:NS, 0, :]; s_t1 = sg2[0:NS, 1, :]; s_mi = sg2[0:NS, 2, :]
            S.op('act', lambda e: e.activation(out=s_t0, in_=fps, func=AF.Exp, scale=-1.0), reads=['gsm'], writes=['s_t0'])
            S.op('act', lambda e: e.activation(out=s_t1, in_=s_t0, func=AF.Ln, bias=1.0), reads=['s_t0'], writes=['s_t1'])
            S.op('dve', lambda e: e.tensor_tensor(out=s_mi, in0=sm_sb[0:NS, :], in1=s_t1, op=ALU.subtract),
                 reads=['s_t1', 'sm_sb'], writes=['s_mi'])
            S.op('dve', lambda e: e.tensor_tensor(out=s_mt, in0=s_mi, in1=igs, op=ALU.max), reads=['s_mi', 'gsm'], writes=['s_mt'])
            S.op('sp', lambda e: e.dma_start(out=smo, in_=s_mt), reads=['s_mt'], dma=True)
            S.op('dve', lambda e: e.tensor_tensor(out=s_t0, in0=igs, in1=s_mt, op=ALU.subtract),
                 reads=['s_mt', 'gsm', 's_t0'], writes=['s_t0b'])
            S.op('act', lambda e: e.activation(out=s_dm, in_=s_t0, func=AF.Exp), reads=['s_t0b'], writes=['s_dm'])
            S.op('dve', lambda e: e.tensor_tensor(out=s_t1, in0=s_mi, in1=s_mt, op=ALU.subtract),
                 reads=['s_mt', 's_mi', 's_t1'], writes=['s_t1b'])
            S.op('act', lambda e: e.activation(out=s_dec, in_=s_t1, func=AF.Exp), reads=['s_t1b'], writes=['s_dec'])
            S.op('act', lambda e: e.activation(out=s_emt, in_=s_mt, func=AF.Exp, scale=-1.0), reads=['s_mt'], writes=['s_emt'])
            S.op('sp', lambda e: e.dma_start(out=dscr.rearrange("h b -> b h"), in_=s_dec, allow_slow_non_contiguous=True),
                 reads=['s_dec'], writes=['dscr'], dma=True)

            S.defer_end()
            bank_mode['m'] = 'main6'
            S.barrier(old=['norm'], new=['conv'])
            S.region = 'conv'
            o_cu = HT0
            cu_sb = A.f32(o_cu, [2050])
            gct = [A.f32(o_cu + 2050 + i * 512, [512]) for i in range(2)]
            acc = [A.f32(o_cu + 2050 + 1024 + i * 512, [512]) for i in range(2)]
            gbt = [A.f32(o_cu + 2050 + 2048 + i * 512, [512]) for i in range(2)]
            o_cs = o_cu + 2050 + 3072
            sct = A.f32(o_cs, [1024])
            cvs = A.f32(o_cs + 1024, [3, 16])
            cvt = A.f32(o_cs + 1024 + 48, [2, 16])
            cuo = A.f32(o_cs + 1024 + 48 + 32, [512])
            S.op('pool', lambda e: e.memset(cu_sb[:, 0:2], 0.0), writes=['cu0'])
            S.op('sp', lambda e: e.dma_start(out=sct[0:NS, :], in_=sconv.rearrange("b j c -> b (j c)")), writes=['sct'], dma=True)
            S.op('sp', lambda e: e.dma_start(out=sconvo[:, 0, :], in_=sconv[:, 1, :]), dma=True)
            bs = bank()

            def trsc(e):
                for i in range(8):
                    inst = e.transpose(out=Pf(bs)[:, i * 16:(i + 1) * 16], in_=sct[0:NS, i * 128:(i + 1) * 128],
                                       identity=identf[0:NS, 0:NS])
                return inst
            S.op('pe', trsc, reads=['sct', 'identf'], writes=[PSR(bs)])
            S.op('act', lambda e: e.activation(out=bufT.rearrange("p a b -> p (a b)"), in_=Pf(bs)[:, 0:128], func=AF.Copy),
                 reads=[PSR(bs)], writes=['bufT'])
            for c in range(4):
                wi = c % 2; wbuf = wbufs[wi]
                for jj in range(3):
                    col0 = 2056 + jj * 512 + c * 128
                    S.op('pool', lambda e, jj=jj, col0=col0, wbuf=wbuf: e.dma_start(out=wbuf[:, :, jj, :], in_=win_v[:, :, col0:col0 + 128]),
                         writes=[('wbuf', wi, jj)], dma=True)
                for tb in range(4):
                    bb = [bank(), bank(), bank()]
                    for jj in range(3):
                        mm_group(Pf(bb[jj]), [(wbuf[:, kc, jj, :], hT[:, kc, tb * 512:(tb + 1) * 512]) for kc in range(8)],
                                 hT_reads(gs=[tb]) + [('wbuf', wi, jj)], bb[jj])
                    i2 = tb % 2
                    S.op('act', lambda e, i2=i2, b=bb[1]: e.activation(out=gct[i2], in_=Pf(b), func=AF.Copy),
                         reads=[PSR(bb[1])], writes=[('gct', i2)])
                    S.op('dve', lambda e, i2=i2, b=bb[2], tb=tb: e.tensor_tensor(
                        out=cu_sb[:, 2 + tb * 512: 2 + (tb + 1) * 512], in0=gct[i2], in1=Pf(b), op=ALU.mult),
                        reads=[('gct', i2), PSR(bb[2])], writes=[('cu', tb)])
                    S.op('act', lambda e, i2=i2, b=bb[0]: e.activation(out=gbt[i2], in_=Pf(b), func=AF.Copy),
                         reads=[PSR(bb[0])], writes=[('gbt', i2)])
                    cur = [('cu', tb), ('cu', tb - 1) if tb > 0 else 'cu0']
                    S.op('dve', lambda e, i2=i2, tb=tb, c=c: e.tensor_scalar(
                        out=acc[i2], in0=cu_sb[:, tb * 512: tb * 512 + 512], scalar1=cwcol[:, c, 0:1], scalar2=None, op0=ALU.mult),
                        reads=cur + ['cwcol'], writes=[('acc', i2)])
                    for jw in (1, 2):
                        S.op('dve', lambda e, i2=i2, tb=tb, c=c, jw=jw: e.scalar_tensor_tensor(
                            out=acc[i2], in0=cu_sb[:, tb * 512 + jw: tb * 512 + jw + 512], scalar=cwcol[:, c, jw:jw + 1],
                            in1=acc[i2], op0=ALU.mult, op1=ALU.add), reads=cur + ['cwcol', ('acc', i2)], writes=[('acc', i2)])
                    S.op('dve', lambda e, i2=i2, tb=tb, c=c: e.tensor_tensor(
                        out=mixT[:, 4 + c, tb * 512:(tb + 1) * 512], in0=acc[i2], in1=gbt[i2], op=ALU.mult),
                        reads=[('acc', i2), ('gbt', i2)], writes=[('mixT', 4 + c, tb)])
                    S.replay_stage()
                S.op('sp', lambda e, c=c: e.dma_start(out=pconv.rearrange("j (c p) -> p c j", p=128)[:, c, :],
                                                      in_=cu_sb[:, 2048:2050], allow_slow_non_contiguous=True),
                     reads=[('cu', 3)], dma=True)
                bsx = bank()
                for jj in range(3):
                    mm_group(Pf(bsx)[:, jj * 16:(jj + 1) * 16], [(wbuf[:, kc, jj, :], hT[:, kc, SEQ:TC]) for kc in range(8)],
                             hT_reads(gs=[4]) + [('wbuf', wi, jj)], bsx)
                S.op('act', lambda e, b=bsx: e.activation(out=cvs.rearrange("p a b -> p (a b)"), in_=Pf(b)[:, 0:48], func=AF.Copy),
                     reads=[PSR(bsx)], writes=['cvs'])
                S.op('dve', lambda e, c=c: e.tensor_tensor(out=cuTs[:, c, :], in0=cvs[:, 1, :], in1=cvs[:, 2, :], op=ALU.mult),
                     reads=['cvs'], writes=[('cuTs', c)])
                S.op('dve', lambda e, c=c: e.tensor_scalar(out=cvt[:, 0, :], in0=bufT[:, c, :], scalar1=cwcol[:, c, 0:1],
                                                           scalar2=None, op0=ALU.mult), reads=['bufT', 'cwcol'], writes=['cvt0'])
                S.op('dve', lambda e, c=c: e.scalar_tensor_tensor(out=cvt[:, 1, :], in0=bufT[:, 4 + c, :], scalar=cwcol[:, c, 1:2],
                                                                  in1=cvt[:, 0, :], op0=ALU.mult, op1=ALU.add),
                     reads=['bufT', 'cwcol', 'cvt0'], writes=['cvt1'])
                S.op('dve', lambda e, c=c: e.scalar_tensor_tensor(out=cvt[:, 0, :], in0=cuTs[:, c, :], scalar=cwcol[:, c, 2:3],
                                                                  in1=cvt[:, 1, :], op0=ALU.mult, op1=ALU.add),
                     reads=[('cuTs', c), 'cwcol', 'cvt1', 'cvt0'], writes=['cvt2'])
                S.op('dve', lambda e, c=c: e.tensor_tensor(out=mixT[:, 4 + c, SEQ:TC], in0=cvt[:, 0, :], in1=cvs[:, 0, :], op=ALU.mult),
                     reads=['cvt2', 'cvs'], writes=[('mixT', 4 + c, 4)])
            S.replay_all()
            bank_mode['m'] = None
            bso = bank()

            def trcu(e):
                for c in range(4):
                    inst = e.transpose(out=Pf(bso)[0:NS, c * 128:(c + 1) * 128], in_=cuTs[:, c, :], identity=identf)
                return inst
            S.op('pe', trcu, reads=[('cuTs', c) for c in range(4)] + ['identf'], writes=[PSR(bso)])
            S.op('act', lambda e: e.activation(out=cuo[0:NS, :], in_=Pf(bso)[0:NS, :], func=AF.Copy), reads=[PSR(bso)], writes=['cuo'])
            S.op('sp', lambda e: e.dma_start(out=sconvo[:, 1, :], in_=cuo[0:NS, :]), reads=['cuo'], dma=True)
            dbg('mixT', mixT, [128, 8, TC], [('mixT', 4 + c, g) for c in range(4) for g in range(5)])

            done('conv')
            S.barrier(old=['conv'], new=['headp'])
            A.top = HT0
            o_qT = A.alloc(TC // 2); qT = A.bf16(o_qT, [TC])
            o_kT = A.alloc(TC // 2); kT = A.bf16(o_kT, [TC])
            o_ktm = A.alloc(16 * 64); k_tm = A.bf16(o_ktm, [16, 128])
            o_va = A.alloc(16 * 65); v_aug = A.bf16(o_va, [16, 130])
            o_og = A.alloc(16 * 64); og_t = A.bf16(o_og, [16, 128])
            o_kw = A.alloc(2 * 64); kw = [A.bf16(o_kw + i * 64, [128]) for i in range(2)]
            o_na = A.alloc(16 * 129); numaug = A.f32(o_na, [16, 129])
            o_gbc = A.alloc(2048); gbc = A.f32(o_gbc, [16, 128]); sqtmp = gbc
            o_E = A.alloc(2 * 128); E = [A.f32(o_E + i * 128, [128]) for i in range(2)]
            o_PT = A.alloc(2 * 64); PT = [A.bf16(o_PT + i * 64, [128]) for i in range(2)]
            o_ti = A.alloc(2 * 129); tmpi = [A.f32(o_ti + i * 129, [129]) for i in range(2)]
            o_Cs = A.alloc(2 * 129); Cst = [A.f32(o_Cs + i * 129, [129]) for i in range(2)]
            o_Cb = A.alloc(16 * 65); Cb_all = A.bf16(o_Cb, [16, 130])
            o_sm2 = A.alloc(80); hsm = A.f32(o_sm2, [5, 16])
            hmix = og_t
            HEAD_END = A.top
            assert HEAD_END <= NW - 2048, (HEAD_END, NW)

            def load_head_w(h_):
                wi_ = h_ % 2
                for jj in range(4):
                    col0 = jj * 512 + h_ * 128
                    S.op('pool', lambda e, jj=jj, col0=col0, wb_=wbufs[wi_]: e.dma_start(out=wb_[:, :, jj, :], in_=win_v[:, :, col0:col0 + 128]),
                         writes=[('wbuf', wi_, jj)], dma=True)

            load_head_w(0)
            for h in range(4):
                S.region = 'headp'
                wi = h % 2; wbuf = wbufs[wi]
                S.op('pool', lambda e: e.memset(v_aug[:, :, 128:130], 1.0), writes=['v_one'])
                done('h%da' % h)
                def fm_proj(h=h, wi=wi, wbuf=wbuf):
                    for jj, dstT, sc in ((0, qT, QSCALE), (1, kT, 1.0)):
                        for tb in range(4):
                            b = bank()
                            mm_group(Pf(b), [(wbuf[:, kc, jj, :], hT[:, kc, tb * 512:(tb + 1) * 512]) for kc in range(8)],
                                     hT_reads(gs=[tb]) + [('wbuf', wi, jj)], b)
                            nm = 'qT' if jj == 0 else 'kT'
                            if tb % 2 == 0:
                                S.op('act', lambda e, b=b, dstT=dstT, tb=tb, sc=sc: e.activation(
                                    out=dstT[:, tb * 512:(tb + 1) * 512], in_=Pf(b), func=AF.Copy, scale=sc),
                                    reads=[PSR(b)], writes=[(nm, tb)])
                            else:
                                S.op('dve', lambda e, b=b, dstT=dstT, tb=tb, sc=sc: e.tensor_scalar(
                                    out=dstT[:, tb * 512:(tb + 1) * 512], in0=Pf(b), scalar1=sc, scalar2=None, op0=ALU.mult),
                                    reads=[PSR(b)], writes=[(nm, tb)])
                            yield
                            yield
                        b = bank()
                        mm_group(Pf(b)[:, 0:16], [(wbuf[:, kc, jj, :], hT[:, kc, SEQ:TC]) for kc in range(8)],
                                 hT_reads(gs=[4]) + [('wbuf', wi, jj)], b)
                        dsts = qTs_all[:, h, :] if jj == 0 else kTs_all[:, h, :]
                        S.op('act', lambda e, b=b, dsts=dsts, sc=sc: e.activation(out=dsts, in_=Pf(b)[:, 0:16], func=AF.Copy, scale=sc),
                             reads=[PSR(b)], writes=[('qTs', h) if jj == 0 else ('kTs', h)])
                done('h%db' % h)
                for t in range(NT):
                    b = bank()
                    mm_group(Pf(b)[:, 0:384], [(hT[:, kc, t * 128:(t + 1) * 128], wbuf[:, kc, 1:4, :]) for kc in range(8)],
                             hT_reads(gs=[t // 4]) + [('wbuf', wi, jj) for jj in (1, 2, 3)], b)
                    import os
                    VAR = int(os.environ.get('KVAR', '7'))
                    if VAR & 1:
                        S.op('act', lambda e, b=b, t=t: e.activation(out=k_tm[:, t, :], in_=Pf(b)[:, 0:128], func=AF.Copy),
                             reads=[PSR(b)], writes=[('k_tm', t)])
                    if VAR & 2:
                        S.op('dve', lambda e, b=b, t=t: e.tensor_copy(out=v_aug[:, t, 0:128], in_=Pf(b)[:, 128:256]),
                             reads=[PSR(b)], writes=[('v_aug', t)])
                    if VAR & 4:
                        S.op('act', lambda e, b=b, t=t: e.activation(out=og_t[:, t, :], in_=Pf(b)[:, 256:384], func=AF.Tanh, scale=0.5),
                             reads=[PSR(b)], writes=[('og_t', t)])
                done('h%dc' % h)
                b = bank()
                mm_group(Pf(b)[0:NS, 0:512], [(hT[:, kc, SEQ:TC], wbuf[:, kc, 0:4, :]) for kc in range(8)],
                         hT_reads(gs=[4]) + [('wbuf', wi, jj) for jj in range(4)], b)
                S.op('act', lambda e, b=b, h=h: e.activation(out=zsh_all[0:NS, h, :], in_=Pf(b)[0:NS, 0:512], func=AF.Copy),
                     reads=[PSR(b)], writes=[('zsh', h)])
                done('h%dproj' % h)
                if h + 1 < 4:
                    load_head_w(h + 1)
                S.op('sp', lambda e, h=h: e.dma_start(
                    out=gbc, in_=bass.AP(tensor=gscr_t, offset=h * 128, ap=[[0, 128], [512, 16], [1, 128]])),
                    reads=['gscr'], writes=['gbc'], dma=True)
                def pass12(h=h):
                    S.op('pool', lambda e: e.memset(Cst[0], 0.0), writes=[('Cst', 0)])
                    S.op('pool', lambda e: e.memset(Cb_all[:, 0, :], 0.0), writes=[('Cb', 0)])
                    dcb = {}
                    for i in range(NT + 4):
                        if i < NT:
                            c = i; i3 = c % 2
                            S.op('dve', lambda e, i3=i3, c=c, h=h: e.tensor_scalar(out=kw[i3], in0=k_tm[:, c, :], scalar1=tokq[:, 1, c, h:h + 1],
                                                                                   scalar2=None, op0=ALU.mult),
                                 reads=[('k_tm', c), 'tokq'], writes=[('kw', i3)])
                        if 0 <= i - 1 < NT:
                            c = i - 1; i3 = c % 2
                            b3 = bank(); dcb[c] = b3
                            mm_group(Pf(b3)[:, 0:129], [(kw[i3], v_aug[:, c, 0:129])], [('kw', i3), ('v_aug', c), 'v_one'], b3)
                        if 0 <= i - 2 < NT:
                            c = i - 2
                            S.op('act', lambda e, c=c, b3=dcb[c]: e.activation(out=numaug[:, c, :], in_=Pf(b3)[:, 0:129], func=AF.Copy),
                                 reads=[PSR(dcb[c])], writes=[('numaug', c)])
                        if 0 <= i - 3 < NT:
                            c = i - 3; j0_ = c % 2; j1_ = (c + 1) % 2
                            S.op('dve', lambda e, j0_=j0_, j1_=j1_, c=c, h=h: e.scalar_tensor_tensor(
                                out=Cst[j1_], in0=Cst[j0_], scalar=cdec_bc[:, c, h:h + 1], in1=numaug[:, c, :], op0=ALU.mult, op1=ALU.add),
                                reads=[('numaug', c), ('Cst', j0_), 'cdec_bc'], writes=[('Cst', j1_)])
                            if c < NT - 1:
                                S.op('act', lambda e, j1_=j1_, c=c: e.activation(out=Cb_all[:, c + 1, 0:129], in_=Cst[j1_], func=AF.Copy),
                                     reads=[('Cst', j1_)], writes=[('Cb', c + 1)])
                        yield

                g1 = pass12(); g2 = fm_proj()
                alive = True
                while alive:
                    alive = False
                    for g_ in (g1, g2):
                        try:
                            next(g_); alive = True
                        except StopIteration:
                            pass
                sb_ = {}; pvb = {}
                ssn = hsm[:, 4, :]
                for i in range(NT + 4):
                    if i < NT:
                        c = i; i3 = c % 2
                        cs = slice(c * 128, (c + 1) * 128)
                        b1 = bank(); sb_[c] = b1
                        mm_group(Pf(b1)[:, 0:128], [(kT[:, cs], qT[:, cs])], [('kT', c // 4), ('qT', c // 4)], b1)
                        S.op('act', lambda e, i3=i3, c=c, h=h: e.activation(out=E[i3], in_=gbc[:, c, :], func=AF.Exp, scale=-1.0,
                                                                            bias=tokq[:, 0, c, h:h + 1]),
                             reads=['gbc', 'tokq'], writes=[('E', i3)])
                        S.op('pool', lambda e, i3=i3: e.affine_select(out=E[i3], in_=E[i3], pattern=[[1, 128]], compare_op=ALU.is_ge,
                                                                      fill=0.0, base=0, channel_multiplier=-1),
                             reads=[('E', i3)], writes=[('E', i3)])
                    if 0 <= i - 1 < NT:
                        c = i - 1; i3 = c % 2
                        b1 = sb_[c]
                        S.op('dve', lambda e, i3=i3, b1=b1: e.tensor_tensor(out=PT[i3], in0=E[i3], in1=Pf(b1)[:, 0:128], op=ALU.mult),
                             reads=[('E', i3), PSR(b1)], writes=[('PT', i3)])
                    if 0 <= i - 2 < NT:
                        c = i - 2; i3 = c % 2
                        cs = slice(c * 128, (c + 1) * 128)
                        b2 = bank(); pvb[c] = b2

                        def pv(e, b2=b2, i3=i3, c=c, cs=cs):
                            e.matmul(Pf(b2)[:, 0:129], PT[i3], v_aug[:, c, 0:129], start=True, stop=True)
                            return e.matmul(Pf(b2)[:, 256:385], qT[:, cs], Cb_all[:, c, 0:129], start=True, stop=True)
                        S.op('pe', pv, reads=[('PT', i3), ('v_aug', c), 'v_one', ('qT', c // 4), ('Cb', c)], writes=[PSR(b2)])
                    if 0 <= i - 3 < NT:
                        c = i - 3; i2 = c % 2
                        b2 = pvb[c]
                        S.op('act', lambda e, b2=b2, i2=i2, c=c, h=h: e.activation(out=tmpi[i2], in_=Pf(b2)[:, 256:385], func=AF.Copy,
                                                                                   scale=tokq[:, 2, c, h:h + 1]),
                             reads=[PSR(b2), 'tokq'], writes=[('tmpi', i2)])
                        S.op('dve', lambda e, b2=b2, i2=i2, c=c: e.tensor_tensor(out=numaug[:, c, :], in0=tmpi[i2], in1=Pf(b2)[:, 0:129],
                                                                                 op=ALU.add),
                             reads=[PSR(b2), ('tmpi', i2)], writes=[('numaug', c)])
                    if 0 <= i - 4 < NT:
                        c = i - 4
                        S.op('dve', lambda e, c=c: e.scalar_tensor_tensor(out=kw[0], in0=numaug[:, c, 0:128], scalar=1.0, in1=numaug[:, c, 0:128],
                                                                          op0=ALU.mult, op1=ALU.mult, accum_out=ssn[:, c:c + 1]),
                             reads=[('numaug', c)], writes=[('kw', 0), ('ssn', c)])
                jf = NT % 2
                S.op('sp', lambda e, h=h, jf=jf: e.dma_start(out=pC[h], in_=Cst[jf][:, 0:128]), reads=[('Cst', jf)], dma=True)
                S.op('sp', lambda e, h=h, jf=jf: e.dma_start(out=pn[h].rearrange("(p o) -> p o", o=1), in_=Cst[jf][:, 128:129]),
                     reads=[('Cst', jf)], dma=True)
                done('h%dchunk' % h)
                na_r = [('numaug', c) for c in range(NT)]
                dn = hsm[:, 0, :]; rr = hsm[:, 1, :]; ss2 = hsm[:, 2, :]; rs = hsm[:, 3, :]
                hmv = numaug[:, :, 0:128]
                S.op('dve', lambda e: e.scalar_tensor_tensor(out=dn, in0=numaug[:, :, 128], scalar=-1.0, in1=numaug[:, :, 128],
                                                             op0=ALU.mult, op1=ALU.max), reads=na_r, writes=['dn0'])
                S.op('dve', lambda e, h=h: e.tensor_tensor(out=dn, in0=dn, in1=tokq[:, 3, :, h], op=ALU.max),
                     reads=['dn0', 'tokq'], writes=['dn'])
                S.op('dve', lambda e: e.reciprocal(out=rr, in_=dn), reads=['dn'], writes=['rr'])
                S.op('dve', lambda e: e.tensor_tensor(out=ss2, in0=ssn, in1=rr, op=ALU.mult), reads=['rr'] + [('ssn', c) for c in range(NT)], writes=['ss2'])
                S.op('dve', lambda e: e.tensor_tensor(out=ss2, in0=ss2, in1=rr, op=ALU.mult), reads=['ss2', 'rr'], writes=['ss2a'])
                S.op('dve', lambda e: e.tensor_scalar(out=ss2, in0=ss2, scalar1=1.0 / 128, scalar2=EPS, op0=ALU.mult, op1=ALU.add),
                     reads=['ss2a'], writes=['ss2b'])
                S.op('pool', lambda e: e.tensor_tensor(out=rs, in0=ss2, in1=mhalf[:, 0:16], op=ALU.pow), reads=['ss2b', 'mhalf'], writes=['rs'])
                S.op('dve', lambda e: e.scalar_tensor_tensor(out=rs, in0=rs, scalar=0.5, in1=rr, op0=ALU.mult, op1=ALU.mult),
                     reads=['rs', 'rr'], writes=['rsb'])
                S.op('dve', lambda e: e.tensor_tensor(out=hmv, in0=hmv, in1=rs.unsqueeze(2).broadcast_to([128, 16, 128]), op=ALU.mult),
                     reads=na_r + ['rsb'], writes=['hmn'])
                S.op('dve', lambda e: e.scalar_tensor_tensor(out=hmix, in0=og_t, scalar=1.0, in1=hmv, op0=ALU.add, op1=ALU.mult),
                     reads=['hmn'] + [('og_t', t) for t in range(NT)], writes=[('og_t', t) for t in range(NT)])
                for half in range(2):
                    b = bank()

                    def trh(e, b=b, half=half):
                        for i in range(8):
                            inst = e.transpose(out=Pb(b)[:, i * 128:(i + 1) * 128], in_=hmix[:, half * 8 + i, :], identity=identb)
                        return inst
                    S.op('pe', trh, reads=['identb'] + [('og_t', half * 8 + i) for i in range(8)], writes=[PSR(b)])
                    if half == 0:
                        S.op('act', lambda e, b=b, h=h: e.activation(out=mixT[:, h, 0:1024], in_=Pb(b), func=AF.Copy,
                                                                     scale=mhcol[:, h:h + 1]),
                             reads=[PSR(b), 'mhcol'], writes=[('mixT', h, 0), ('mixT', h, 1)])
                    else:
                        S.op('dve', lambda e, b=b, h=h: e.tensor_scalar(out=mixT[:, h, 1024:2048], in0=Pb(b), scalar1=mhcol[:, h:h + 1],
                                                                        scalar2=None, op0=ALU.mult),
                             reads=[PSR(b), 'mhcol'], writes=[('mixT', h, 2), ('mixT', h, 3)])

                done('h%dpost' % h)
            dbg('mixT2', mixT, [128, 8, TC], [('mixT', k, g) for k in range(8) for g in range(5)])

            done('heads')
            S.barrier(old=['headp'], new=['wo'])
            S.region = 'wo'
            A.top = HT0
            o_wo = A.alloc(8 * D // 2); wo = A.bf16(o_wo, [8, D])
            sChb = [A.f32(A.alloc(2048), [16, 128]) for i in range(2)]
            wvbc = A.f32(o_wbuf, [16, 128])
            tmpC = [A.f32(A.alloc(128), [128]) for i in range(2)]
            snhb = [A.f32(A.alloc(128), [128]) for i in range(2)]
            dbcb = [A.f32(A.alloc(16), [16]) for i in range(2)]
            st0 = A.f32(A.alloc(128), [128]); st1 = A.f32(A.alloc(128), [128]); st2 = A.f32(A.alloc(128), [128])
            qCT = A.f32(A.alloc(16), [16]); ssm = A.f32(A.alloc(16), [16]); hms = A.bf16(A.alloc(64), [128])
            for half in range(2):
                S.op('pool', lambda e, half=half: e.dma_start(out=wo[:, :, half * 512:(half + 1) * 512],
                                                              in_=w_out.rearrange("(kc p) n -> p kc n", p=128)[:, :, half * 512:(half + 1) * 512]),
                     writes=[('wo', half)], dma=True)
            mix_all = lambda g: [('mixT', k, g) for k in range(8)]

            def load_state(h):
                hb = h % 2
                S.op('sp', lambda e, h=h, hb=hb: e.dma_start(out=sChb[hb], in_=sC[:, h, :, :].rearrange("b k v -> k b v")),
                     writes=[('sCh', hb)], dma=True)
                S.op('sp', lambda e, h=h, hb=hb: e.dma_start(out=snhb[hb][0:NS, :], in_=sn[:, h, :]), writes=[('snh', hb)], dma=True)
                S.op('sp', lambda e, h=h, hb=hb: e.dma_start(out=dbcb[hb], in_=bass.AP(tensor=dscr_t, offset=h * NS, ap=[[0, 128], [1, NS]])),
                     reads=['dscr'], writes=[('dbc', hb)], dma=True)

            def out_proj(t):
                r = rows(t)
                cs = slice(t * 128, t * 128 + r)
                for nb in range(2):
                    b = bank()
                    mm_group(Pf(b)[0:r, :], [(mixT[:, kc, cs], wo[:, kc, nb * 512:(nb + 1) * 512]) for kc in range(8)],
                             mix_all(t // 4 if t < 16 else 4) + [('wo', nb)], b)
                    S.op('dve', lambda e, b=b, t=t, r=r, nb=nb: e.tensor_tensor(out=x_sb[0:r, t, nb * 512:(nb + 1) * 512],
                                                                                in0=x_sb[0:r, t, nb * 512:(nb + 1) * 512], in1=Pf(b)[0:r, :], op=ALU.add),
                         reads=[PSR(b), ('x', t)], writes=[('x', t)])
                norm_sq(t, hT[:, 0, 0:1024], [('hT', 0, 0), ('hT', 0, 1)])
                if t == 16 or t % 4 == 3:
                    norm_rstd(4 if t == 16 else t // 4)

            def sample_head(h):
                hb = h % 2
                sCh = sChb[hb]; snh = snhb[hb]; dbc = dbcb[hb]
                zs = zsh_all[0:NS, h, :]
                qs = zs[:, 0:128]; ks = zs[:, 128:256]; vs = zs[:, 256:384]; ogs = zs[:, 384:512]
                qTs = qTs_all[:, h, :]; kTs = kTs_all[:, h, :]
                ZS = ('zsh', h); ZQ = ('zsh_q', h); SC = ('sCh', hb)
                t0 = st0[0:NS, :]; t1 = st1[0:NS, :]; t2 = st2[0:NS, :]
                sc = lambda i: ssm[0:NS, i:i + 1]
                S.op('dve', lambda e: e.tensor_scalar(out=qs, in0=qs, scalar1=QSCALE, scalar2=None, op0=ALU.mult),
                     reads=[ZS], writes=[ZQ])
                S.op('dve', lambda e: e.tensor_tensor(out=t0, in0=qs, in1=ks, op=ALU.mult), reads=[ZQ, ZS], writes=['st0'])
                S.op('dve', lambda e: e.tensor_reduce(out=sc(0), in_=t0, axis=AX.X, op=ALU.add), reads=['st0'], writes=['qk'])
                S.op('dve', lambda e: e.tensor_tensor(out=t0, in0=qs, in1=snh[0:NS, :], op=ALU.mult), reads=[ZQ, ('snh', hb), 'qk'], writes=['st0b'])
                S.op('dve', lambda e: e.tensor_reduce(out=sc(1), in_=t0, axis=AX.X, op=ALU.add), reads=['st0b'], writes=['qn'])
                S.op('dve', lambda e: e.tensor_tensor(out=sc(2), in0=sc(0), in1=s_dm[:, h:h + 1], op=ALU.mult),
                     reads=['qk', 's_dm'], writes=['scores'])
                S.op('dve', lambda e: e.scalar_tensor_tensor(out=sc(3), in0=sc(1), scalar=s_dec[:, h:h + 1], in1=sc(2),
                                                             op0=ALU.mult, op1=ALU.add), reads=['qn', 's_dec', 'scores'], writes=['den'])
                S.op('dve', lambda e: e.scalar_tensor_tensor(out=sc(4), in0=sc(3), scalar=-1.0, in1=sc(3), op0=ALU.mult, op1=ALU.max),
                     reads=['den'], writes=['denom0'])
                S.op('dve', lambda e: e.tensor_tensor(out=sc(4), in0=sc(4), in1=s_emt[:, h:h + 1], op=ALU.max),
                     reads=['denom0', 's_emt'], writes=['denom'])
                S.op('dve', lambda e: e.reciprocal(out=sc(5), in_=sc(4)), reads=['denom'], writes=['rden'])
                bq = bank()

                def qc(e, bq=bq):
                    for b_ in range(NS):
                        inst = e.matmul(Pf(bq)[:, b_:b_ + 1], sCh[:, b_, :], qTs[:, b_:b_ + 1], start=True, stop=True)
                    return inst
                S.op('pe', qc, reads=[SC, ('qTs', h)], writes=[PSR(bq)])
                S.op('act', lambda e, bq=bq: e.activation(out=qCT, in_=Pf(bq)[:, 0:16], func=AF.Copy), reads=[PSR(bq)], writes=['qCT'])
                bq2 = bank()
                S.op('pe', lambda e, bq2=bq2: e.transpose(out=Pf(bq2)[0:NS, 0:128], in_=qCT, identity=identf),
                     reads=['qCT', 'identf'], writes=[PSR(bq2)])
                S.op('dve', lambda e, bq2=bq2: e.tensor_scalar(out=t1, in0=Pf(bq2)[0:NS, 0:128], scalar1=s_dec[:, h:h + 1],
                                                               scalar2=None, op0=ALU.mult), reads=[PSR(bq2), 's_dec'], writes=['st1'])
                S.op('dve', lambda e: e.scalar_tensor_tensor(out=t1, in0=vs, scalar=sc(2), in1=t1, op0=ALU.mult, op1=ALU.add),
                     reads=[ZS, 'scores', 'st1'], writes=['num_s'])
                S.op('dve', lambda e: e.tensor_scalar(out=t1, in0=t1, scalar1=sc(5), scalar2=None, op0=ALU.mult),
                     reads=['num_s', 'rden'], writes=['hm_s'])
                S.op('dve', lambda e: e.tensor_tensor(out=t0, in0=t1, in1=t1, op=ALU.mult), reads=['hm_s', 'qn'], writes=['st0c'])
                S.op('dve', lambda e: e.tensor_reduce(out=sc(6), in_=t0, axis=AX.X, op=ALU.add), reads=['st0c'], writes=['ss_s'])
                S.op('dve', lambda e: e.tensor_scalar(out=sc(6), in0=sc(6), scalar1=1.0 / 128, scalar2=EPS, op0=ALU.mult, op1=ALU.add),
                     reads=['ss_s'], writes=['ss_sb'])
                S.op('pool', lambda e: e.tensor_tensor(out=sc(7), in0=sc(6), in1=mhalf[0:NS, 0:1], op=ALU.pow), reads=['ss_sb', 'mhalf'], writes=['rs_s'])
                S.op('dve', lambda e: e.tensor_scalar(out=sc(7), in0=sc(7), scalar1=0.5, scalar2=None, op0=ALU.mult), reads=['rs_s'], writes=['rs_sb'])
                S.op('act', lambda e: e.activation(out=t2, in_=ogs, func=AF.Tanh, scale=0.5), reads=[ZS], writes=['st2'])
                S.op('dve', lambda e: e.tensor_scalar(out=t1, in0=t1, scalar1=sc(7), scalar2=None, op0=ALU.mult), reads=['hm_s', 'rs_sb'], writes=['hmn_s'])
                S.op('dve', lambda e: e.scalar_tensor_tensor(out=hms[0:NS, :], in0=t2, scalar=1.0, in1=t1, op0=ALU.add, op1=ALU.mult),
                     reads=['st2', 'hmn_s'], writes=['hms'])
                bq3 = bank()
                S.op('pe', lambda e, bq3=bq3: e.transpose(out=Pb(bq3)[:, 0:16], in_=hms[0:NS, :], identity=identb[0:NS, 0:NS]),
                     reads=['hms', 'identb'], writes=[PSR(bq3)])
                S.op('dve', lambda e, bq3=bq3: e.tensor_scalar(out=mixT[:, h, SEQ:TC], in0=Pb(bq3)[:, 0:16], scalar1=mhcol[:, h:h + 1],
                                                               scalar2=None, op0=ALU.mult), reads=[PSR(bq3), 'mhcol'], writes=[('mixT', h, 4)])
                S.op('dve', lambda e: e.tensor_scalar(out=t0, in0=ks, scalar1=s_dm[:, h:h + 1], scalar2=None, op0=ALU.mult),
                     reads=[ZS, 's_dm', 'ss_s'], writes=['st0d'])
                S.op('dve', lambda e: e.scalar_tensor_tensor(out=t0, in0=snh[0:NS, :], scalar=s_dec[:, h:h + 1], in1=t0,
                                                             op0=ALU.mult, op1=ALU.add), reads=[('snh', hb), 's_dec', 'st0d'], writes=['nnew'])
                S.op('sp', lambda e: e.dma_start(out=sno[:, h, :], in_=t0), reads=['nnew'], writes=['st0'], dma=True)
                S.op('dve', lambda e: e.tensor_scalar(out=t2, in0=vs, scalar1=s_dm[:, h:h + 1], scalar2=None, op0=ALU.mult),
                     reads=[ZS, 's_dm', 'hms'], writes=['wv'])
                S.op('sp', lambda e: e.dma_start(out=wvscr[h], in_=t2), reads=['wv'], writes=[('wvscr', h), 'st2'], dma=True)
                S.op('sp', lambda e: e.dma_start(out=wvbc, in_=bass.AP(tensor=wvscr_t, offset=h * NS * 128,
                                                                        ap=[[0, 128], [128, NS], [1, 128]])),
                     reads=[('wvscr', h)], writes=['wvbc'] + [('wbuf', 0, jj) for jj in range(4)], dma=True)
                for b_ in range(NS):
                    i2 = b_ % 2
                    S.op('act', lambda e, b_=b_, i2=i2: e.activation(out=tmpC[i2], in_=sCh[:, b_, :], func=AF.Copy, scale=dbc[:, b_:b_ + 1]),
                         reads=[SC, ('dbc', hb), 'qCT'], writes=[('tmpC', i2)])
                    S.op('dve', lambda e, b_=b_, i2=i2: e.scalar_tensor_tensor(out=sCh[:, b_, :], in0=wvbc[:, b_, :], scalar=kTs[:, b_:b_ + 1],
                                                                               in1=tmpC[i2], op0=ALU.mult, op1=ALU.add),
                         reads=['wvbc', ('kTs', h), ('tmpC', i2)], writes=[('sChn', hb, b_)])
                S.op('sp', lambda e: e.dma_start(out=sCo[:, h, :, :].rearrange("b k v -> k b v"), in_=sCh),
                     reads=[('sChn', hb, b_) for b_ in range(NS)], writes=[SC], dma=True)

            load_state(0)
            load_state(1)
            for h in range(4):
                for t in range(4 * h, 4 * h + 4):
                    out_proj(t)
                sample_head(h)
                if h + 2 < 4:
                    load_state(h + 2)
            out_proj(16)
            dbg('x1', x_sb, [128, 17, D], [('x', t) for t in range(17)])
            done('outproj')

            A.top = PH
            wu = [None, None]; wd = [None, None]; hid = [None, None]
            wu[0] = A.bf16(A.alloc(8 * 512 // 2), [8, 512]); wd[0] = A.bf16(A.alloc(4 * D // 2), [4, D])
            hid[0] = A.bf16(A.alloc(4 * TC // 2), [4, TC])
            rtmp = [A.f32(A.alloc(512), [512]) for i in range(2)]
            o_nt = A.top
            wu[1] = A.bf16(A.alloc(8 * 512 // 2), [8, 512]); wd[1] = A.bf16(A.alloc(4 * D // 2), [4, D])
            hid[1] = A.bf16(A.alloc(4 * TC // 2), [4, TC])
            B_END = A.top
            o_wgp = A.alloc(8 * D // 2); wgp = A.bf16(o_wgp, [8, D])
            o_wpp = A.alloc(2 * D // 2); wpp = A.bf16(o_wpp, [2, D])
            o_pT = A.alloc(2 * TC // 2); pT = A.bf16(o_pT, [2, TC])
            o_gf = A.alloc(D); gfin = A.f32(o_gf, [D])
            pld = [A.f32(A.alloc(256), [256]) for i in range(1)]
            pbf = [A.bf16(A.alloc(128), [256]) for i in range(1)]
            NORM_NAMES = ['junk'] + [('xn', i_, k_) for i_ in range(2) for k_ in range(4)]
            S.barrier(old=None)
            S.region = 'B'
            S.op('sp', lambda e: e.nop(), writes=['phaseB_ok'])
            def prefetch_c():
                S.region = 'Cpre'
                S.op('pool', lambda e: e.dma_start(out=wgp, in_=w_pg.rearrange("(kc p) n -> p kc n", p=128)), writes=['wgp'], dma=True)
                S.op('pool', lambda e: e.dma_start(out=wpp, in_=w_pp.rearrange("(kc p) n -> p kc n", p=128)), writes=['wpp'], dma=True)
                S.op('sp', lambda e: e.dma_start(out=gfin, in_=nfin.partition_broadcast(128)), writes=['gfin'], dma=True)
                S.region = 'B'
                yield
                for t in range(17):
                    S.region = 'Cpre'
                    r = rows(t)
                    i2 = 0
                    src = pp[t * 128:(t + 1) * 128, :] if t < 16 else psm
                    S.op('sp', lambda e, i2=i2, r=r, src=src: e.dma_start(out=pld[i2][0:r, :], in_=src), writes=[('pld', i2)], dma=True)
                    S.op('act', lambda e, i2=i2, r=r: e.activation(out=pbf[i2][0:r, :], in_=pld[i2][0:r, :], func=AF.Copy),
                         reads=[('pld', i2)], writes=[('pbf', i2)])
                    b = bank()

                    def trp(e, b=b, i2=i2, r=r):
                        for kc in range(2):
                            inst = e.transpose(out=Pb(b)[:, kc * 128:kc * 128 + r], in_=pbf[i2][0:r, kc * 128:(kc + 1) * 128],
                                               identity=identb[0:r, 0:r])
                        return inst
                    S.op('pe', trp, reads=[('pbf', i2), 'identb'], writes=[PSR(b)])
                    S.op('dve', lambda e, b=b, t=t, r=r: e.tensor_copy(out=pT[:, :, t * 128:t * 128 + r],
                                                                       in_=Pb(b)[:, 0:256].rearrange("p (k t) -> p k t", k=2)[:, :, 0:r]),
                         reads=[PSR(b)], writes=[('pT', t)])
                    S.region = 'B'
                    yield

            wu_v = w_up.rearrange("(kc p) n -> p kc n", p=128)
            wd_v = w_down.rearrange("(fc p) n -> p fc n", p=128)
            NFB = 8
            for fb in range(NFB):
                i2 = fb % 2
                extra = NORM_NAMES if fb == 1 else []
                S.op('pool', lambda e, fb=fb, i2=i2: e.dma_start(out=wu[i2], in_=wu_v[:, :, fb * 512:(fb + 1) * 512]),
                     reads=['phaseB_ok'], writes=[('wu', i2)] + extra, dma=True)
                S.op('pool', lambda e, fb=fb, i2=i2: e.dma_start(out=wd[i2], in_=wd_v[:, fb * 4:(fb + 1) * 4, :]),
                     reads=['phaseB_ok'], writes=[('wd', i2)] + extra, dma=True)
                if fb == 0:
                    norm_to_hT(1, o_nt, presq=True)
                    S.region = 'B'
                if fb == 1:
                    pre_gen = prefetch_c()
                    next(pre_gen, None)
                ri = 0
                for fc in range(4):
                    for tb in range(5):
                        ncol = 512 if tb < 4 else NS
                        cs = slice(tb * 512, tb * 512 + ncol)
                        b = bank()
                        mm_group(Pf(b)[:, 0:ncol], [(wu[i2][:, kc, fc * 128:(fc + 1) * 128], hT[:, kc, cs]) for kc in range(8)],
                                 hT_reads(gs=[tb]) + [('wu', i2), 'phaseB_ok'], b)
                        if fb >= 1 and tb in (0, 2):
                            next(pre_gen, None)
                        rt = rtmp[ri % 2]; rk = ri % 2; ri += 1
                        S.op('act', lambda e, b=b, rt=rt, ncol=ncol: e.activation(out=rt[:, 0:ncol], in_=Pf(b)[:, 0:ncol], func=AF.Relu),
                             reads=[PSR(b)], writes=[('rtmp', rk)])
                        S.op('dve', lambda e, b=b, rt=rt, ncol=ncol, i2=i2, fc=fc, cs=cs: e.tensor_tensor(
                            out=hid[i2][:, fc, cs], in0=rt[:, 0:ncol], in1=Pf(b)[:, 0:ncol], op=ALU.mult),
                            reads=[PSR(b), ('rtmp', rk)], writes=[('hid', i2, fc, tb)] + (NORM_NAMES if (fb == 1 and fc == 0 and tb == 0) else []))
                for t in range(17):
                    r = rows(t)
                    cs = slice(t * 128, t * 128 + r)
                    tbk = t // 4 if t < 16 else 4
                    for nb in range(2):
                        b = bank()
                        mm_group(Pf(b)[0:r, :], [(hid[i2][:, fc, cs], wd[i2][:, fc, nb * 512:(nb + 1) * 512]) for fc in range(4)],
                                 [('hid', i2, fc, tbk) for fc in range(4)] + [('wd', i2)], b)
                        S.op('dve', lambda e, b=b, t=t, r=r, nb=nb: e.tensor_tensor(out=x_sb[0:r, t, nb * 512:(nb + 1) * 512],
                                                                                    in0=x_sb[0:r, t, nb * 512:(nb + 1) * 512], in1=Pf(b)[0:r, :], op=ALU.add),
                             reads=[PSR(b), ('x', t)], writes=[('x', t)])
                    if fb == NFB - 1:
                        S.region = 'norm'
                        norm_sq(t, hT[:, 0, 0:1024], [('hT', 0, 0), ('hT', 0, 1)])
                        if t == 16 or t % 4 == 3:
                            norm_rstd(4 if t == 16 else t // 4)
                        S.region = 'B'
            for _ in pre_gen:
                pass
            dbg('x2', x_sb, [128, 17, D], [('x', t) for t in range(17)])
            done('mlp')

            A.top = PH
            o_nt = A.alloc(512 + 4096)
            tht = [A.f32(A.alloc(D), [D]) for i in range(2)]
            ut = [A.f32(A.alloc(D), [D]) for i in range(2)]
            ysb = [A.f32(A.alloc(D), [D]) for i in range(2)]
            sqj = [A.bf16(A.alloc(D // 2), [D]) for i in range(2)]
            o_fs = A.alloc(40); fss = A.f32(o_fs, [2, 20])
            assert A.top <= B_END
            S.barrier(old=['B'], new=['norm'])
            S.region = 'C'
            S.op('pool', lambda e: e.memset(fss, 1.0), writes=['fss0'])
            norm_to_hT(2, o_nt, presq=True)
            S.region = 'C'
            def c_main(t):
                r = rows(t)
                cs = slice(t * 128, t * 128 + r)
                tbk = t // 4 if t < 16 else 4
                i2 = t % 2
                for nb in range(2):
                    bg_ = bank(); bp_ = bank()
                    ns = slice(nb * 512, (nb + 1) * 512)
                    mm_group(Pf(bg_)[0:r, :], [(hT[:, kc, cs], wgp[:, kc, ns]) for kc in range(8)], hT_reads(gs=[tbk]) + ['wgp'], bg_)
                    mm_group(Pf(bp_)[0:r, :], [(pT[:, kc, cs], wpp[:, kc, ns]) for kc in range(2)], [('pT', t), 'wpp'], bp_)
                    S.op('act', lambda e, b=bg_, i2=i2, r=r, ns=ns: e.activation(out=tht[i2][0:r, ns], in_=Pf(b)[0:r, :], func=AF.Tanh, scale=0.5),
                         reads=[PSR(bg_)], writes=[('tht', i2, nb)])
                    S.op('dve', lambda e, b=bp_, i2=i2, r=r, ns=ns: e.scalar_tensor_tensor(out=ut[i2][0:r, ns], in0=tht[i2][0:r, ns], scalar=1.0,
                                                                                          in1=Pf(b)[0:r, :], op0=ALU.add, op1=ALU.mult),
                         reads=[PSR(bp_), ('tht', i2, nb)], writes=[('ut', i2, nb)])
                    S.op('dve', lambda e, i2=i2, r=r, ns=ns, t=t: e.scalar_tensor_tensor(out=x_sb[0:r, t, ns], in0=ut[i2][0:r, ns], scalar=0.5,
                                                                                        in1=x_sb[0:r, t, ns], op0=ALU.mult, op1=ALU.add),
                         reads=[('ut', i2, nb), ('x', t)], writes=[('x', t)])

            def c_fin1(t):
                r = rows(t)
                i2 = t % 2
                S.op('act', lambda e, i2=i2, r=r, t=t: e.activation(out=sqj[i2][0:r, :], in_=x_sb[0:r, t, :], func=AF.Square,
                                                                   accum_out=fss[0:r, 0, t:t + 1]),
                     reads=[('x', t), 'fss0'], writes=[('sqj', i2), ('fss', t)])
                S.op('dve', lambda e, r=r, t=t: e.tensor_scalar(out=fss[0:r, 1, t:t + 1], in0=fss[0:r, 0, t:t + 1], scalar1=1.0 / D, scalar2=EPS,
                                                                op0=ALU.mult, op1=ALU.add), reads=[('fss', t)], writes=[('fss1', t)])
                S.op('pool', lambda e, r=r, t=t: e.tensor_tensor(out=fss[0:r, 0, t:t + 1], in0=fss[0:r, 1, t:t + 1], in1=mhalf[0:r, 0:1], op=ALU.pow),
                     reads=[('fss1', t), 'mhalf'], writes=[('frs', t)])

            def c_fin2(t):
                r = rows(t)
                i2 = t % 2
                S.op('dve', lambda e, i2=i2, r=r, t=t: e.scalar_tensor_tensor(out=ysb[i2][0:r, :], in0=x_sb[0:r, t, :], scalar=fss[0:r, 0, t:t + 1],
                                                                             in1=gfin[0:r, :], op0=ALU.mult, op1=ALU.mult),
                     reads=[('x', t), ('frs', t), 'gfin'], writes=[('ysb', i2)])
                dst = yp[t * 128:(t + 1) * 128, :] if t < 16 else ys
                S.op('sp', lambda e, i2=i2, r=r, dst=dst: e.dma_start(out=dst, in_=ysb[i2][0:r, :]), reads=[('ysb', i2)], dma=True)

            for i in range(17 + 2):
                if i < 17:
                    c_main(i)
                if 0 <= i - 1 < 17:
                    c_fin1(i - 1)
                if 0 <= i - 2 < 17:
                    c_fin2(i - 2)

        except _Stop:
            pass
        S.emit(st)
    return nc, dbg_outs


_CACHE = {}


def _prep(inputs, c):
    f = lambda a: np.ascontiguousarray(np.asarray(a, dtype=np.float32))
    sl = slice(c * NS, (c + 1) * NS)
    return {
        "xp": f(inputs["x_prompt"][c]), "xs": f(inputs["x_sample"][sl, 0]),
        "pp": f(inputs["p_prompt"][0, c]), "psm": f(inputs["p_sample"][0, sl, 0]),
        "sC": f(inputs["state_mlstm_C"][0, sl]), "sn": f(inputs["state_mlstm_n"][0, sl]),
        "sm": f(inputs["state_mlstm_m"][0, sl]), "sconv": f(inputs["state_conv"][0, sl]),
        "w_in": f(inputs["w_in"][0]), "w_out": f(inputs["w_out"][0]), "w_up": f(inputs["w_up"][0]),
        "w_down": f(inputs["w_down"][0]), "w_pg": f(inputs["w_ple_gate"][0]), "w_pp": f(inputs["w_ple_proj"][0]),
        "nmix": f(inputs["norm_mix"][0]), "nmlp": f(inputs["norm_mlp"][0]), "nple": f(inputs["norm_ple"][0]),
        "nfin": f(inputs["norm_final"]), "bgi": f(inputs["b_gate_i"][0]), "bgf": f(inputs["b_gate_f"][0]),
        "mhn": f(inputs["mh_norm"][0]), "cw": f(inputs["conv_w"][0]),
    }


def kernel(**inputs):
    if 'nc' not in _CACHE:
        _CACHE['nc'] = build()[0]
    nc = _CACHE['nc']
    in_maps = [_prep(inputs, c) for c in range(8)]
    res = run_bass_kernel_spmd(nc, in_maps, core_ids=list(range(8)))
    R = res.results
    g = lambda k: [np.asarray(R[c][k], dtype=np.float32) for c in range(8)]
    y_prompt = np.stack(g("yp"), 0)
    y_sample = np.concatenate(g("ys"), 0)[:, None, :]
    pC_ = np.stack(g("pC"), 0)[None]
    pn_ = np.stack(g("pn"), 0)[None]
    pm_ = np.stack(g("pm"), 0)[None]
    pconv_ = np.stack(g("pconv"), 0)[None]
    sC_ = np.concatenate(g("sCo"), 0)[None]
    sn_ = np.concatenate(g("sno"), 0)[None]
    sm_ = np.concatenate(g("smo"), 0)[None]
    sconv_ = np.concatenate(g("sconvo"), 0)[None]
    return (y_prompt, y_sample, pC_, pn_, pm_, pconv_, sC_, sn_, sm_, sconv_)
```

```python
import numpy as np
import concourse.bass as bass
import concourse.mybir as mybir
from concourse.bass_utils import run_bass_kernel_spmd
from contextlib import ExitStack

F32 = mybir.dt.float32
BF16 = mybir.dt.bfloat16
ALU = mybir.AluOpType
AF = mybir.ActivationFunctionType
AX = mybir.AxisListType

D = 1024
SEQ = 2048
NT = 16
NS = 16
TC = SEQ + NS
NIN = 3592
DFF = 4096
EPS = 1e-6
M_INIT = -1e30
QSCALE = 128.0 ** -0.5

ENGS = ['pe', 'act', 'dve', 'pool', 'sp']
PERSIST = {'x', 'hT', 'ss', 'rs1', 'rstd', 'ones', 'identf', 'identb', 'mhalf', 'gcol', 'mhcol', 'cwcol', 'gbb',
           'ps0', 'ps1', 'ps2', 'ps3', 'ps4', 'ps5', 'ps6', 'ps7', 'mixT', 'wbuf', 'wg', 'tokq', 'cdec_bc', 'cols',
           'zsh', 'zsh_q', 'qTs', 'kTs', 'gscr', 'dscr', 'wvscr', 's_dm', 's_dec', 's_emt', 's_mt', 'gsm', 'cuTs',
           'bufT', 'gtm', 'R_ig', 'R_t1', 'R_sp', 'R_bn', 'R_a', 'R_A', 'R_g', 'R_m', 'R_emt', 'R_dec', 'R_w',
           'rowA', 'mnew', 'mprev', 'mprev0', 'cdec_a', 'cdec_b', 'cdecr', 'wlr', 'sm_sb', 's_t0', 's_t1', 's_mi',
           's_t0b', 's_t1b'}
NDMASEM = 28
DMA_POOL = {'sp': list(range(0, 16)), 'pool': list(range(16, 24)), 'act': list(range(24, 28))}


class Sched:
    def __init__(self, nc):
        self.nc = nc
        self.ops = {e: [] for e in ENGS}
        self.last_write = {}
        self.readers = {}
        self.dma_n = {e: 0 for e in DMA_POOL}
        self.dma_cnt = [0] * NDMASEM
        self.region = None
        self.regnames = {}
        self.cur_barrier = None
        self.deferred = None
        self._replaying = False

    @staticmethod
    def _base(r):
        return r[0] if isinstance(r, tuple) else r

    def barrier(self, old=None, new=()):
        names = set()
        if old is None:
            names |= set(self.last_write) | set(self.readers)
        else:
            for rg in old:
                names |= self.regnames.get(rg, set())
        wr = set(names)
        for rg in new:
            wr |= self.regnames.get(rg, set())
        saved = self.region
        self.region = None
        b = self.op('sp', lambda e: e.nop(), reads=sorted(names, key=str), writes=sorted(wr, key=str))
        self.region = saved
        self.cur_barrier = b

    def defer_begin(self):
        self.deferred = []

    def mark(self):
        if self.deferred is not None and not self._replaying:
            self.deferred.append(None)

    def defer_end(self):
        self._stages = self.deferred
        self.deferred = None

    def replay_stage(self):
        st_ = getattr(self, '_stages', None)
        if not st_:
            return
        self._replaying = True
        while st_:
            item = st_.pop(0)
            if item is None:
                break
            self.op(*item)
        self._replaying = False

    def replay_all(self):
        while getattr(self, '_stages', None):
            self.replay_stage()

    def op(self, eng, fn, reads=(), writes=(), dma=False):
        if self.deferred is not None and not self._replaying:
            self.deferred.append((eng, fn, list(reads), list(writes), dma))
            return None
        idx = len(self.ops[eng])
        deps = set()
        for r in list(reads) + list(writes):
            if self.region is not None and self._base(r) not in PERSIST:
                self.regnames.setdefault(self.region, set()).add(r)
            if self.cur_barrier is not None and r not in self.last_write and r not in self.readers:
                deps.add(self.cur_barrier)
        ps_reads = [r for r in reads if isinstance(r, str) and r.startswith('ps') and r[2:].isdigit()]
        reads = [r for r in reads if r not in ps_reads]
        writes = list(writes) + ps_reads
        for r in reads:
            w = self.last_write.get(r)
            if w is not None:
                deps.add(w)
        for r in writes:
            w = self.last_write.get(r)
            if w is not None:
                deps.add(w)
            for rd in self.readers.get(r, ()):
                deps.add(rd)
        deps.discard((eng, idx))
        self._g = getattr(self, '_g', 0) + 1
        rec = dict(fn=fn, deps=deps, dma=dma, g=self._g)
        if dma:
            pool = DMA_POOL[eng]
            k = pool[self.dma_n[eng] % len(pool)]
            self.dma_n[eng] += 1
            self.dma_cnt[k] += 1
            rec['dsem'] = k
            rec['dval'] = 16 * self.dma_cnt[k]
        self.ops[eng].append(rec)
        for r in reads:
            self.readers.setdefault(r, []).append((eng, idx))
        for r in writes:
            self.last_write[r] = (eng, idx)
            self.readers[r] = []
        return (eng, idx)

    def emit(self, stack, final_wait_eng='sp'):
        nc = self.nc
        sem = {e: stack.enter_context(nc.semaphore('s_' + e)) for e in ENGS}
        dsem = [stack.enter_context(nc.semaphore('d_%d' % i)) for i in range(NDMASEM)]
        needed = set()
        for e in ENGS:
            for i, rec in enumerate(self.ops[e]):
                for d in rec['deps']:
                    de, di = d
                    if de == 'pe' and e == 'pe':
                        continue
                    if not self.ops[de][di]['dma']:
                        needed.add(d)
        rank = {}
        for e in ENGS:
            c = 0
            for i, rec in enumerate(self.ops[e]):
                if (e, i) in needed:
                    c += 1
                    rank[(e, i)] = c
        upto = {}
        for e in ENGS:
            c = 0
            for i, rec in enumerate(self.ops[e]):
                if (e, i) in rank:
                    c = rank[(e, i)]
                upto[(e, i)] = c
        K = {}
        prevK = {e: {} for e in ENGS}
        order = sorted((rec['g'], e, i) for e in ENGS for i, rec in enumerate(self.ops[e]))
        for _, e, i in order:
            rec = self.ops[e][i]
            k = dict(prevK[e])
            for d in rec['deps']:
                de, di = d
                if de == 'pe' and e == 'pe':
                    continue
                for en, r in K[d].items():
                    if k.get(en, 0) < r:
                        k[en] = r
            prevK[e] = k
            kk = dict(k)
            if not rec['dma']:
                if kk.get(e, 0) < upto[(e, i)]:
                    kk[e] = upto[(e, i)]
            K[(e, i)] = kk
        self._K = K
        block = stack.enter_context(nc.Block())
        handles = {'pe': block.tensor, 'act': block.scalar, 'dve': block.vector,
                   'pool': block.gpsimd, 'sp': block.sync}
        sched = self

        def make_body(e):
            def body(eng):
                waited = {}
                known = {}

                def learn(d):
                    for en, r in sched._K[d].items():
                        if known.get(en, 0) < r:
                            known[en] = r

                def wait(s, key, val):
                    if waited.get(key, 0) >= val:
                        return
                    eng.wait_ge(s, val)
                    waited[key] = val

                for i, rec in enumerate(sched.ops[e]):
                    for d in sorted(rec['deps']):
                        de, di = d
                        if de == 'pe' and e == 'pe':
                            continue
                        prod = sched.ops[de][di]
                        if prod['dma']:
                            wait(dsem[prod['dsem']], ('d', prod['dsem']), prod['dval'])
                        elif known.get(de, 0) < rank[d]:
                            wait(sem[de], ('c', de), rank[d])
                        learn(d)
                    if rec['dma']:
                        k = rec['dsem']
                        if rec['dval'] > 16:
                            wait(dsem[k], ('d', k), rec['dval'] - 16)
                        inst = rec['fn'](eng)
                        inst.then_inc(dsem[k], 16)
                    else:
                        inst = rec['fn'](eng)
                        if (e, i) in needed:
                            inst.then_inc(sem[e], 1)
                if e == final_wait_eng:
                    for k in range(NDMASEM):
                        if sched.dma_cnt[k] > 0:
                            wait(dsem[k], ('d', k), 16 * sched.dma_cnt[k])
            return body

        for e in ENGS:
            if self.ops[e] or e == final_wait_eng:
                handles[e](make_body(e))


def _view(ap2d, shape):
    if len(shape) == 1:
        return ap2d
    names = ['a%d' % i for i in range(len(shape))]
    pat = "p (" + " ".join(names) + ") -> p " + " ".join(names)
    kw = {n: s for n, s in zip(names[1:], shape[1:])}
    return ap2d.rearrange(pat, **kw)


class Arena:
    def __init__(self, t, nwords):
        self.t = t
        self.n = nwords
        self.top = 0

    def alloc(self, words):
        off = self.top
        self.top += int(words)
        assert self.top <= self.n, ("arena overflow", self.top, self.n)
        return off

    def f32(self, off, shape, parts=128):
        w = int(np.prod(shape))
        return _view(self.t[0:parts, off:off + w], shape)

    def bf16(self, off, shape, parts=128):
        n = int(np.prod(shape))
        w = (n + 1) // 2
        ap = self.t[0:parts, off:off + w].bitcast(BF16)
        if n != 2 * w:
            ap = ap[:, 0:n]
        return _view(ap, shape)


class _Stop(Exception):
    pass


def build(debug=None, stop=None):
    nc = bass.Bass("TRN2", target_bir_lowering=False)

    def din(name, shape):
        return nc.dram_tensor(name, list(shape), F32, kind="ExternalInput").ap()

    def dout(name, shape):
        return nc.dram_tensor(name, list(shape), F32, kind="ExternalOutput").ap()

    xp = din("xp", [SEQ, D]); xs = din("xs", [NS, D])
    pp = din("pp", [SEQ, 256]); psm = din("psm", [NS, 256])
    sC = din("sC", [NS, 4, 128, 128]); sn = din("sn", [NS, 4, 128]); sm = din("sm", [NS, 4])
    sconv = din("sconv", [NS, 2, 512])
    w_in = din("w_in", [D, NIN]); w_out = din("w_out", [D, D]); w_up = din("w_up", [D, DFF])
    w_down = din("w_down", [DFF, D]); w_pg = din("w_pg", [D, D]); w_pp = din("w_pp", [256, D])
    nmix = din("nmix", [D]); nmlp = din("nmlp", [D]); nple = din("nple", [D]); nfin = din("nfin", [D])
    bgi = din("bgi", [4]); bgf = din("bgf", [4]); mhn = din("mhn", [512]); cw = din("cw", [3, 512])

    yp = dout("yp", [SEQ, D]); ys = dout("ys", [NS, D])
    pC = dout("pC", [4, 128, 128]); pn = dout("pn", [4, 128]); pm = dout("pm", [4]); pconv = dout("pconv", [2, 512])
    sCo = dout("sCo", [NS, 4, 128, 128]); sno = dout("sno", [NS, 4, 128]); smo = dout("smo", [NS, 4])
    sconvo = dout("sconvo", [NS, 2, 512])

    gscr_t = nc.dram_tensor("gscr", [64, 128], F32, kind="Internal")
    gscr = gscr_t.ap()
    wvscr_t = nc.dram_tensor("wvscr", [4, NS, 128], F32, kind="Internal")
    wvscr = wvscr_t.ap()
    dscr_t = nc.dram_tensor("dscr", [4, NS], F32, kind="Internal")
    dscr = dscr_t.ap()

    dbg_outs = {}

    with ExitStack() as st:
        NW = 53000
        arena_t = st.enter_context(nc.sbuf_tensor("arena", [128, NW], F32))
        P = st.enter_context(nc.psum_tensor("psum", [128, 8, 512], F32))
        A = Arena(arena_t, NW)
        S = Sched(nc)

        psi = [0]

        bank_mode = {'m': None}
        psj = [0]

        def bank():
            if bank_mode['m'] == 'aux':
                b = 6 + psj[0] % 2
                psj[0] += 1
                return b
            if bank_mode['m'] == 'main6':
                b = psi[0] % 6
                psi[0] += 1
                return b
            b = psi[0] % 8
            psi[0] += 1
            return b

        def Pf(b):
            return P[:, b, :]

        def Pb(b):
            return P[:, b, :].bitcast(BF16)

        def PSR(b):
            return 'ps%d' % b

        def dbg(name, ap, shape, reads):
            if debug is None or name not in debug:
                return
            o = dout("dbg_" + name, shape)
            dbg_outs[name] = shape
            S.op('pool', lambda e: e.dma_start(out=o, in_=ap), reads=reads, dma=True)

        def done(tag):
            if stop == tag:
                raise _Stop()

        try:
            o_x = A.alloc(17 * D)
            x_sb = A.f32(o_x, [17, D])
            o_ht = A.alloc(8 * TC // 2)
            hT = A.bf16(o_ht, [8, TC])
            o_identf = A.alloc(128); identf = A.f32(o_identf, [128])
            o_identb = A.alloc(64); identb = A.bf16(o_identb, [128])
            o_ones = A.alloc(128); ones_f = A.f32(o_ones, [128])
            o_gcol = A.alloc(32); gcol = A.f32(o_gcol, [4, 8])
            o_mhc = A.alloc(4); mhcol = A.f32(o_mhc, [4])
            o_cwc = A.alloc(12); cwcol = A.f32(o_cwc, [4, 3])
            o_gbb = A.alloc(128); gb_bc = A.f32(o_gbb, [2, 16, 4])
            o_mh = A.alloc(32); mhalf = A.f32(o_mh, [32])
            o_ss = A.alloc(20); ss = A.f32(o_ss, [20])
            o_rs1 = A.alloc(20); rs1 = A.f32(o_rs1, [20])
            o_rstd = A.alloc(20); rstd = A.f32(o_rstd, [20])
            PH = A.top

            for t in range(NT):
                S.op('sp', lambda e, t=t: e.dma_start(out=x_sb[:, t, :], in_=xp[t * 128:(t + 1) * 128, :]),
                     writes=[('x', t)], dma=True)
            S.op('sp', lambda e: e.dma_start(out=x_sb[0:NS, 16, :], in_=xs), writes=[('x', 16)], dma=True)

            S.op('pool', lambda e: e.memset(ones_f, 1.0), writes=['ones'])
            S.op('pool', lambda e: e.affine_select(out=identf, in_=ones_f, pattern=[[1, 128]], compare_op=ALU.is_equal,
                                                   fill=0.0, base=0, channel_multiplier=-1), reads=['ones'], writes=['identf'])
            S.op('pool', lambda e: e.tensor_copy(out=identb, in_=identf), reads=['identf'], writes=['identb'])
            S.op('pool', lambda e: e.memset(mhalf, -0.5), writes=['mhalf'])
            S.op('pool', lambda e: e.memset(ss, 1.0), writes=[('ss', t) for t in range(17)])
            for j, nv in enumerate([nmix, nmlp, nple]):
                S.op('sp', lambda e, j=j, nv=nv: e.dma_start(out=gcol[:, j, :], in_=nv.rearrange("(kc p) -> p kc", p=128),
                                                             allow_slow_non_contiguous=True), writes=[('gcol', j)], dma=True)
            S.op('sp', lambda e: e.dma_start(out=mhcol, in_=mhn.rearrange("(h p) -> p h", p=128),
                                             allow_slow_non_contiguous=True), writes=['mhcol'], dma=True)
            for jw in range(3):
                S.op('sp', lambda e, jw=jw: e.dma_start(out=cwcol[:, :, jw], in_=cw[jw].rearrange("(c p) -> p c", p=128),
                                                        allow_slow_non_contiguous=True), writes=['cwcol'], dma=True)
            for g, bv in enumerate([bgi, bgf]):
                S.op('sp', lambda e, g=g, bv=bv: e.dma_start(
                    out=gb_bc[:, g, :, :], in_=bass.AP(tensor=bv.tensor, offset=0, ap=[[0, 128], [0, 16], [1, 4]])),
                    writes=[('gbb', g)], dma=True)

            def rows(t):
                return 128 if t < NT else NS

            def norm_sq(t, junk_ap, junk_names):
                r = rows(t)
                S.op('act', lambda e, t=t, r=r: e.activation(out=junk_ap[0:r, :], in_=x_sb[0:r, t, :], func=AF.Square,
                                                             accum_out=ss[0:r, t:t + 1]),
                     reads=[('x', t)], writes=list(junk_names) + [('ss', t)])

            def norm_rstd(g):
                tiles = list(range(4 * g, 4 * g + 4)) if g < 4 else [16]
                c0, c1 = tiles[0], tiles[-1] + 1
                S.op('dve', lambda e: e.tensor_scalar(out=rs1[:, c0:c1], in0=ss[:, c0:c1], scalar1=1.0 / D, scalar2=EPS,
                                                      op0=ALU.mult, op1=ALU.add),
                     reads=[('ss', t) for t in tiles], writes=[('rs1', g)])
                S.op('pool', lambda e: e.tensor_tensor(out=rstd[:, c0:c1], in0=rs1[:, c0:c1], in1=mhalf[:, c0:c1], op=ALU.pow),
                     reads=[('rs1', g), 'mhalf'], writes=[('rstd', g)])

            def norm_to_hT(j, o_tmp, presq=False):
                S.region = 'norm'
                junk = A.bf16(o_tmp, [D])
                xn = [A.bf16(o_tmp + 512 + i * 2048, [4, D]) for i in range(2)]
                groups = [list(range(4 * g, 4 * g + 4)) for g in range(4)] + [[16]]

                def stage_sq(g):
                    tiles = groups[g]
                    if presq:
                        return
                    for t in tiles:
                        norm_sq(t, junk, ['junk'])
                    norm_rstd(g)

                def _unused(g):
                    tiles = groups[g]
                    c0, c1 = tiles[0], tiles[-1] + 1
                    S.op('dve', lambda e: e.tensor_scalar(out=rs1[:, c0:c1], in0=ss[:, c0:c1], scalar1=1.0 / D, scalar2=EPS,
                                                          op0=ALU.mult, op1=ALU.add),
                         reads=[('ss', t) for t in tiles], writes=[('rs1', g)])
                    S.op('pool', lambda e: e.tensor_tensor(out=rstd[:, c0:c1], in0=rs1[:, c0:c1], in1=mhalf[:, c0:c1], op=ALU.pow),
                         reads=[('rs1', g), 'mhalf'], writes=[('rstd', g)])

                def stage_main(g):
                    buf = xn[g % 2]
                    tiles = groups[g]
                    for i, t in enumerate(tiles):
                        r = rows(t)
                        if (t % 2) == 0 and (presq or t % 4 == 0):
                            S.op('act', lambda e, t=t, r=r, i=i, buf=buf: e.activation(
                                out=buf[0:r, i, :], in_=x_sb[0:r, t, :], func=AF.Copy, scale=rstd[0:r, t:t + 1]),
                                reads=[('x', t), ('rstd', g)], writes=[('xn', g % 2, i)])
                        else:
                            S.op('dve', lambda e, t=t, r=r, i=i, buf=buf: e.tensor_scalar(
                                out=buf[0:r, i, :], in0=x_sb[0:r, t, :], scalar1=rstd[0:r, t:t + 1], scalar2=None,
                                op0=ALU.mult), reads=[('x', t), ('rstd', g)], writes=[('xn', g % 2, i)])
                    if g < 4:
                        for kc in range(8):
                            b = bank()

                            def tr(e, b=b, kc=kc, buf=buf):
                                for i in range(4):
                                    inst = e.transpose(out=Pb(b)[:, i * 128:(i + 1) * 128],
                                                       in_=buf[:, i, kc * 128:(kc + 1) * 128], identity=identb)
                                return inst
                            S.op('pe', tr, reads=[('xn', g % 2, i) for i in range(4)] + ['identb'], writes=[PSR(b)])
                            dst = hT[:, kc, g * 512:(g + 1) * 512]
                            if kc % 2 == 0:
                                S.op('dve', lambda e, b=b, kc=kc, dst=dst: e.tensor_scalar(
                                    out=dst, in0=Pb(b)[:, 0:512], scalar1=gcol[:, j, kc:kc + 1], scalar2=None, op0=ALU.mult),
                                    reads=[PSR(b), ('gcol', j)], writes=[('hT', kc, g)])
                            else:
                                S.op('act', lambda e, b=b, kc=kc, dst=dst: e.activation(
                                    out=dst, in_=Pb(b)[:, 0:512], func=AF.Copy, scale=gcol[:, j, kc:kc + 1]),
                                    reads=[PSR(b), ('gcol', j)], writes=[('hT', kc, g)])
                    else:
                        b = bank()

                        def trs(e, b=b, buf=buf):
                            for kc in range(8):
                                inst = e.transpose(out=Pb(b)[:, kc * 16:(kc + 1) * 16],
                                                   in_=buf[0:NS, 0, kc * 128:(kc + 1) * 128], identity=identb[0:NS, 0:NS])
                            return inst
                        S.op('pe', trs, reads=[('xn', g % 2, 0), 'identb'], writes=[PSR(b)])
                        for kc in range(8):
                            S.op('dve', lambda e, b=b, kc=kc: e.tensor_scalar(
                                out=hT[:, kc, SEQ:TC], in0=Pb(b)[:, kc * 16:(kc + 1) * 16], scalar1=gcol[:, j, kc:kc + 1],
                                scalar2=None, op0=ALU.mult), reads=[PSR(b), ('gcol', j)], writes=[('hT', kc, 4)])

                stage_sq(0)
                for g in range(5):
                    if g + 1 < 5:
                        stage_sq(g + 1)
                    stage_main(g)

            def hT_reads(kcs=range(8), gs=range(5)):
                return [('hT', kc, g) for kc in kcs for g in gs]

            def mm_group(out_ap, pairs, reads, b):
                def fn(e):
                    n = len(pairs)
                    for i, (l, r) in enumerate(pairs):
                        inst = e.matmul(out_ap, l, r, start=(i == 0), stop=(i == n - 1))
                    return inst
                S.op('pe', fn, reads=reads, writes=[PSR(b)])

            win_v = w_in.rearrange("(kc p) n -> p kc n", p=128)

            A.top = PH
            o_mixT = A.alloc(8 * TC // 2); mixT = A.bf16(o_mixT, [8, TC])
            o_wbuf = A.alloc(2048); wbufs = [A.bf16(o_wbuf, [8, 4, 128]), A.bf16(NW - 2048, [8, 4, 128])]
            o_wg = A.alloc(32); wg = A.bf16(o_wg, [8, 8])
            o_gtm = A.alloc(128); gtm = A.f32(o_gtm, [128])
            o_gs = A.alloc(8); gsm = A.f32(o_gs, [8])
            o_tokq = A.alloc(256); tokq = A.f32(o_tokq, [4, 16, 4])
            o_cdbc = A.alloc(64); cdec_bc = A.f32(o_cdbc, [16, 4])
            o_sg = A.alloc(64); sg = A.f32(o_sg, [16, 4])
            o_cuTs = A.alloc(64); cuTs = A.f32(o_cuTs, [4, 16])
            o_bufT = A.alloc(128); bufT = A.f32(o_bufT, [8, 16])
            o_R0 = A.top
            R = {}
            for nm in ['ig', 't1', 'sp', 'bn', 'a', 'A', 'g', 'm', 'emt', 'dec', 'w']:
                R[nm] = A.f32(A.alloc(128), [128])
            o_rows = A.alloc(64 * 6); rowsb = A.f32(o_rows, [6, 64])
            o_cols = A.alloc(4); colsb = A.f32(o_cols, [4])
            A.top = max(A.top, o_R0 + 2048 + 128)
            zsh_all = A.f32(o_R0, [4, 512]); qTs_all = A.f32(o_R0 + 2048, [4, 16]); kTs_all = A.f32(o_R0 + 2112, [4, 16])
            GATE_NAMES = ['R_ig', 'R_t1', 'R_sp', 'R_bn', 'R_a', 'R_A', 'R_g', 'R_m', 'R_emt', 'R_dec', 'R_w', 'rowA', 'mprev',
                          'mprev0', 'cdec_a', 'cdec_b', 'cdecr', 'wlr', 'cols'] + [('mnew', h_) for h_ in range(4)]
            o_sm = A.alloc(4); sm_sb = A.f32(o_sm, [4])
            o_sg2 = A.alloc(16); sg2 = A.f32(o_sg2, [4, 4])
            HT0 = A.top

            norm_to_hT(0, HT0)
            dbg('hT', hT, [128, 8, TC], hT_reads())
            done('norm1')

            S.region = None
            S.defer_begin()
            bank_mode['m'] = 'aux'
            S.op('pool', lambda e: e.dma_start(out=wg, in_=win_v[:, :, 2048:2056]), writes=['wg'], dma=True)
            bg1 = bank()
            for t in range(NT):
                mm_group(Pf(bg1)[:, t * 8:(t + 1) * 8],
                         [(hT[:, kc, t * 128:(t + 1) * 128], wg[:, kc, :]) for kc in range(8)],
                         hT_reads(gs=[t // 4]) + ['wg'], bg1)
            bg2 = bank()
            mm_group(Pf(bg2)[0:NS, 0:8], [(hT[:, kc, SEQ:TC], wg[:, kc, :]) for kc in range(8)],
                     hT_reads(gs=[4]) + ['wg'], bg2)
            S.mark()
            S.op('dve', lambda e: e.tensor_tensor(
                out=gtm.rearrange("p (g c h) -> p g c h", g=2, c=16),
                in0=Pf(bg1)[:, 0:128].rearrange("p (c g h) -> p g c h", c=16, g=2),
                in1=gb_bc, op=ALU.add), reads=[PSR(bg1), ('gbb', 0), ('gbb', 1)], writes=['gtm'])
            S.op('dve', lambda e: e.tensor_tensor(
                out=gsm[0:NS, :].rearrange("p (g h) -> p g h", g=2), in0=Pf(bg2)[0:NS, 0:8].rearrange("p (g h) -> p g h", g=2),
                in1=gb_bc[0:NS, :, 0, :], op=ALU.add), reads=[PSR(bg2), ('gbb', 0), ('gbb', 1)], writes=['gsm'])
            bt = bank()

            def trg(e):
                e.transpose(out=Pf(bt)[0:64, 0:128], in_=gtm[:, 0:64], identity=identf)
                return e.transpose(out=Pf(bt)[0:64, 128:256], in_=gtm[:, 64:128], identity=identf)
            S.mark()
            S.op('pe', trg, reads=['gtm', 'identf'], writes=[PSR(bt)])
            S.mark()
            r64 = lambda nm: R[nm][0:64, :]
            S.op('act', lambda e: e.activation(out=r64('ig'), in_=Pf(bt)[0:64, 0:128], func=AF.Copy),
                 reads=[PSR(bt)], writes=['R_ig'])
            S.op('act', lambda e: e.activation(out=r64('t1'), in_=Pf(bt)[0:64, 128:256], func=AF.Exp, scale=-1.0),
                 reads=[PSR(bt)], writes=['R_t1'])
            S.op('act', lambda e: e.activation(out=r64('sp'), in_=r64('t1'), func=AF.Ln, bias=1.0),
                 reads=['R_t1'], writes=['R_sp'])
            S.op('dve', lambda e: e.tensor_tensor_scan(out=r64('bn'), data0=ones_f[0:64, :], data1=r64('sp'), initial=0.0,
                                                       op0=ALU.mult, op1=ALU.add), reads=['R_sp', 'ones'], writes=['R_bn'])
            S.op('dve', lambda e: e.tensor_tensor(out=r64('a'), in0=r64('ig'), in1=r64('bn'), op=ALU.add),
                 reads=['R_ig', 'R_bn'], writes=['R_a'])
            S.op('dve', lambda e: e.tensor_tensor_scan(out=r64('A'), data0=ones_f[0:64, :], data1=r64('a'), initial=-3.0e38,
                                                       op0=ALU.mult, op1=ALU.max), reads=['R_a', 'ones'], writes=['R_A'])
            bt2 = bank()

            def trl(e):
                e.transpose(out=Pf(bt2)[0:1, 0:64], in_=R['A'][0:64, 127:128], identity=identf[0:64, 0:64])
                return e.transpose(out=Pf(bt2)[0:1, 64:128], in_=R['bn'][0:64, 127:128], identity=identf[0:64, 0:64])
            S.mark()
            S.op('pe', trl, reads=['R_A', 'R_bn', 'identf'], writes=[PSR(bt2)])
            S.mark()
            rowA = rowsb[0:1, 0, :]; rowbn = rowsb[0:1, 1, :]; mnew = rowsb[0:1, 2, :]; mprev = rowsb[0:1, 3, :]
            cdecr = rowsb[0:1, 4, :]; wlr = rowsb[0:1, 5, :]
            S.op('act', lambda e: e.activation(out=rowsb[0:1, 0:2, :], in_=Pf(bt2)[0:1, 0:128].rearrange("p (a b) -> p a b", a=2),
                                               func=AF.Copy), reads=[PSR(bt2)], writes=['rowA'])
            for h in range(4):
                sl = lambda ap, h=h: ap.rearrange("p (c h) -> p c h", h=4)[:, :, h]
                S.op('dve', lambda e, sl=sl: e.tensor_tensor_scan(out=sl(mnew), data0=sl(rowA), data1=sl(rowbn),
                                                                  initial=M_INIT, op0=ALU.max, op1=ALU.subtract),
                     reads=['rowA'], writes=[('mnew', h)])
            mn_r = [('mnew', h) for h in range(4)]
            S.op('pool', lambda e: e.memset(mprev[:, 0:4], M_INIT), writes=['mprev0'])
            S.op('dve', lambda e: e.tensor_copy(out=mprev[:, 4:64], in_=mnew[:, 0:60]), reads=mn_r, writes=['mprev'])
            S.op('dve', lambda e: e.tensor_tensor(out=cdecr, in0=mprev, in1=mnew, op=ALU.subtract),
                 reads=mn_r + ['mprev', 'mprev0'], writes=['cdec_a'])
            S.op('dve', lambda e: e.tensor_tensor(out=cdecr, in0=cdecr, in1=rowbn, op=ALU.subtract),
                 reads=['cdec_a', 'rowA'], writes=['cdec_b'])
            S.op('act', lambda e: e.activation(out=cdecr, in_=cdecr, func=AF.Exp), reads=['cdec_b'], writes=['cdecr'])
            S.op('dve', lambda e: e.scalar_tensor_tensor(out=wlr, in0=rowbn, scalar=-1.0, in1=mnew, op0=ALU.mult,
                                                         op1=ALU.subtract), reads=mn_r + ['rowA'], writes=['wlr'])
            S.op('sp', lambda e: e.dma_start(out=pm.rearrange("(o h) -> o h", o=1), in_=mnew[:, 60:64]), reads=mn_r, dma=True)
            bt3 = bank()

            def col_mm(e):
                e.matmul(Pf(bt3)[0:64, 0:2], mprev, ones_f[0:1, 0:2], start=True, stop=True)
                e.matmul(Pf(bt3)[0:64, 2:4], wlr, ones_f[0:1, 0:2], start=True, stop=True)
                return e.matmul(Pf(bt3)[:, 64:128], ones_f[0:1, :], cdecr, start=True, stop=True)
            S.mark()
            S.op('pe', col_mm, reads=['mprev', 'mprev0', 'wlr', 'cdecr', 'ones'], writes=[PSR(bt3)])
            S.mark()
            S.op('act', lambda e: e.activation(out=colsb[0:64, :], in_=Pf(bt3)[0:64, 0:4], func=AF.Copy),
                 reads=[PSR(bt3)], writes=['cols'])
            S.op('act', lambda e: e.activation(out=cdec_bc.rearrange("p c h -> p (c h)"), in_=Pf(bt3)[:, 64:128], func=AF.Copy),
                 reads=[PSR(bt3)], writes=['cdec_bc'])
            mprev_col = colsb[0:64, 0:1]; wl_col = colsb[0:64, 2:3]
            S.op('dve', lambda e: e.tensor_scalar(out=r64('g'), in0=r64('A'), scalar1=mprev_col, scalar2=None, op0=ALU.max),
                 reads=['R_A', 'cols'], writes=['R_g'])
            S.op('sp', lambda e: e.dma_start(out=gscr, in_=r64('g')), reads=['R_g'], writes=['gscr'], dma=True)
            S.op('dve', lambda e: e.tensor_tensor(out=r64('m'), in0=r64('g'), in1=r64('bn'), op=ALU.subtract),
                 reads=['R_g', 'R_bn'], writes=['R_m'])
            S.op('act', lambda e: e.activation(out=r64('emt'), in_=r64('m'), func=AF.Exp, scale=-1.0),
                 reads=['R_m'], writes=['R_emt'])
            S.op('act', lambda e: e.activation(out=r64('dec'), in_=r64('g'), func=AF.Exp, scale=-1.0, bias=mprev_col),
                 reads=['R_g', 'cols'], writes=['R_dec'])
            S.op('act', lambda e: e.activation(out=r64('w'), in_=r64('a'), func=AF.Exp, bias=wl_col),
                 reads=['R_a', 'cols'], writes=['R_w'])
            bt4 = bank()

            def trq(e):
                for q, nm in enumerate(['a', 'w', 'dec', 'emt']):
                    inst = e.transpose(out=Pf(bt4)[:, q * 64:(q + 1) * 64], in_=r64(nm), identity=identf[0:64, 0:64])
                return inst
            S.mark()
            S.op('pe', trq, reads=['R_a', 'R_w', 'R_dec', 'R_emt', 'identf'], writes=[PSR(bt4)])
            S.mark()
            S.op('act', lambda e: e.activation(out=tokq.rearrange("p q c h -> p (q c h)"), in_=Pf(bt4)[:, 0:256], func=AF.Copy),
                 reads=[PSR(bt4)], writes=['tokq'])
            dbg('tokq', tokq.rearrange("p q c h -> p (q c h)"), [128, 256], ['tokq'])
            dbg('cdec', cdec_bc.rearrange("p c h -> p (c h)"), [128, 64], ['cdec_bc'])
            S.op('sp', lambda e: e.nop(), reads=GATE_NAMES, writes=GATE_NAMES + [(n_, h_) for n_ in ('zsh', 'qTs', 'kTs') for h_ in range(4)])

            S.op('sp', lambda e: e.dma_start(out=sm_sb[0:NS, :], in_=sm), writes=['sm_sb'], dma=True)
            igs = gsm[0:NS, 0:4]; fps = gsm[0:NS, 4:8]
            s_mt = sg[0:NS, 0, :]; s_dm = sg[0:NS, 1, :]; s_dec = sg[0:NS, 2, :]; s_emt = sg[0:NS, 3, :]
            s_t0 = sg2[0:NS, 0, :]; s_t1 = sg2[0:NS, 1, :]; s_mi = sg2[0:NS, 2, :]
            S.op('act', lambda e: e.activation(out=s_t0, in_=fps, func=AF.Exp, scale=-1.0), reads=['gsm'], writes=['s_t0'])
            S.op('act', lambda e: e.activation(out=s_t1, in_=s_t0, func=AF.Ln, bias=1.0), reads=['s_t0'], writes=['s_t1'])
            S.op('dve', lambda e: e.tensor_tensor(out=s_mi, in0=sm_sb[0:NS, :], in1=s_t1, op=ALU.subtract),
                 reads=['s_t1', 'sm_sb'], writes=['s_mi'])
            S.op('dve', lambda e: e.tensor_tensor(out=s_mt, in0=s_mi, in1=igs, op=ALU.max), reads=['s_mi', 'gsm'], writes=['s_mt'])
            S.op('sp', lambda e: e.dma_start(out=smo, in_=s_mt), reads=['s_mt'], dma=True)
            S.op('dve', lambda e: e.tensor_tensor(out=s_t0, in0=igs, in1=s_mt, op=ALU.subtract),
                 reads=['s_mt', 'gsm', 's_t0'], writes=['s_t0b'])
            S.op('act', lambda e: e.activation(out=s_dm, in_=s_t0, func=AF.Exp), reads=['s_t0b'], writes=['s_dm'])
            S.op('dve', lambda e: e.tensor_tensor(out=s_t1, in0=s_mi, in1=s_mt, op=ALU.subtract),
                 reads=['s_mt', 's_mi', 's_t1'], writes=['s_t1b'])
            S.op('act', lambda e: e.activation(out=s_dec, in_=s_t1, func=AF.Exp), reads=['s_t1b'], writes=['s_dec'])
            S.op('act', lambda e: e.activation(out=s_emt, in_=s_mt, func=AF.Exp, scale=-1.0), reads=['s_mt'], writes=['s_emt'])
            S.op('sp', lambda e: e.dma_start(out=dscr.rearrange("h b -> b h"), in_=s_dec, allow_slow_non_contiguous=True),
                 reads=['s_dec'], writes=['dscr'], dma=True)

            S.defer_end()
            bank_mode['m'] = 'main6'
            S.barrier(old=['norm'], new=['conv'])
            S.region = 'conv'
            o_cu = HT0
            cu_sb = A.f32(o_cu, [2050])
            gct = [A.f32(o_cu + 2050 + i * 512, [512]) for i in range(2)]
            acc = [A.f32(o_cu + 2050 + 1024 + i * 512, [512]) for i in range(2)]
            gbt = [A.f32(o_cu + 2050 + 2048 + i * 512, [512]) for i in range(2)]
            o_cs = o_cu + 2050 + 3072
            sct = A.f32(o_cs, [1024])
            cvs = A.f32(o_cs + 1024, [3, 16])
            cvt = A.f32(o_cs + 1024 + 48, [2, 16])
            cuo = A.f32(o_cs + 1024 + 48 + 32, [512])
            S.op('pool', lambda e: e.memset(cu_sb[:, 0:2], 0.0), writes=['cu0'])
            S.op('sp', lambda e: e.dma_start(out=sct[0:NS, :], in_=sconv.rearrange("b j c -> b (j c)")), writes=['sct'], dma=True)
            S.op('sp', lambda e: e.dma_start(out=sconvo[:, 0, :], in_=sconv[:, 1, :]), dma=True)
            bs = bank()

            def trsc(e):
                for i in range(8):
                    inst = e.transpose(out=Pf(bs)[:, i * 16:(i + 1) * 16], in_=sct[0:NS, i * 128:(i + 1) * 128],
                                       identity=identf[0:NS, 0:NS])
                return inst
            S.op('pe', trsc, reads=['sct', 'identf'], writes=[PSR(bs)])
            S.op('act', lambda e: e.activation(out=bufT.rearrange("p a b -> p (a b)"), in_=Pf(bs)[:, 0:128], func=AF.Copy),
                 reads=[PSR(bs)], writes=['bufT'])
            for c in range(4):
                wi = c % 2; wbuf = wbufs[wi]
                for jj in range(3):
                    col0 = 2056 + jj * 512 + c * 128
                    S.op('pool', lambda e, jj=jj, col0=col0, wbuf=wbuf: e.dma_start(out=wbuf[:, :, jj, :], in_=win_v[:, :, col0:col0 + 128]),
                         writes=[('wbuf', wi, jj)], dma=True)
                for tb in range(4):
                    bb = [bank(), bank(), bank()]
                    for jj in range(3):
                        mm_group(Pf(bb[jj]), [(wbuf[:, kc, jj, :], hT[:, kc, tb * 512:(tb + 1) * 512]) for kc in range(8)],
                                 hT_reads(gs=[tb]) + [('wbuf', wi, jj)], bb[jj])
                    i2 = tb % 2
                    S.op('act', lambda e, i2=i2, b=bb[1]: e.activation(out=gct[i2], in_=Pf(b), func=AF.Copy),
                         reads=[PSR(bb[1])], writes=[('gct', i2)])
                    S.op('dve', lambda e, i2=i2, b=bb[2], tb=tb: e.tensor_tensor(
                        out=cu_sb[:, 2 + tb * 512: 2 + (tb + 1) * 512], in0=gct[i2], in1=Pf(b), op=ALU.mult),
                        reads=[('gct', i2), PSR(bb[2])], writes=[('cu', tb)])
                    S.op('act', lambda e, i2=i2, b=bb[0]: e.activation(out=gbt[i2], in_=Pf(b), func=AF.Copy),
                         reads=[PSR(bb[0])], writes=[('gbt', i2)])
                    cur = [('cu', tb), ('cu', tb - 1) if tb > 0 else 'cu0']
                    S.op('dve', lambda e, i2=i2, tb=tb, c=c: e.tensor_scalar(
                        out=acc[i2], in0=cu_sb[:, tb * 512: tb * 512 + 512], scalar1=cwcol[:, c, 0:1], scalar2=None, op0=ALU.mult),
                        reads=cur + ['cwcol'], writes=[('acc', i2)])
                    for jw in (1, 2):
                        S.op('dve', lambda e, i2=i2, tb=tb, c=c, jw=jw: e.scalar_tensor_tensor(
                            out=acc[i2], in0=cu_sb[:, tb * 512 + jw: tb * 512 + jw + 512], scalar=cwcol[:, c, jw:jw + 1],
                            in1=acc[i2], op0=ALU.mult, op1=ALU.add), reads=cur + ['cwcol', ('acc', i2)], writes=[('acc', i2)])
                    S.op('dve', lambda e, i2=i2, tb=tb, c=c: e.tensor_tensor(
                        out=mixT[:, 4 + c, tb * 512:(tb + 1) * 512], in0=acc[i2], in1=gbt[i2], op=ALU.mult),
                        reads=[('acc', i2), ('gbt', i2)], writes=[('mixT', 4 + c, tb)])
                    S.replay_stage()
                S.op('sp', lambda e, c=c: e.dma_start(out=pconv.rearrange("j (c p) -> p c j", p=128)[:, c, :],
                                                      in_=cu_sb[:, 2048:2050], allow_slow_non_contiguous=True),
                     reads=[('cu', 3)], dma=True)
                bsx = bank()
                for jj in range(3):
                    mm_group(Pf(bsx)[:, jj * 16:(jj + 1) * 16], [(wbuf[:, kc, jj, :], hT[:, kc, SEQ:TC]) for kc in range(8)],
                             hT_reads(gs=[4]) + [('wbuf', wi, jj)], bsx)
                S.op('act', lambda e, b=bsx: e.activation(out=cvs.rearrange("p a b -> p (a b)"), in_=Pf(b)[:, 0:48], func=AF.Copy),
                     reads=[PSR(bsx)], writes=['cvs'])
                S.op('dve', lambda e, c=c: e.tensor_tensor(out=cuTs[:, c, :], in0=cvs[:, 1, :], in1=cvs[:, 2, :], op=ALU.mult),
                     reads=['cvs'], writes=[('cuTs', c)])
                S.op('dve', lambda e, c=c: e.tensor_scalar(out=cvt[:, 0, :], in0=bufT[:, c, :], scalar1=cwcol[:, c, 0:1],
                                                           scalar2=None, op0=ALU.mult), reads=['bufT', 'cwcol'], writes=['cvt0'])
                S.op('dve', lambda e, c=c: e.scalar_tensor_tensor(out=cvt[:, 1, :], in0=bufT[:, 4 + c, :], scalar=cwcol[:, c, 1:2],
                                                                  in1=cvt[:, 0, :], op0=ALU.mult, op1=ALU.add),
                     reads=['bufT', 'cwcol', 'cvt0'], writes=['cvt1'])
                S.op('dve', lambda e, c=c: e.scalar_tensor_tensor(out=cvt[:, 0, :], in0=cuTs[:, c, :], scalar=cwcol[:, c, 2:3],
                                                                  in1=cvt[:, 1, :], op0=ALU.mult, op1=ALU.add),
                     reads=[('cuTs', c), 'cwcol', 'cvt1', 'cvt0'], writes=['cvt2'])
                S.op('dve', lambda e, c=c: e.tensor_tensor(out=mixT[:, 4 + c, SEQ:TC], in0=cvt[:, 0, :], in1=cvs[:, 0, :], op=ALU.mult),
                     reads=['cvt2', 'cvs'], writes=[('mixT', 4 + c, 4)])
            S.replay_all()
            bank_mode['m'] = None
            bso = bank()

            def trcu(e):
                for c in range(4):
                    inst = e.transpose(out=Pf(bso)[0:NS, c * 128:(c + 1) * 128], in_=cuTs[:, c, :], identity=identf)
                return inst
            S.op('pe', trcu, reads=[('cuTs', c) for c in range(4)] + ['identf'], writes=[PSR(bso)])
            S.op('act', lambda e: e.activation(out=cuo[0:NS, :], in_=Pf(bso)[0:NS, :], func=AF.Copy), reads=[PSR(bso)], writes=['cuo'])
            S.op('sp', lambda e: e.dma_start(out=sconvo[:, 1, :], in_=cuo[0:NS, :]), reads=['cuo'], dma=True)
            dbg('mixT', mixT, [128, 8, TC], [('mixT', 4 + c, g) for c in range(4) for g in range(5)])

            done('conv')
            S.barrier(old=['conv'], new=['headp'])
            A.top = HT0
            o_qT = A.alloc(TC // 2); qT = A.bf16(o_qT, [TC])
            o_kT = A.alloc(TC // 2); kT = A.bf16(o_kT, [TC])
            o_ktm = A.alloc(16 * 64); k_tm = A.bf16(o_ktm, [16, 128])
            o_va = A.alloc(16 * 65); v_aug = A.bf16(o_va, [16, 130])
            o_og = A.alloc(16 * 64); og_t = A.bf16(o_og, [16, 128])
            o_kw = A.alloc(2 * 64); kw = [A.bf16(o_kw + i * 64, [128]) for i in range(2)]
            o_na = A.alloc(16 * 129); numaug = A.f32(o_na, [16, 129])
            o_gbc = A.alloc(2048); gbc = A.f32(o_gbc, [16, 128]); sqtmp = gbc
            o_E = A.alloc(2 * 128); E = [A.f32(o_E + i * 128, [128]) for i in range(2)]
            o_PT = A.alloc(2 * 64); PT = [A.bf16(o_PT + i * 64, [128]) for i in range(2)]
            o_ti = A.alloc(2 * 129); tmpi = [A.f32(o_ti + i * 129, [129]) for i in range(2)]
            o_Cs = A.alloc(2 * 129); Cst = [A.f32(o_Cs + i * 129, [129]) for i in range(2)]
            o_Cb = A.alloc(16 * 65); Cb_all = A.bf16(o_Cb, [16, 130])
            o_sm2 = A.alloc(80); hsm = A.f32(o_sm2, [5, 16])
            hmix = og_t
            HEAD_END = A.top
            assert HEAD_END <= NW - 2048, (HEAD_END, NW)

            def load_head_w(h_):
                wi_ = h_ % 2
                for jj in range(4):
                    col0 = jj * 512 + h_ * 128
                    S.op('pool', lambda e, jj=jj, col0=col0, wb_=wbufs[wi_]: e.dma_start(out=wb_[:, :, jj, :], in_=win_v[:, :, col0:col0 + 128]),
                         writes=[('wbuf', wi_, jj)], dma=True)

            load_head_w(0)
            for h in range(4):
                S.region = 'headp'
                wi = h % 2; wbuf = wbufs[wi]
                S.op('pool', lambda e: e.memset(v_aug[:, :, 128:130], 1.0), writes=['v_one'])
                done('h%da' % h)
                def fm_proj(h=h, wi=wi, wbuf=wbuf):
                    for jj, dstT, sc in ((0, qT, QSCALE), (1, kT, 1.0)):
                        for tb in range(4):
                            b = bank()
                            mm_group(Pf(b), [(wbuf[:, kc, jj, :], hT[:, kc, tb * 512:(tb + 1) * 512]) for kc in range(8)],
                                     hT_reads(gs=[tb]) + [('wbuf', wi, jj)], b)
                            nm = 'qT' if jj == 0 else 'kT'
                            if tb % 2 == 0:
                                S.op('act', lambda e, b=b, dstT=dstT, tb=tb, sc=sc: e.activation(
                                    out=dstT[:, tb * 512:(tb + 1) * 512], in_=Pf(b), func=AF.Copy, scale=sc),
                                    reads=[PSR(b)], writes=[(nm, tb)])
                            else:
                                S.op('dve', lambda e, b=b, dstT=dstT, tb=tb, sc=sc: e.tensor_scalar(
                                    out=dstT[:, tb * 512:(tb + 1) * 512], in0=Pf(b), scalar1=sc, scalar2=None, op0=ALU.mult),
                                    reads=[PSR(b)], writes=[(nm, tb)])
                            yield
                            yield
                        b = bank()
                        mm_group(Pf(b)[:, 0:16], [(wbuf[:, kc, jj, :], hT[:, kc, SEQ:TC]) for kc in range(8)],
                                 hT_reads(gs=[4]) + [('wbuf', wi, jj)], b)
                        dsts = qTs_all[:, h, :] if jj == 0 else kTs_all[:, h, :]
                        S.op('act', lambda e, b=b, dsts=dsts, sc=sc: e.activation(out=dsts, in_=Pf(b)[:, 0:16], func=AF.Copy, scale=sc),
                             reads=[PSR(b)], writes=[('qTs', h) if jj == 0 else ('kTs', h)])
                done('h%db' % h)
                for t in range(NT):
                    b = bank()
                    mm_group(Pf(b)[:, 0:384], [(hT[:, kc, t * 128:(t + 1) * 128], wbuf[:, kc, 1:4, :]) for kc in range(8)],
                             hT_reads(gs=[t // 4]) + [('wbuf', wi, jj) for jj in (1, 2, 3)], b)
                    import os
                    VAR = int(os.environ.get('KVAR', '7'))
                    if VAR & 1:
                        S.op('act', lambda e, b=b, t=t: e.activation(out=k_tm[:, t, :], in_=Pf(b)[:, 0:128], func=AF.Copy),
                             reads=[PSR(b)], writes=[('k_tm', t)])
                    if VAR & 2:
                        S.op('dve', lambda e, b=b, t=t: e.tensor_copy(out=v_aug[:, t, 0:128], in_=Pf(b)[:, 128:256]),
                             reads=[PSR(b)], writes=[('v_aug', t)])
                    if VAR & 4:
                        S.op('act', lambda e, b=b, t=t: e.activation(out=og_t[:, t, :], in_=Pf(b)[:, 256:384], func=AF.Tanh, scale=0.5),
                             reads=[PSR(b)], writes=[('og_t', t)])
                done('h%dc' % h)
                b = bank()
                mm_group(Pf(b)[0:NS, 0:512], [(hT[:, kc, SEQ:TC], wbuf[:, kc, 0:4, :]) for kc in range(8)],
                         hT_reads(gs=[4]) + [('wbuf', wi, jj) for jj in range(4)], b)
                S.op('act', lambda e, b=b, h=h: e.activation(out=zsh_all[0:NS, h, :], in_=Pf(b)[0:NS, 0:512], func=AF.Copy),
                     reads=[PSR(b)], writes=[('zsh', h)])
                done('h%dproj' % h)
                if h + 1 < 4:
                    load_head_w(h + 1)
                S.op('sp', lambda e, h=h: e.dma_start(
                    out=gbc, in_=bass.AP(tensor=gscr_t, offset=h * 128, ap=[[0, 128], [512, 16], [1, 128]])),
                    reads=['gscr'], writes=['gbc'], dma=True)
                def pass12(h=h):
                    S.op('pool', lambda e: e.memset(Cst[0], 0.0), writes=[('Cst', 0)])
                    S.op('pool', lambda e: e.memset(Cb_all[:, 0, :], 0.0), writes=[('Cb', 0)])
                    dcb = {}
                    for i in range(NT + 4):
                        if i < NT:
                            c = i; i3 = c % 2
                            S.op('dve', lambda e, i3=i3, c=c, h=h: e.tensor_scalar(out=kw[i3], in0=k_tm[:, c, :], scalar1=tokq[:, 1, c, h:h + 1],
                                                                                   scalar2=None, op0=ALU.mult),
                                 reads=[('k_tm', c), 'tokq'], writes=[('kw', i3)])
                        if 0 <= i - 1 < NT:
                            c = i - 1; i3 = c % 2
                            b3 = bank(); dcb[c] = b3
                            mm_group(Pf(b3)[:, 0:129], [(kw[i3], v_aug[:, c, 0:129])], [('kw', i3), ('v_aug', c), 'v_one'], b3)
                        if 0 <= i - 2 < NT:
                            c = i - 2
                            S.op('act', lambda e, c=c, b3=dcb[c]: e.activation(out=numaug[:, c, :], in_=Pf(b3)[:, 0:129], func=AF.Copy),
                                 reads=[PSR(dcb[c])], writes=[('numaug', c)])
                        if 0 <= i - 3 < NT:
                            c = i - 3; j0_ = c % 2; j1_ = (c + 1) % 2
                            S.op('dve', lambda e, j0_=j0_, j1_=j1_, c=c, h=h: e.scalar_tensor_tensor(
                                out=Cst[j1_], in0=Cst[j0_], scalar=cdec_bc[:, c, h:h + 1], in1=numaug[:, c, :], op0=ALU.mult, op1=ALU.add),
                                reads=[('numaug', c), ('Cst', j0_), 'cdec_bc'], writes=[('Cst', j1_)])
                            if c < NT - 1:
                                S.op('act', lambda e, j1_=j1_, c=c: e.activation(out=Cb_all[:, c + 1, 0:129], in_=Cst[j1_], func=AF.Copy),
                                     reads=[('Cst', j1_)], writes=[('Cb', c + 1)])
                        yield

                g1 = pass12(); g2 = fm_proj()
                alive = True
                while alive:
                    alive = False
                    for g_ in (g1, g2):
                        try:
                            next(g_); alive = True
                        except StopIteration:
                            pass
                sb_ = {}; pvb = {}
                ssn = hsm[:, 4, :]
                for i in range(NT + 4):
                    if i < NT:
                        c = i; i3 = c % 2
                        cs = slice(c * 128, (c + 1) * 128)
                        b1 = bank(); sb_[c] = b1
                        mm_group(Pf(b1)[:, 0:128], [(kT[:, cs], qT[:, cs])], [('kT', c // 4), ('qT', c // 4)], b1)
                        S.op('act', lambda e, i3=i3, c=c, h=h: e.activation(out=E[i3], in_=gbc[:, c, :], func=AF.Exp, scale=-1.0,
                                                                            bias=tokq[:, 0, c, h:h + 1]),
                             reads=['gbc', 'tokq'], writes=[('E', i3)])
                        S.op('pool', lambda e, i3=i3: e.affine_select(out=E[i3], in_=E[i3], pattern=[[1, 128]], compare_op=ALU.is_ge,
                                                                      fill=0.0, base=0, channel_multiplier=-1),
                             reads=[('E', i3)], writes=[('E', i3)])
                    if 0 <= i - 1 < NT:
                        c = i - 1; i3 = c % 2
                        b1 = sb_[c]
                        S.op('dve', lambda e, i3=i3, b1=b1: e.tensor_tensor(out=PT[i3], in0=E[i3], in1=Pf(b1)[:, 0:128], op=ALU.mult),
                             reads=[('E', i3), PSR(b1)], writes=[('PT', i3)])
                    if 0 <= i - 2 < NT:
                        c = i - 2; i3 = c % 2
                        cs = slice(c * 128, (c + 1) * 128)
                        b2 = bank(); pvb[c] = b2

                        def pv(e, b2=b2, i3=i3, c=c, cs=cs):
                            e.matmul(Pf(b2)[:, 0:129], PT[i3], v_aug[:, c, 0:129], start=True, stop=True)
                            return e.matmul(Pf(b2)[:, 256:385], qT[:, cs], Cb_all[:, c, 0:129], start=True, stop=True)
                        S.op('pe', pv, reads=[('PT', i3), ('v_aug', c), 'v_one', ('qT', c // 4), ('Cb', c)], writes=[PSR(b2)])
                    if 0 <= i - 3 < NT:
                        c = i - 3; i2 = c % 2
                        b2 = pvb[c]
                        S.op('act', lambda e, b2=b2, i2=i2, c=c, h=h: e.activation(out=tmpi[i2], in_=Pf(b2)[:, 256:385], func=AF.Copy,
                                                                                   scale=tokq[:, 2, c, h:h + 1]),
                             reads=[PSR(b2), 'tokq'], writes=[('tmpi', i2)])
                        S.op('dve', lambda e, b2=b2, i2=i2, c=c: e.tensor_tensor(out=numaug[:, c, :], in0=tmpi[i2], in1=Pf(b2)[:, 0:129],
                                                                                 op=ALU.add),
                             reads=[PSR(b2), ('tmpi', i2)], writes=[('numaug', c)])
                    if 0 <= i - 4 < NT:
                        c = i - 4
                        S.op('dve', lambda e, c=c: e.scalar_tensor_tensor(out=kw[0], in0=numaug[:, c, 0:128], scalar=1.0, in1=numaug[:, c, 0:128],
                                                                          op0=ALU.mult, op1=ALU.mult, accum_out=ssn[:, c:c + 1]),
                             reads=[('numaug', c)], writes=[('kw', 0), ('ssn', c)])
                jf = NT % 2
                S.op('sp', lambda e, h=h, jf=jf: e.dma_start(out=pC[h], in_=Cst[jf][:, 0:128]), reads=[('Cst', jf)], dma=True)
                S.op('sp', lambda e, h=h, jf=jf: e.dma_start(out=pn[h].rearrange("(p o) -> p o", o=1), in_=Cst[jf][:, 128:129]),
                     reads=[('Cst', jf)], dma=True)
                done('h%dchunk' % h)
                na_r = [('numaug', c) for c in range(NT)]
                dn = hsm[:, 0, :]; rr = hsm[:, 1, :]; ss2 = hsm[:, 2, :]; rs = hsm[:, 3, :]
                hmv = numaug[:, :, 0:128]
                S.op('dve', lambda e: e.scalar_tensor_tensor(out=dn, in0=numaug[:, :, 128], scalar=-1.0, in1=numaug[:, :, 128],
                                                             op0=ALU.mult, op1=ALU.max), reads=na_r, writes=['dn0'])
                S.op('dve', lambda e, h=h: e.tensor_tensor(out=dn, in0=dn, in1=tokq[:, 3, :, h], op=ALU.max),
                     reads=['dn0', 'tokq'], writes=['dn'])
                S.op('dve', lambda e: e.reciprocal(out=rr, in_=dn), reads=['dn'], writes=['rr'])
                S.op('dve', lambda e: e.tensor_tensor(out=ss2, in0=ssn, in1=rr, op=ALU.mult), reads=['rr'] + [('ssn', c) for c in range(NT)], writes=['ss2'])
                S.op('dve', lambda e: e.tensor_tensor(out=ss2, in0=ss2, in1=rr, op=ALU.mult), reads=['ss2', 'rr'], writes=['ss2a'])
                S.op('dve', lambda e: e.tensor_scalar(out=ss2, in0=ss2, scalar1=1.0 / 128, scalar2=EPS, op0=ALU.mult, op1=ALU.add),
                     reads=['ss2a'], writes=['ss2b'])
                S.op('pool', lambda e: e.tensor_tensor(out=rs, in0=ss2, in1=mhalf[:, 0:16], op=ALU.pow), reads=['ss2b', 'mhalf'], writes=['rs'])
                S.op('dve', lambda e: e.scalar_tensor_tensor(out=rs, in0=rs, scalar=0.5, in1=rr, op0=ALU.mult, op1=ALU.mult),
                     reads=['rs', 'rr'], writes=['rsb'])
                S.op('dve', lambda e: e.tensor_tensor(out=hmv, in0=hmv, in1=rs.unsqueeze(2).broadcast_to([128, 16, 128]), op=ALU.mult),
                     reads=na_r + ['rsb'], writes=['hmn'])
                S.op('dve', lambda e: e.scalar_tensor_tensor(out=hmix, in0=og_t, scalar=1.0, in1=hmv, op0=ALU.add, op1=ALU.mult),
                     reads=['hmn'] + [('og_t', t) for t in range(NT)], writes=[('og_t', t) for t in range(NT)])
                for half in range(2):
                    b = bank()

                    def trh(e, b=b, half=half):
                        for i in range(8):
                            inst = e.transpose(out=Pb(b)[:, i * 128:(i + 1) * 128], in_=hmix[:, half * 8 + i, :], identity=identb)
                        return inst
                    S.op('pe', trh, reads=['identb'] + [('og_t', half * 8 + i) for i in range(8)], writes=[PSR(b)])
                    if half == 0:
                        S.op('act', lambda e, b=b, h=h: e.activation(out=mixT[:, h, 0:1024], in_=Pb(b), func=AF.Copy,
                                                                     scale=mhcol[:, h:h + 1]),
                             reads=[PSR(b), 'mhcol'], writes=[('mixT', h, 0), ('mixT', h, 1)])
                    else:
                        S.op('dve', lambda e, b=b, h=h: e.tensor_scalar(out=mixT[:, h, 1024:2048], in0=Pb(b), scalar1=mhcol[:, h:h + 1],
                                                                        scalar2=None, op0=ALU.mult),
                             reads=[PSR(b), 'mhcol'], writes=[('mixT', h, 2), ('mixT', h, 3)])

                done('h%dpost' % h)
            dbg('mixT2', mixT, [128, 8, TC], [('mixT', k, g) for k in range(8) for g in range(5)])

            done('heads')
            S.barrier(old=['headp'], new=['wo'])
            S.region = 'wo'
            A.top = HT0
            o_wo = A.alloc(8 * D // 2); wo = A.bf16(o_wo, [8, D])
            sChb = [A.f32(A.alloc(2048), [16, 128]) for i in range(2)]
            wvbc = A.f32(o_wbuf, [16, 128])
            tmpC = [A.f32(A.alloc(128), [128]) for i in range(2)]
            snhb = [A.f32(A.alloc(128), [128]) for i in range(2)]
            dbcb = [A.f32(A.alloc(16), [16]) for i in range(2)]
            st0 = A.f32(A.alloc(128), [128]); st1 = A.f32(A.alloc(128), [128]); st2 = A.f32(A.alloc(128), [128])
            qCT = A.f32(A.alloc(16), [16]); ssm = A.f32(A.alloc(16), [16]); hms = A.bf16(A.alloc(64), [128])
            for half in range(2):
                S.op('pool', lambda e, half=half: e.dma_start(out=wo[:, :, half * 512:(half + 1) * 512],
                                                              in_=w_out.rearrange("(kc p) n -> p kc n", p=128)[:, :, half * 512:(half + 1) * 512]),
                     writes=[('wo', half)], dma=True)
            mix_all = lambda g: [('mixT', k, g) for k in range(8)]

            def load_state(h):
                hb = h % 2
                S.op('sp', lambda e, h=h, hb=hb: e.dma_start(out=sChb[hb], in_=sC[:, h, :, :].rearrange("b k v -> k b v")),
                     writes=[('sCh', hb)], dma=True)
                S.op('sp', lambda e, h=h, hb=hb: e.dma_start(out=snhb[hb][0:NS, :], in_=sn[:, h, :]), writes=[('snh', hb)], dma=True)
                S.op('sp', lambda e, h=h, hb=hb: e.dma_start(out=dbcb[hb], in_=bass.AP(tensor=dscr_t, offset=h * NS, ap=[[0, 128], [1, NS]])),
                     reads=['dscr'], writes=[('dbc', hb)], dma=True)

            def out_proj(t):
                r = rows(t)
                cs = slice(t * 128, t * 128 + r)
                for nb in range(2):
                    b = bank()
                    mm_group(Pf(b)[0:r, :], [(mixT[:, kc, cs], wo[:, kc, nb * 512:(nb + 1) * 512]) for kc in range(8)],
                             mix_all(t // 4 if t < 16 else 4) + [('wo', nb)], b)
                    S.op('dve', lambda e, b=b, t=t, r=r, nb=nb: e.tensor_tensor(out=x_sb[0:r, t, nb * 512:(nb + 1) * 512],
                                                                                in0=x_sb[0:r, t, nb * 512:(nb + 1) * 512], in1=Pf(b)[0:r, :], op=ALU.add),
                         reads=[PSR(b), ('x', t)], writes=[('x', t)])
                norm_sq(t, hT[:, 0, 0:1024], [('hT', 0, 0), ('hT', 0, 1)])
                if t == 16 or t % 4 == 3:
                    norm_rstd(4 if t == 16 else t // 4)

            def sample_head(h):
                hb = h % 2
                sCh = sChb[hb]; snh = snhb[hb]; dbc = dbcb[hb]
                zs = zsh_all[0:NS, h, :]
                qs = zs[:, 0:128]; ks = zs[:, 128:256]; vs = zs[:, 256:384]; ogs = zs[:, 384:512]
                qTs = qTs_all[:, h, :]; kTs = kTs_all[:, h, :]
                ZS = ('zsh', h); ZQ = ('zsh_q', h); SC = ('sCh', hb)
                t0 = st0[0:NS, :]; t1 = st1[0:NS, :]; t2 = st2[0:NS, :]
                sc = lambda i: ssm[0:NS, i:i + 1]
                S.op('dve', lambda e: e.tensor_scalar(out=qs, in0=qs, scalar1=QSCALE, scalar2=None, op0=ALU.mult),
                     reads=[ZS], writes=[ZQ])
                S.op('dve', lambda e: e.tensor_tensor(out=t0, in0=qs, in1=ks, op=ALU.mult), reads=[ZQ, ZS], writes=['st0'])
                S.op('dve', lambda e: e.tensor_reduce(out=sc(0), in_=t0, axis=AX.X, op=ALU.add), reads=['st0'], writes=['qk'])
                S.op('dve', lambda e: e.tensor_tensor(out=t0, in0=qs, in1=snh[0:NS, :], op=ALU.mult), reads=[ZQ, ('snh', hb), 'qk'], writes=['st0b'])
                S.op('dve', lambda e: e.tensor_reduce(out=sc(1), in_=t0, axis=AX.X, op=ALU.add), reads=['st0b'], writes=['qn'])
                S.op('dve', lambda e: e.tensor_tensor(out=sc(2), in0=sc(0), in1=s_dm[:, h:h + 1], op=ALU.mult),
                     reads=['qk', 's_dm'], writes=['scores'])
                S.op('dve', lambda e: e.scalar_tensor_tensor(out=sc(3), in0=sc(1), scalar=s_dec[:, h:h + 1], in1=sc(2),
                                                             op0=ALU.mult, op1=ALU.add), reads=['qn', 's_dec', 'scores'], writes=['den'])
                S.op('dve', lambda e: e.scalar_tensor_tensor(out=sc(4), in0=sc(3), scalar=-1.0, in1=sc(3), op0=ALU.mult, op1=ALU.max),
                     reads=['den'], writes=['denom0'])
                S.op('dve', lambda e: e.tensor_tensor(out=sc(4), in0=sc(4), in1=s_emt[:, h:h + 1], op=ALU.max),
                     reads=['denom0', 's_emt'], writes=['denom'])
                S.op('dve', lambda e: e.reciprocal(out=sc(5), in_=sc(4)), reads=['denom'], writes=['rden'])
                bq = bank()

                def qc(e, bq=bq):
                    for b_ in range(NS):
                        inst = e.matmul(Pf(bq)[:, b_:b_ + 1], sCh[:, b_, :], qTs[:, b_:b_ + 1], start=True, stop=True)
                    return inst
                S.op('pe', qc, reads=[SC, ('qTs', h)], writes=[PSR(bq)])
                S.op('act', lambda e, bq=bq: e.activation(out=qCT, in_=Pf(bq)[:, 0:16], func=AF.Copy), reads=[PSR(bq)], writes=['qCT'])
                bq2 = bank()
                S.op('pe', lambda e, bq2=bq2: e.transpose(out=Pf(bq2)[0:NS, 0:128], in_=qCT, identity=identf),
                     reads=['qCT', 'identf'], writes=[PSR(bq2)])
                S.op('dve', lambda e, bq2=bq2: e.tensor_scalar(out=t1, in0=Pf(bq2)[0:NS, 0:128], scalar1=s_dec[:, h:h + 1],
                                                               scalar2=None, op0=ALU.mult), reads=[PSR(bq2), 's_dec'], writes=['st1'])
                S.op('dve', lambda e: e.scalar_tensor_tensor(out=t1, in0=vs, scalar=sc(2), in1=t1, op0=ALU.mult, op1=ALU.add),
                     reads=[ZS, 'scores', 'st1'], writes=['num_s'])
                S.op('dve', lambda e: e.tensor_scalar(out=t1, in0=t1, scalar1=sc(5), scalar2=None, op0=ALU.mult),
                     reads=['num_s', 'rden'], writes=['hm_s'])
                S.op('dve', lambda e: e.tensor_tensor(out=t0, in0=t1, in1=t1, op=ALU.mult), reads=['hm_s', 'qn'], writes=['st0c'])
                S.op('dve', lambda e: e.tensor_reduce(out=sc(6), in_=t0, axis=AX.X, op=ALU.add), reads=['st0c'], writes=['ss_s'])
                S.op('dve', lambda e: e.tensor_scalar(out=sc(6), in0=sc(6), scalar1=1.0 / 128, scalar2=EPS, op0=ALU.mult, op1=ALU.add),
                     reads=['ss_s'], writes=['ss_sb'])
                S.op('pool', lambda e: e.tensor_tensor(out=sc(7), in0=sc(6), in1=mhalf[0:NS, 0:1], op=ALU.pow), reads=['ss_sb', 'mhalf'], writes=['rs_s'])
                S.op('dve', lambda e: e.tensor_scalar(out=sc(7), in0=sc(7), scalar1=0.5, scalar2=None, op0=ALU.mult), reads=['rs_s'], writes=['rs_sb'])
                S.op('act', lambda e: e.activation(out=t2, in_=ogs, func=AF.Tanh, scale=0.5), reads=[ZS], writes=['st2'])
                S.op('dve', lambda e: e.tensor_scalar(out=t1, in0=t1, scalar1=sc(7), scalar2=None, op0=ALU.mult), reads=['hm_s', 'rs_sb'], writes=['hmn_s'])
                S.op('dve', lambda e: e.scalar_tensor_tensor(out=hms[0:NS, :], in0=t2, scalar=1.0, in1=t1, op0=ALU.add, op1=ALU.mult),
                     reads=['st2', 'hmn_s'], writes=['hms'])
                bq3 = bank()
                S.op('pe', lambda e, bq3=bq3: e.transpose(out=Pb(bq3)[:, 0:16], in_=hms[0:NS, :], identity=identb[0:NS, 0:NS]),
                     reads=['hms', 'identb'], writes=[PSR(bq3)])
                S.op('dve', lambda e, bq3=bq3: e.tensor_scalar(out=mixT[:, h, SEQ:TC], in0=Pb(bq3)[:, 0:16], scalar1=mhcol[:, h:h + 1],
                                                               scalar2=None, op0=ALU.mult), reads=[PSR(bq3), 'mhcol'], writes=[('mixT', h, 4)])
                S.op('dve', lambda e: e.tensor_scalar(out=t0, in0=ks, scalar1=s_dm[:, h:h + 1], scalar2=None, op0=ALU.mult),
                     reads=[ZS, 's_dm', 'ss_s'], writes=['st0d'])
                S.op('dve', lambda e: e.scalar_tensor_tensor(out=t0, in0=snh[0:NS, :], scalar=s_dec[:, h:h + 1], in1=t0,
                                                             op0=ALU.mult, op1=ALU.add), reads=[('snh', hb), 's_dec', 'st0d'], writes=['nnew'])
                S.op('sp', lambda e: e.dma_start(out=sno[:, h, :], in_=t0), reads=['nnew'], writes=['st0'], dma=True)
                S.op('dve', lambda e: e.tensor_scalar(out=t2, in0=vs, scalar1=s_dm[:, h:h + 1], scalar2=None, op0=ALU.mult),
                     reads=[ZS, 's_dm', 'hms'], writes=['wv'])
                S.op('sp', lambda e: e.dma_start(out=wvscr[h], in_=t2), reads=['wv'], writes=[('wvscr', h), 'st2'], dma=True)
                S.op('sp', lambda e: e.dma_start(out=wvbc, in_=bass.AP(tensor=wvscr_t, offset=h * NS * 128,
                                                                        ap=[[0, 128], [128, NS], [1, 128]])),
                     reads=[('wvscr', h)], writes=['wvbc'] + [('wbuf', 0, jj) for jj in range(4)], dma=True)
                for b_ in range(NS):
                    i2 = b_ % 2
                    S.op('act', lambda e, b_=b_, i2=i2: e.activation(out=tmpC[i2], in_=sCh[:, b_, :], func=AF.Copy, scale=dbc[:, b_:b_ + 1]),
                         reads=[SC, ('dbc', hb), 'qCT'], writes=[('tmpC', i2)])
                    S.op('dve', lambda e, b_=b_, i2=i2: e.scalar_tensor_tensor(out=sCh[:, b_, :], in0=wvbc[:, b_, :], scalar=kTs[:, b_:b_ + 1],
                                                                               in1=tmpC[i2], op0=ALU.mult, op1=ALU.add),
                         reads=['wvbc', ('kTs', h), ('tmpC', i2)], writes=[('sChn', hb, b_)])
                S.op('sp', lambda e: e.dma_start(out=sCo[:, h, :, :].rearrange("b k v -> k b v"), in_=sCh),
                     reads=[('sChn', hb, b_) for b_ in range(NS)], writes=[SC], dma=True)

            load_state(0)
            load_state(1)
            for h in range(4):
                for t in range(4 * h, 4 * h + 4):
                    out_proj(t)
                sample_head(h)
                if h + 2 < 4:
                    load_state(h + 2)
            out_proj(16)
            dbg('x1', x_sb, [128, 17, D], [('x', t) for t in range(17)])
            done('outproj')

            A.top = PH
            wu = [None, None]; wd = [None, None]; hid = [None, None]
            wu[0] = A.bf16(A.alloc(8 * 512 // 2), [8, 512]); wd[0] = A.bf16(A.alloc(4 * D // 2), [4, D])
            hid[0] = A.bf16(A.alloc(4 * TC // 2), [4, TC])
            rtmp = [A.f32(A.alloc(512), [512]) for i in range(2)]
            o_nt = A.top
            wu[1] = A.bf16(A.alloc(8 * 512 // 2), [8, 512]); wd[1] = A.bf16(A.alloc(4 * D // 2), [4, D])
            hid[1] = A.bf16(A.alloc(4 * TC // 2), [4, TC])
            B_END = A.top
            o_wgp = A.alloc(8 * D // 2); wgp = A.bf16(o_wgp, [8, D])
            o_wpp = A.alloc(2 * D // 2); wpp = A.bf16(o_wpp, [2, D])
            o_pT = A.alloc(2 * TC // 2); pT = A.bf16(o_pT, [2, TC])
            o_gf = A.alloc(D); gfin = A.f32(o_gf, [D])
            pld = [A.f32(A.alloc(256), [256]) for i in range(1)]
            pbf = [A.bf16(A.alloc(128), [256]) for i in range(1)]
            NORM_NAMES = ['junk'] + [('xn', i_, k_) for i_ in range(2) for k_ in range(4)]
            S.barrier(old=None)
            S.region = 'B'
            S.op('sp', lambda e: e.nop(), writes=['phaseB_ok'])
            def prefetch_c():
                S.region = 'Cpre'
                S.op('pool', lambda e: e.dma_start(out=wgp, in_=w_pg.rearrange("(kc p) n -> p kc n", p=128)), writes=['wgp'], dma=True)
                S.op('pool', lambda e: e.dma_start(out=wpp, in_=w_pp.rearrange("(kc p) n -> p kc n", p=128)), writes=['wpp'], dma=True)
                S.op('sp', lambda e: e.dma_start(out=gfin, in_=nfin.partition_broadcast(128)), writes=['gfin'], dma=True)
                S.region = 'B'
                yield
                for t in range(17):
                    S.region = 'Cpre'
                    r = rows(t)
                    i2 = 0
                    src = pp[t * 128:(t + 1) * 128, :] if t < 16 else psm
                    S.op('sp', lambda e, i2=i2, r=r, src=src: e.dma_start(out=pld[i2][0:r, :], in_=src), writes=[('pld', i2)], dma=True)
                    S.op('act', lambda e, i2=i2, r=r: e.activation(out=pbf[i2][0:r, :], in_=pld[i2][0:r, :], func=AF.Copy),
                         reads=[('pld', i2)], writes=[('pbf', i2)])
                    b = bank()

                    def trp(e, b=b, i2=i2, r=r):
                        for kc in range(2):
                            inst = e.transpose(out=Pb(b)[:, kc * 128:kc * 128 + r], in_=pbf[i2][0:r, kc * 128:(kc + 1) * 128],
                                               identity=identb[0:r, 0:r])
                        return inst
                    S.op('pe', trp, reads=[('pbf', i2), 'identb'], writes=[PSR(b)])
                    S.op('dve', lambda e, b=b, t=t, r=r: e.tensor_copy(out=pT[:, :, t * 128:t * 128 + r],
                                                                       in_=Pb(b)[:, 0:256].rearrange("p (k t) -> p k t", k=2)[:, :, 0:r]),
                         reads=[PSR(b)], writes=[('pT', t)])
                    S.region = 'B'
                    yield

            wu_v = w_up.rearrange("(kc p) n -> p kc n", p=128)
            wd_v = w_down.rearrange("(fc p) n -> p fc n", p=128)
            NFB = 8
            for fb in range(NFB):
                i2 = fb % 2
                extra = NORM_NAMES if fb == 1 else []
                S.op('pool', lambda e, fb=fb, i2=i2: e.dma_start(out=wu[i2], in_=wu_v[:, :, fb * 512:(fb + 1) * 512]),
                     reads=['phaseB_ok'], writes=[('wu', i2)] + extra, dma=True)
                S.op('pool', lambda e, fb=fb, i2=i2: e.dma_start(out=wd[i2], in_=wd_v[:, fb * 4:(fb + 1) * 4, :]),
                     reads=['phaseB_ok'], writes=[('wd', i2)] + extra, dma=True)
                if fb == 0:
                    norm_to_hT(1, o_nt, presq=True)
                    S.region = 'B'
                if fb == 1:
                    pre_gen = prefetch_c()
                    next(pre_gen, None)
                ri = 0
                for fc in range(4):
                    for tb in range(5):
                        ncol = 512 if tb < 4 else NS
                        cs = slice(tb * 512, tb * 512 + ncol)
                        b = bank()
                        mm_group(Pf(b)[:, 0:ncol], [(wu[i2][:, kc, fc * 128:(fc + 1) * 128], hT[:, kc, cs]) for kc in range(8)],
                                 hT_reads(gs=[tb]) + [('wu', i2), 'phaseB_ok'], b)
                        if fb >= 1 and tb in (0, 2):
                            next(pre_gen, None)
                        rt = rtmp[ri % 2]; rk = ri % 2; ri += 1
                        S.op('act', lambda e, b=b, rt=rt, ncol=ncol: e.activation(out=rt[:, 0:ncol], in_=Pf(b)[:, 0:ncol], func=AF.Relu),
                             reads=[PSR(b)], writes=[('rtmp', rk)])
                        S.op('dve', lambda e, b=b, rt=rt, ncol=ncol, i2=i2, fc=fc, cs=cs: e.tensor_tensor(
                            out=hid[i2][:, fc, cs], in0=rt[:, 0:ncol], in1=Pf(b)[:, 0:ncol], op=ALU.mult),
                            reads=[PSR(b), ('rtmp', rk)], writes=[('hid', i2, fc, tb)] + (NORM_NAMES if (fb == 1 and fc == 0 and tb == 0) else []))
                for t in range(17):
                    r = rows(t)
                    cs = slice(t * 128, t * 128 + r)
                    tbk = t // 4 if t < 16 else 4
                    for nb in range(2):
                        b = bank()
                        mm_group(Pf(b)[0:r, :], [(hid[i2][:, fc, cs], wd[i2][:, fc, nb * 512:(nb + 1) * 512]) for fc in range(4)],
                                 [('hid', i2, fc, tbk) for fc in range(4)] + [('wd', i2)], b)
                        S.op('dve', lambda e, b=b, t=t, r=r, nb=nb: e.tensor_tensor(out=x_sb[0:r, t, nb * 512:(nb + 1) * 512],
                                                                                    in0=x_sb[0:r, t, nb * 512:(nb + 1) * 512], in1=Pf(b)[0:r, :], op=ALU.add),
                             reads=[PSR(b), ('x', t)], writes=[('x', t)])
                    if fb == NFB - 1:
                        S.region = 'norm'
                        norm_sq(t, hT[:, 0, 0:1024], [('hT', 0, 0), ('hT', 0, 1)])
                        if t == 16 or t % 4 == 3:
                            norm_rstd(4 if t == 16 else t // 4)
                        S.region = 'B'
            for _ in pre_gen:
                pass
            dbg('x2', x_sb, [128, 17, D], [('x', t) for t in range(17)])
            done('mlp')

            A.top = PH
            o_nt = A.alloc(512 + 4096)
            tht = [A.f32(A.alloc(D), [D]) for i in range(2)]
            ut = [A.f32(A.alloc(D), [D]) for i in range(2)]
            ysb = [A.f32(A.alloc(D), [D]) for i in range(2)]
            sqj = [A.bf16(A.alloc(D // 2), [D]) for i in range(2)]
            o_fs = A.alloc(40); fss = A.f32(o_fs, [2, 20])
            assert A.top <= B_END
            S.barrier(old=['B'], new=['norm'])
            S.region = 'C'
            S.op('pool', lambda e: e.memset(fss, 1.0), writes=['fss0'])
            norm_to_hT(2, o_nt, presq=True)
            S.region = 'C'
            def c_main(t):
                r = rows(t)
                cs = slice(t * 128, t * 128 + r)
                tbk = t // 4 if t < 16 else 4
                i2 = t % 2
                for nb in range(2):
                    bg_ = bank(); bp_ = bank()
                    ns = slice(nb * 512, (nb + 1) * 512)
                    mm_group(Pf(bg_)[0:r, :], [(hT[:, kc, cs], wgp[:, kc, ns]) for kc in range(8)], hT_reads(gs=[tbk]) + ['wgp'], bg_)
                    mm_group(Pf(bp_)[0:r, :], [(pT[:, kc, cs], wpp[:, kc, ns]) for kc in range(2)], [('pT', t), 'wpp'], bp_)
                    S.op('act', lambda e, b=bg_, i2=i2, r=r, ns=ns: e.activation(out=tht[i2][0:r, ns], in_=Pf(b)[0:r, :], func=AF.Tanh, scale=0.5),
                         reads=[PSR(bg_)], writes=[('tht', i2, nb)])
                    S.op('dve', lambda e, b=bp_, i2=i2, r=r, ns=ns: e.scalar_tensor_tensor(out=ut[i2][0:r, ns], in0=tht[i2][0:r, ns], scalar=1.0,
                                                                                          in1=Pf(b)[0:r, :], op0=ALU.add, op1=ALU.mult),
                         reads=[PSR(bp_), ('tht', i2, nb)], writes=[('ut', i2, nb)])
                    S.op('dve', lambda e, i2=i2, r=r, ns=ns, t=t: e.scalar_tensor_tensor(out=x_sb[0:r, t, ns], in0=ut[i2][0:r, ns], scalar=0.5,
                                                                                        in1=x_sb[0:r, t, ns], op0=ALU.mult, op1=ALU.add),
                         reads=[('ut', i2, nb), ('x', t)], writes=[('x', t)])

            def c_fin1(t):
                r = rows(t)
                i2 = t % 2
                S.op('act', lambda e, i2=i2, r=r, t=t: e.activation(out=sqj[i2][0:r, :], in_=x_sb[0:r, t, :], func=AF.Square,
                                                                   accum_out=fss[0:r, 0, t:t + 1]),
                     reads=[('x', t), 'fss0'], writes=[('sqj', i2), ('fss', t)])
                S.op('dve', lambda e, r=r, t=t: e.tensor_scalar(out=fss[0:r, 1, t:t + 1], in0=fss[0:r, 0, t:t + 1], scalar1=1.0 / D, scalar2=EPS,
                                                                op0=ALU.mult, op1=ALU.add), reads=[('fss', t)], writes=[('fss1', t)])
                S.op('pool', lambda e, r=r, t=t: e.tensor_tensor(out=fss[0:r, 0, t:t + 1], in0=fss[0:r, 1, t:t + 1], in1=mhalf[0:r, 0:1], op=ALU.pow),
                     reads=[('fss1', t), 'mhalf'], writes=[('frs', t)])

            def c_fin2(t):
                r = rows(t)
                i2 = t % 2
                S.op('dve', lambda e, i2=i2, r=r, t=t: e.scalar_tensor_tensor(out=ysb[i2][0:r, :], in0=x_sb[0:r, t, :], scalar=fss[0:r, 0, t:t + 1],
                                                                             in1=gfin[0:r, :], op0=ALU.mult, op1=ALU.mult),
                     reads=[('x', t), ('frs', t), 'gfin'], writes=[('ysb', i2)])
                dst = yp[t * 128:(t + 1) * 128, :] if t < 16 else ys
                S.op('sp', lambda e, i2=i2, r=r, dst=dst: e.dma_start(out=dst, in_=ysb[i2][0:r, :]), reads=[('ysb', i2)], dma=True)

            for i in range(17 + 2):
                if i < 17:
                    c_main(i)
                if 0 <= i - 1 < 17:
                    c_fin1(i - 1)
                if 0 <= i - 2 < 17:
                    c_fin2(i - 2)

        except _Stop:
            pass
        S.emit(st)
    return nc, dbg_outs


_CACHE = {}


def _prep(inputs, c):
    f = lambda a: np.ascontiguousarray(np.asarray(a, dtype=np.float32))
    sl = slice(c * NS, (c + 1) * NS)
    return {
        "xp": f(inputs["x_prompt"][c]), "xs": f(inputs["x_sample"][sl, 0]),
        "pp": f(inputs["p_prompt"][0, c]), "psm": f(inputs["p_sample"][0, sl, 0]),
        "sC": f(inputs["state_mlstm_C"][0, sl]), "sn": f(inputs["state_mlstm_n"][0, sl]),
        "sm": f(inputs["state_mlstm_m"][0, sl]), "sconv": f(inputs["state_conv"][0, sl]),
        "w_in": f(inputs["w_in"][0]), "w_out": f(inputs["w_out"][0]), "w_up": f(inputs["w_up"][0]),
        "w_down": f(inputs["w_down"][0]), "w_pg": f(inputs["w_ple_gate"][0]), "w_pp": f(inputs["w_ple_proj"][0]),
        "nmix": f(inputs["norm_mix"][0]), "nmlp": f(inputs["norm_mlp"][0]), "nple": f(inputs["norm_ple"][0]),
        "nfin": f(inputs["norm_final"]), "bgi": f(inputs["b_gate_i"][0]), "bgf": f(inputs["b_gate_f"][0]),
        "mhn": f(inputs["mh_norm"][0]), "cw": f(inputs["conv_w"][0]),
    }


def kernel(**inputs):
    if 'nc' not in _CACHE:
        _CACHE['nc'] = build()[0]
    nc = _CACHE['nc']
    in_maps = [_prep(inputs, c) for c in range(8)]
    res = run_bass_kernel_spmd(nc, in_maps, core_ids=list(range(8)))
    R = res.results
    g = lambda k: [np.asarray(R[c][k], dtype=np.float32) for c in range(8)]
    y_prompt = np.stack(g("yp"), 0)
    y_sample = np.concatenate(g("ys"), 0)[:, None, :]
    pC_ = np.stack(g("pC"), 0)[None]
    pn_ = np.stack(g("pn"), 0)[None]
    pm_ = np.stack(g("pm"), 0)[None]
    pconv_ = np.stack(g("pconv"), 0)[None]
    sC_ = np.concatenate(g("sCo"), 0)[None]
    sn_ = np.concatenate(g("sno"), 0)[None]
    sm_ = np.concatenate(g("smo"), 0)[None]
    sconv_ = np.concatenate(g("sconvo"), 0)[None]
    return (y_prompt, y_sample, pC_, pn_, pm_, pconv_, sC_, sn_, sm_, sconv_)
```
